# Optimizing a Trainium2 kernel written in Bass

```python
import jax, jax.numpy as jnp
from jax import lax
import numpy as np

D_MODEL = 1024
BATCH = 32
SEQ = 256
DEPTH = 2
DEC_BATCH = 4
DEC_SEQ = 2048
PAST_LEN = 512

GRID_W = 64
MIX_W = D_MODEL
FOURIER_W = MIX_W // 2
FOURIER_HEADS = 4
FOURIER_HD = FOURIER_W // FOURIER_HEADS
GLA_DV_W = MIX_W - FOURIER_W
GLA_DK_W = GLA_DV_W // 2
GLA_HEADS = 4
DV = GLA_DV_W // GLA_HEADS
DK = GLA_DK_W // GLA_HEADS
GATE_RANK = 16
GATE_TEMP = 16.0
CHUNK = 64
D_FF = 11 * D_MODEL // 4
EPS = 1e-6
POS_BASE = 10000.0
S_F = FOURIER_W
S_Q = S_F + GLA_DK_W
S_K = S_Q + GLA_DK_W
S_V = S_K + GLA_DV_W
S_AF = S_V + GATE_RANK
S_AB = S_AF + GATE_RANK
IN_COLS = S_AB + GLA_DV_W

kernel_name = "hymba_fnet_gla_convffn_diffusion_step"


def rmsnorm(x, g):
    xf = x.astype(jnp.float32)
    y = xf * lax.rsqrt(jnp.mean(xf * xf, axis=-1, keepdims=True) + EPS)
    return (y * g.astype(jnp.float32)).astype(x.dtype)


def grid_pos_embed(rows, d):
    quarter = d // 4
    omega = 1.0 / (POS_BASE ** (jnp.arange(quarter, dtype=jnp.float32) / quarter))
    er = jnp.arange(rows, dtype=jnp.float32)[:, None] * omega
    ec = jnp.arange(GRID_W, dtype=jnp.float32)[:, None] * omega
    pr = jnp.concatenate([jnp.sin(er), jnp.cos(er)], axis=-1)
    pc = jnp.concatenate([jnp.sin(ec), jnp.cos(ec)], axis=-1)
    pe = jnp.concatenate([jnp.broadcast_to(pr[:, None], (rows, GRID_W, d // 2)),
                          jnp.broadcast_to(pc[None], (rows, GRID_W, d // 2))], axis=-1)
    return pe.reshape(rows * GRID_W, d)


def gla_chunk_scan(q, k, v, log_a, s0):
    B, H, T, _ = q.shape
    nc = T // CHUNK
    rs = lambda a: a.reshape(B, H, nc, CHUNK, a.shape[-1])
    q, k, v, log_a = rs(q), rs(k), rs(v), rs(log_a)
    b = jnp.cumsum(log_a, axis=3)
    b_last = b[:, :, :, -1:, :]
    q_t = q * jnp.exp(b)
    k_t = k * jnp.exp(-b)
    k_end = k * jnp.exp(b_last - b)
    mask = jnp.tril(jnp.ones((CHUNK, CHUNK), dtype=bool))
    att = jnp.where(mask, jnp.einsum('bhnid,bhnjd->bhnij', q_t, k_t), 0.0)
    o_intra = jnp.einsum('bhnij,bhnjv->bhniv', att, v)
    kv_chunk = jnp.einsum('bhnjd,bhnjv->bhndv', k_end, v)
    decay_chunk = jnp.exp(b_last[:, :, :, 0, :])

    def step(s, xs):
        dec, kv = xs
        return dec[..., None] * s + kv, s

    s_final, s_before = lax.scan(step, s0.astype(jnp.float32),
                                 (jnp.moveaxis(decay_chunk, 2, 0), jnp.moveaxis(kv_chunk, 2, 0)))
    s_before = jnp.moveaxis(s_before, 0, 2)
    o_inter = jnp.einsum('bhnid,bhndv->bhniv', q_t, s_before)
    return (o_intra + o_inter).reshape(B, H, T, DV), s_final


def token_mix(h, s0, lp):
    B, T, _ = h.shape
    z = h @ lp['w_in']
    zf, zq, zk, zv, zaf, zab, zg = jnp.split(
        z, (S_F, S_Q, S_K, S_V, S_AF, S_AB), axis=-1)
    zf = zf.reshape(B, T, FOURIER_HEADS, FOURIER_HD).astype(jnp.float32)
    yf = jnp.real(jnp.fft.fft2(zf, axes=(1, 3), norm='ortho')).reshape(B, T, FOURIER_W)
    heads = lambda a, d: a.astype(jnp.float32).reshape(B, T, GLA_HEADS, d).transpose(0, 2, 1, 3)
    q = heads(zq, DK) * (DK ** -0.5)
    k = heads(zk, DK)
    v = heads(zv, DV)
    la_f = heads(jax.nn.log_sigmoid((zaf @ lp['w_gate_f'] + lp['b_gate_f']).astype(jnp.float32)), DK) / GATE_TEMP
    la_b = heads(jax.nn.log_sigmoid((zab @ lp['w_gate_b'] + lp['b_gate_b']).astype(jnp.float32)), DK) / GATE_TEMP
    o_f, s_f = gla_chunk_scan(q, k, v, la_f, s0[:, 0])
    o_b, s_b = gla_chunk_scan(q[:, :, ::-1], k[:, :, ::-1], v[:, :, ::-1], la_b[:, :, ::-1], s0[:, 1])
    o = rmsnorm(o_f + o_b[:, :, ::-1], lp['g_gla'])
    o = o.transpose(0, 2, 1, 3).reshape(B, T, GLA_DV_W) * jax.nn.silu(zg.astype(jnp.float32))
    y = jnp.concatenate([yf, o], axis=-1).astype(h.dtype) @ lp['w_out']
    return y, jnp.stack([s_f, s_b], axis=1)


def conv_ffn(h, n_seg, lp):
    B, T, _ = h.shape
    u = (h @ lp['w_up']).reshape(B, n_seg, T // n_seg, 2 * D_FF)
    up = jnp.pad(u, ((0, 0), (0, 0), (1, 1), (0, 0)))
    cw = lp['conv_w']
    u = up[:, :, :-2] * cw[0] + up[:, :, 1:-1] * cw[1] + up[:, :, 2:] * cw[2] + lp['conv_b']
    val, gate = jnp.split(u.reshape(B, T, 2 * D_FF), 2, axis=-1)
    return (jax.nn.silu(gate) * val) @ lp['w_down']


def layer(x, mod, s0, n_seg, lp):
    shift_m, scale_m, gate_m, shift_f, scale_f, gate_f = jnp.split(mod, 6, axis=-1)
    h = rmsnorm(x, lp['g_pre_mix']) * (1.0 + scale_m) + shift_m
    o, s_fin = token_mix(h, s0, lp)
    x = x + gate_m * rmsnorm(o, lp['g_post_mix'])
    h = rmsnorm(x, lp['g_pre_ffn']) * (1.0 + scale_f) + shift_f
    x = x + gate_f * rmsnorm(conv_ffn(h, n_seg, lp), lp['g_post_ffn'])
    return x, s_fin


def setup_inputs(seed: int = 0) -> dict:
    key = jax.random.key(seed)
    ks = jax.random.split(key, 22)
    f32 = jnp.float32
    nrm = lambda k, shape, s: jax.random.normal(k, shape, f32) * s
    return {
        'x_prompt': nrm(ks[0], (BATCH, SEQ, D_MODEL), 1.0),
        'x_sample': nrm(ks[1], (DEC_BATCH, DEC_SEQ, D_MODEL), 1.0),
        'state_gla': nrm(ks[2], (DEC_BATCH, DEPTH, 2, GLA_HEADS, DK, DV), 1.0),
        'c': nrm(ks[3], (DEC_BATCH, D_MODEL), 1.0),
        'c_ctx': nrm(ks[4], (D_MODEL,), 1.0),
        'g_pre_mix': 1.0 + nrm(ks[5], (DEPTH, D_MODEL), 0.01),
        'g_post_mix': 1.0 + nrm(ks[6], (DEPTH, D_MODEL), 0.01),
        'g_pre_ffn': 1.0 + nrm(ks[7], (DEPTH, D_MODEL), 0.01),
        'g_post_ffn': 1.0 + nrm(ks[8], (DEPTH, D_MODEL), 0.01),
        'w_ada': nrm(ks[9], (DEPTH, D_MODEL, 6 * D_MODEL), 0.5 * D_MODEL ** -0.5),
        'b_ada': nrm(ks[10], (DEPTH, 6 * D_MODEL), 0.01),
        'w_in': nrm(ks[11], (DEPTH, D_MODEL, IN_COLS), D_MODEL ** -0.5),
        'w_gate_f': nrm(ks[12], (DEPTH, GATE_RANK, GLA_DK_W), GATE_RANK ** -0.5),
        'b_gate_f': nrm(ks[13], (DEPTH, GLA_DK_W), 0.5),
        'w_gate_b': nrm(ks[14], (DEPTH, GATE_RANK, GLA_DK_W), GATE_RANK ** -0.5),
        'b_gate_b': nrm(ks[15], (DEPTH, GLA_DK_W), 0.5),
        'g_gla': 1.0 + nrm(ks[16], (DEPTH, DV), 0.01),
        'w_out': nrm(ks[17], (DEPTH, MIX_W, D_MODEL), MIX_W ** -0.5),
        'w_up': nrm(ks[18], (DEPTH, D_MODEL, 2 * D_FF), D_MODEL ** -0.5),
        'conv_w': nrm(ks[19], (DEPTH, 3, 2 * D_FF), 3 ** -0.5),
        'conv_b': nrm(ks[20], (DEPTH, 2 * D_FF), 0.01),
        'w_down': nrm(ks[21], (DEPTH, D_FF, D_MODEL), D_FF ** -0.5),
    }


def reference(x_prompt, x_sample, state_gla, c, c_ctx, g_pre_mix, g_post_mix, g_pre_ffn,
              g_post_ffn, w_ada, b_ada, w_in, w_gate_f, b_gate_f, w_gate_b, b_gate_b, g_gla,
              w_out, w_up, conv_w, conv_b, w_down):
    def layer_params(l):
        return dict(g_pre_mix=g_pre_mix[l], g_post_mix=g_post_mix[l], g_pre_ffn=g_pre_ffn[l],
                    g_post_ffn=g_post_ffn[l], w_in=w_in[l], w_gate_f=w_gate_f[l],
                    b_gate_f=b_gate_f[l], w_gate_b=w_gate_b[l], b_gate_b=b_gate_b[l],
                    g_gla=g_gla[l], w_out=w_out[l], w_up=w_up[l], conv_w=conv_w[l],
                    conv_b=conv_b[l], w_down=w_down[l])

    xp = x_prompt
    s_zero = jnp.zeros((x_prompt.shape[0], 2, GLA_HEADS, DK, DV), jnp.float32)
    ctx_states = []
    for l in range(DEPTH):
        mod_ctx = (jax.nn.silu(c_ctx) @ w_ada[l] + b_ada[l])[None, None]
        xp, s_fin = layer(xp, mod_ctx, s_zero, 1, layer_params(l))
        ctx_states.append(s_fin)
    new_state_gla = jnp.stack(ctx_states, axis=1)

    rows = x_sample.shape[1] // GRID_W
    xs = x_sample + grid_pos_embed(rows, D_MODEL).astype(x_sample.dtype)[None]
    for l in range(DEPTH):
        mod = (jax.nn.silu(c) @ w_ada[l] + b_ada[l])[:, None]
        xs, _ = layer(xs, mod, state_gla[:, l], rows, layer_params(l))

    return (xp, xs, new_state_gla)
```

```python
import os
import numpy as np
import ml_dtypes
from contextlib import ExitStack
import concourse.bass as bass
import concourse.mybir as mybir
from concourse.bass_utils import run_bass_kernel_spmd

F32 = mybir.dt.float32
BF16 = mybir.dt.bfloat16
AF = mybir.ActivationFunctionType
ALU = mybir.AluOpType

ENGS = ("pe", "act", "dve", "pool", "sp")
N_DMA_SEMS = 16

T = 2048
D = 1024
NK = 8
NTC = 16
IN_COLS = 2080
DFF = 2816
NPAIR = 22
EPS = 1e-6
VT_ROWS = 640
LBASE = lambda l: 8 + l * 257


class Prog:
    def __init__(self):
        self.ops = []
        self.keys = set()

    def op(self, eng, fn, reads=(), writes=(), dma=False):
        self.ops.append((eng, fn, tuple(reads), tuple(writes), dma))
        self.keys.update(reads)
        self.keys.update(writes)

    def barrier(self):
        allk = tuple(self.keys)
        self.op("sp", lambda h: h.nop(), reads=(), writes=allk + ("__bar",))
        for e in ("pe", "act", "dve", "pool"):
            self.op(e, None, reads=("__bar",))

    def analyze(self):
        ops = self.ops
        n = len(ops)
        last_writer = {}
        readers = {}
        need = [None] * n
        signal = [False] * n
        for i, (eng, fn, reads, writes, dma) in enumerate(ops):
            raw = set()
            other = set()
            for r in reads:
                j = last_writer.get(r)
                if j is not None:
                    raw.add(j)
            for w in writes:
                j = last_writer.get(w)
                if j is not None:
                    other.add(j)
                for j in readers.get(w, ()):
                    other.add(j)
            other -= raw
            other.discard(i)
            raw.discard(i)
            keep = {}
            for j, is_raw in [(j, True) for j in raw] + [(j, False) for j in other]:
                ej, _, _, _, dj = ops[j]
                if dj:
                    keep[("d", j)] = j
                    continue
                if ej == eng and not dma:
                    if eng == "pe" or not is_raw:
                        continue
                k = ("e", ej)
                if k not in keep or keep[k] < j:
                    keep[k] = j
            need[i] = sorted(keep.values())
            for j in need[i]:
                signal[j] = True
            for r in reads:
                readers.setdefault(r, []).append(i)
            for w in writes:
                last_writer[w] = i
                readers[w] = []
        cnt = {e: 0 for e in ENGS}
        rr = {e: 0 for e in ENGS}
        dcnt = {}
        dprev = {}
        sig = [None] * n
        waits = [None] * n
        seen = {e: {} for e in ENGS}
        for i, (eng, fn, reads, writes, dma) in enumerate(ops):
            w = []
            for j in need[i]:
                w.append((sig[j][0], sig[j][1]))
            if dma:
                s = ("d", eng, rr[eng] % N_DMA_SEMS)
                rr[eng] += 1
                if s in dprev:
                    w.append((s, dprev[s]))
                dcnt[s] = dcnt.get(s, 0) + 16
                sig[i] = (s, dcnt[s], 16)
                dprev[s] = dcnt[s]
            elif signal[i]:
                cnt[eng] += 1
                sig[i] = (("e", eng), cnt[eng], 1)
            m = {}
            for (k, v) in w:
                if seen[eng].get(k, 0) >= v:
                    continue
                m[k] = max(m.get(k, 0), v)
            for k, v in m.items():
                seen[eng][k] = v
            waits[i] = list(m.items())
        self.sig = sig
        self.waits = waits
        self.semkeys = sorted({s[0] for s in sig if s is not None} |
                              {k for w in waits for (k, v) in w}, key=str)
        self.stats = dict(n_ops=n, signals=dict(cnt), n_waits=sum(len(w) for w in waits),
                          per_eng={e: sum(1 for o in ops if o[0] == e) for e in ENGS})

    def emit_engine(self, eng, h, sems):
        for i, (e, fn, reads, writes, dma) in enumerate(self.ops):
            if e != eng:
                continue
            for (k, v) in self.waits[i]:
                h.wait_ge(sems[k], v)
            if fn is None:
                if self.sig[i] is not None:
                    h.nop().then_inc(sems[self.sig[i][0]], self.sig[i][2])
                continue
            ins = fn(h)
            if self.sig[i] is not None:
                assert ins is not None, ("op must return an instruction", i, e)
                ins.then_inc(sems[self.sig[i][0]], self.sig[i][2])


def run_prog(nc, prog):
    prog.analyze()
    with ExitStack() as st:
        sems = {}
        for k in prog.semkeys:
            sems[k] = st.enter_context(nc.semaphore("s_" + "_".join(str(x) for x in k)))
        block = st.enter_context(nc.Block())

        @block.tensor
        def _(h):
            prog.emit_engine("pe", h, sems)

        @block.scalar
        def _(h):
            prog.emit_engine("act", h, sems)

        @block.vector
        def _(h):
            prog.emit_engine("dve", h, sems)

        @block.gpsimd
        def _(h):
            prog.emit_engine("pool", h, sems)

        @block.sync
        def _(h):
            prog.emit_engine("sp", h, sems)


class _Stop(Exception):
    pass


def build_nc(n_layers=2, dbg=None, stop=None):
    nc = bass.Bass("TRN2", target_bir_lowering=False)
    dt_in = lambda name, shape, dt=F32: nc.dram_tensor(name, list(shape), dt, kind="ExternalInput").ap()
    x_in = dt_in("x", [T, D])
    pe_in = dt_in("pe", [T, D])
    vt_in = dt_in("vt", [VT_ROWS, 128])
    s0_in = dt_in("s0", [2, 2, 4, 64, 128])
    kp_in = dt_in("kp", [128, 1])
    hm_in = dt_in("hm", [128, 64])
    w_ada = dt_in("w_ada", [2, D, 6 * D])
    w_in = dt_in("w_in", [2, D, IN_COLS])
    wgate = dt_in("wgate", [2, 33, 512])
    w_out = dt_in("w_out", [2, D, D])
    w_up = dt_in("w_up", [2, D, 2 * DFF])
    w_down = dt_in("w_down", [2, DFF, D])
    cst_in = dt_in("cst", [64, 128, 1024], BF16)
    cc_in = dt_in("cc", [128, 256], BF16)
    mk_in = dt_in("mk", [128, 1024], BF16)
    u_in = dt_in("u", [128, 256])
    idf_in = dt_in("idf", [128, 128])
    idb_in = dt_in("idb", [128, 128], BF16)
    y_out = nc.dram_tensor("y", [T, D], F32, kind="ExternalOutput").ap()
    ns_out = nc.dram_tensor("ns", [2, 8, 2, 4, 64, 128], F32, kind="ExternalOutput").ap()
    xs = nc.dram_tensor("xs", [T, D], F32, kind="Internal").ap()
    modscr = nc.dram_tensor("modscr", [2, 16, 128], F32, kind="Internal").ap()
    dbg_out = {}
    if dbg:
        for name, shape in dbg.items():
            dbg_out[name] = nc.dram_tensor("dbg_" + name, list(shape), F32, kind="ExternalOutput").ap()

    P = Prog()
    out_keys = []

    with ExitStack() as top:
        uid = [0]

        def sb(st, name, shape, dt):
            uid[0] += 1
            return st.enter_context(nc.sbuf_tensor("sb%d_%s" % (uid[0], name), list(shape), dt))

        def ps(st, name, shape, dt=F32):
            uid[0] += 1
            return st.enter_context(nc.psum_tensor("ps%d_%s" % (uid[0], name), list(shape), dt))

        identb = sb(top, "identb", [128, 128], BF16)
        identf = sb(top, "identf", [128, 128], F32)
        onesb = sb(top, "onesb", [128, 128], BF16)
        cc = sb(top, "cc", [128, 256], BF16)
        mk = sb(top, "mk", [128, 1024], BF16)
        uu = sb(top, "uu", [128, 256], F32)
        uub = sb(top, "uub", [128, 256], BF16)
        hm = sb(top, "hm", [128, 64], F32)
        kp = sb(top, "kp", [128, 1], F32)
        VTT = sb(top, "VTT", [128, VT_ROWS], F32)
        scb = sb(top, "scb", [128, 8], BF16)
        modT = sb(top, "modT", [128, 2, 48], F32)
        gs = sb(top, "gs", [128, 2, 16], F32)
        gt = sb(top, "gt", [128, 2, 16], F32)
        gtT = sb(top, "gtT", [16, 128], F32)
        gg = sb(top, "gg", [128, 2048], F32)
        onesf = sb(top, "onesf", [1, 128], F32)
        wada = [sb(top, "wada%d" % i, [128, 8, 256], BF16) for i in range(3)]
        pb = [ps(top, "pb%d" % i, [128, 512]) for i in range(8)]
        pbm = pb[3][:, 256:512]

        def load(eng, dst, src, key, reads=()):
            P.op(eng, lambda h: h.dma_start(out=dst, in_=src), reads=reads, writes=[key], dma=True)

        load("sp", identb[:], idb_in[:, :], "identb")
        load("sp", identf[:], idf_in[:, :], "identf")
        load("sp", cc[:], cc_in[:, :], "cc")
        load("sp", mk[:], mk_in[:, :], "mk")
        load("sp", uu[:], u_in[:, :], "uu")
        load("sp", hm[:], hm_in[:, :], "hm")
        load("sp", kp[:], kp_in[:, :], "kp")
        P.op("dve", lambda h: h.memset(onesb[:], 1.0), writes=["onesb"])
        P.op("dve", lambda h: h.memset(onesf[:], 1.0), writes=["onesf"])
        P.op("dve", lambda h: h.tensor_copy(out=uub[:], in_=uu[:]), reads=["uu"], writes=["uub"])

        with ExitStack() as s0s:
            vtt = [sb(s0s, "vtt%d" % i, [128, 128], F32) for i in range(2)]
            for i in range(VT_ROWS // 128):
                t = vtt[i % 2]
                load("sp", t[:], vt_in[i * 128:(i + 1) * 128, :], ("vtt", i % 2))
                P.op("pe", (lambda t=t, i=i: lambda h: h.matmul(pb[i % 2][:, 0:128], lhsT=t[:], rhs=identf[:], start=True, stop=True))(),
                     reads=[("vtt", i % 2), "identf"], writes=[("pb", i % 2)])
                P.op("dve", (lambda i=i: lambda h: h.tensor_copy(out=VTT[:, i * 128:(i + 1) * 128], in_=pb[i % 2][:, 0:128]))(),
                     reads=[("pb", i % 2)], writes=["VTT"])
            P.op("act", lambda h: h.activation(out=scb[:], in_=VTT[:, 0:8], func=AF.Silu), reads=["VTT"], writes=["scb"])

        modcnt = [0]

        def mod_dma(l, cb):
            wsrc = w_ada[l].rearrange("(k p) c -> p k c", p=128)
            r = (l * 24 + cb) % 3
            t = wada[r]
            P.op("pool", lambda h: h.dma_start(out=t[:], in_=wsrc[:, :, cb * 256:(cb + 1) * 256]), writes=[("wada", r)], dma=True)

        def mod_mm(l, cb):
            r = (l * 24 + cb) % 3
            t = wada[r]

            def mmod(h):
                ins = None
                for j in range(2):
                    fc = cb * 2 + j
                    for k in range(8):
                        ins = h.matmul(pbm[:, l * 48 + fc:l * 48 + fc + 1], lhsT=t[:, k, j * 128:(j + 1) * 128],
                                       rhs=scb[:, k:k + 1], start=(k == 0), stop=(k == 7))
                return ins
            P.op("pe", mmod, reads=[("wada", r), "scb"], writes=[("pb", 3)])

        def mod_finish(l):
            base = LBASE(l)
            P.op("dve", lambda h: h.tensor_tensor(out=modT[:, l, :], in0=pbm[:, l * 48:(l + 1) * 48],
                                                  in1=VTT[:, base + 176:base + 224], op=ALU.add),
                 reads=[("pb", 3), "VTT"], writes=[("modT", l)])
            for j, (sc0, g0) in enumerate([(8, 224), (32, 240)]):
                P.op("dve", (lambda j=j, sc0=sc0, g0=g0: lambda h: h.scalar_tensor_tensor(
                    out=gs[:, l, j * 8:(j + 1) * 8], in0=modT[:, l, sc0:sc0 + 8], scalar=1.0,
                    in1=VTT[:, base + g0:base + g0 + 8], op0=ALU.add, op1=ALU.mult))(),
                    reads=[("modT", l), "VTT"], writes=[("gs", l)])
            for j, (sc0, g0) in enumerate([(16, 232), (40, 248)]):
                P.op("dve", (lambda j=j, sc0=sc0, g0=g0: lambda h: h.tensor_tensor(
                    out=gt[:, l, j * 8:(j + 1) * 8], in0=modT[:, l, sc0:sc0 + 8],
                    in1=VTT[:, base + g0:base + g0 + 8], op=ALU.mult))(),
                    reads=[("modT", l), "VTT"], writes=[("gt", l)])
            P.op("pe", lambda h: h.matmul(pbm[0:16, 128:256], lhsT=gt[:, l, :], rhs=identf[:], start=True, stop=True),
                 reads=[("gt", l), "identf"], writes=[("pb", 3)])
            P.op("dve", lambda h: h.tensor_copy(out=gtT[:], in_=pbm[0:16, 128:256]), reads=[("pb", 3)], writes=["gtT"])
            P.op("sp", lambda h: h.dma_start(out=modscr[l], in_=gtT[:]), reads=["gtT"], writes=[("modscr", l)], dma=True)

        mod_dma(0, 0)
        mod_dma(0, 1)
        for cb in range(24):
            if cb + 2 < 24:
                mod_dma(0, cb + 2)
            mod_mm(0, cb)
        mod_finish(0)
        import os
        MOD_IL = os.environ.get("MOD_IL", "1") == "1"
        if not MOD_IL and n_layers > 1:
            mod_dma(1, 0)
            mod_dma(1, 1)
            for cb in range(24):
                if cb + 2 < 24:
                    mod_dma(1, cb + 2)
                mod_mm(1, cb)
            mod_finish(1)
        P.barrier()

        def norm_to_T(st, l, which, dst_fn, tag, first=False):
            xr = [sb(st, tag + "xr%d" % i, [128, D], F32) for i in range(4)]
            xn = [sb(st, tag + "xn%d" % i, [128, D], BF16) for i in range(4)]
            per = [sb(st, tag + "per%d" % i, [128, D], F32) for i in range(4)] if first else None
            junk = sb(st, tag + "junk", [128, D], BF16)
            ss = sb(st, tag + "ss", [128, NTC], F32)
            lnv = sb(st, tag + "lnv", [128, NTC], F32)
            rstd = sb(st, tag + "rstd", [128, NTC], F32)
            gcol = which * 8
            shcol = 0 if which == 0 else 24
            def stage1(g):
                pT = [pb[(g % 4) * 2], pb[(g % 4) * 2 + 1]]
                pkeys = [("pb", (g % 4) * 2), ("pb", (g % 4) * 2 + 1)]
                for j in range(2):
                    tc = 2 * g + j
                    r = tc % 4
                    if first:
                        load("sp", xr[r][:], x_in[tc * 128:(tc + 1) * 128, :], (tag + "xr", r))
                        load("sp", per[r][:], pe_in[tc * 128:(tc + 1) * 128, :], (tag + "per", r))
                        P.op("dve", (lambda r=r: lambda h: h.tensor_tensor(out=xr[r][:], in0=xr[r][:], in1=per[r][:], op=ALU.add))(),
                             reads=[(tag + "xr", r), (tag + "per", r)], writes=[(tag + "xr", r)])
                        P.op("pool", (lambda r=r, tc=tc: lambda h: h.dma_start(out=xs[tc * 128:(tc + 1) * 128, :], in_=xr[r][:]))(),
                             reads=[(tag + "xr", r)], writes=[("xs", tc)], dma=True)
                    else:
                        load("sp", xr[r][:], xs[tc * 128:(tc + 1) * 128, :], (tag + "xr", r), reads=[("xs", tc)])
                    P.op("act", (lambda r=r, tc=tc: lambda h: h.activation(out=junk[:], in_=xr[r][:], func=AF.Square, accum_out=ss[:, tc:tc + 1]))(),
                         reads=[(tag + "xr", r)], writes=[tag + "junk", (tag + "ss", tc)])
                    P.op("act", (lambda tc=tc: lambda h: h.activation(out=lnv[:, tc:tc + 1], in_=ss[:, tc:tc + 1], func=AF.Ln, scale=1.0 / D, bias=EPS))(),
                         reads=[(tag + "ss", tc)], writes=[(tag + "lnv", tc)])
                    P.op("act", (lambda tc=tc: lambda h: h.activation(out=rstd[:, tc:tc + 1], in_=lnv[:, tc:tc + 1], func=AF.Exp, scale=-0.5))(),
                         reads=[(tag + "lnv", tc)], writes=[(tag + "rstd", tc)])
                    P.op("dve", (lambda r=r, tc=tc: lambda h: h.tensor_scalar(out=xn[tc % 4][:], in0=xr[r][:], scalar1=rstd[:, tc:tc + 1], scalar2=None, op0=ALU.mult))(),
                         reads=[(tag + "xr", r), (tag + "rstd", tc)], writes=[(tag + "xn", tc % 4)])

                    def tr(h, tc=tc, j=j, pT=pT):
                        ins = None
                        for k in range(8):
                            bank = pT[k // 4]
                            dst = bank[:].bitcast(BF16)[:, (k % 4) * 256 + j * 128:(k % 4) * 256 + (j + 1) * 128]
                            ins = h.transpose(dst, xn[tc % 4][:, k * 128:(k + 1) * 128], identb[:])
                        return ins
                    P.op("pe", tr, reads=[(tag + "xn", tc % 4), "identb"], writes=pkeys)

            def stage2(g):
                pT = [pb[(g % 4) * 2], pb[(g % 4) * 2 + 1]]
                pkeys = [("pb", (g % 4) * 2), ("pb", (g % 4) * 2 + 1)]
                for k in range(8):
                    bank = pT[k // 4]
                    src = bank[:].bitcast(BF16)[:, (k % 4) * 256:(k % 4 + 1) * 256]
                    dst, dkey = dst_fn(k, g)
                    if len(dst.shape) == 3:
                        src = src.rearrange("p (s c) -> p s c", c=dst.shape[2])
                    if k < 4:
                        P.op("act", (lambda src=src, dst=dst, k=k: lambda h: h.activation(
                            out=dst, in_=src, func=AF.Identity, scale=gs[:, l, gcol + k:gcol + k + 1],
                            bias=modT[:, l, shcol + k:shcol + k + 1]))(),
                            reads=pkeys + [("gs", l), ("modT", l)], writes=[dkey])
                    else:
                        P.op("dve", (lambda src=src, dst=dst, k=k: lambda h: h.tensor_scalar(
                            out=dst, in0=src, scalar1=gs[:, l, gcol + k:gcol + k + 1],
                            scalar2=modT[:, l, shcol + k:shcol + k + 1], op0=ALU.mult, op1=ALU.add))(),
                            reads=pkeys + [("gs", l), ("modT", l)], writes=[dkey])
            stage1(0)
            for g in range(8):
                if g + 1 < 8:
                    stage1(g + 1)
                stage2(g)

        def post_norm_1(tc, py, pykeys, junk, junkkey, ss2, lnv2, tag):
            for cb in range(2):
                P.op("act", (lambda cb=cb: lambda h: h.activation(out=junk[:, 0:512], in_=py[cb][:], func=AF.Square, accum_out=ss2[:, 2 * tc + cb:2 * tc + cb + 1]))(),
                     reads=[pykeys[cb]], writes=[junkkey, (tag + "ss2", tc, cb)])
            P.op("dve", lambda h: h.tensor_tensor(out=lnv2[:, tc:tc + 1], in0=ss2[:, 2 * tc:2 * tc + 1], in1=ss2[:, 2 * tc + 1:2 * tc + 2], op=ALU.add),
                 reads=[(tag + "ss2", tc, 0), (tag + "ss2", tc, 1)], writes=[(tag + "lnv2", tc)])

        def post_norm_2(l, which, tc, py, pykeys, xr, xrkey, tmp, tmpkey, lnv2, rstd2, tag, final):
            P.op("act", lambda h: h.activation(out=lnv2[:, tc:tc + 1], in_=lnv2[:, tc:tc + 1], func=AF.Ln, scale=1.0 / D, bias=EPS),
                 reads=[(tag + "lnv2", tc)], writes=[(tag + "lnv2", tc)])
            P.op("act", lambda h: h.activation(out=rstd2[:, tc:tc + 1], in_=lnv2[:, tc:tc + 1], func=AF.Exp, scale=-0.5),
                 reads=[(tag + "lnv2", tc)], writes=[(tag + "rstd2", tc)])
            load("sp", xr[:], xs[tc * 128:(tc + 1) * 128, :], xrkey, reads=[("xs", tc)])
            for cb in range(2):
                P.op("dve", (lambda cb=cb: lambda h: h.scalar_tensor_tensor(
                    out=tmp[:, cb * 512:(cb + 1) * 512], in0=py[cb][:], scalar=rstd2[:, tc:tc + 1],
                    in1=gg[:, which * 1024 + cb * 512:which * 1024 + (cb + 1) * 512], op0=ALU.mult, op1=ALU.mult))(),
                    reads=[pykeys[cb], (tag + "rstd2", tc), "gg"], writes=[tmpkey])
            P.op("dve", lambda h: h.tensor_tensor(out=xr[:], in0=xr[:], in1=tmp[:], op=ALU.add),
                 reads=[xrkey, tmpkey], writes=[xrkey])
            if final:
                P.op("sp", lambda h: h.dma_start(out=y_out[tc * 128:(tc + 1) * 128, :], in_=xr[:]),
                     reads=[xrkey], writes=[("yout", tc)], dma=True)
                out_keys.append(("yout", tc))
            else:
                P.op("sp", lambda h: h.dma_start(out=xs[tc * 128:(tc + 1) * 128, :], in_=xr[:]),
                     reads=[xrkey], writes=[("xs", tc)], dma=True)

        def dbg_store(name, src_ap, dst_ap, rkeys):
            if name in dbg_out:
                P.op("sp", lambda h: h.dma_start(out=dst_ap, in_=src_ap), reads=rkeys, writes=[("dbg", name, id(dst_ap))], dma=True)
                out_keys.append(("dbg", name, id(dst_ap)))

        def do_layer(l):
            base = LBASE(l)
            last = (l == n_layers - 1)
            with ExitStack() as gsc:
                ggrow = sb(gsc, "ggrow", [1, 2048], F32)
                P.op("sp", lambda h: h.dma_start(out=ggrow[:], in_=modscr[l:l + 1].rearrange("o a b -> o (a b)")),
                     reads=[("modscr", l)], writes=["ggrow"], dma=True)
                for j in range(4):
                    P.op("pe", (lambda j=j: lambda h: h.matmul(pb[j % 2][:], lhsT=onesf[:], rhs=ggrow[:, j * 512:(j + 1) * 512], start=True, stop=True))(),
                         reads=["onesf", "ggrow"], writes=[("pb", j % 2)])
                    P.op("dve", (lambda j=j: lambda h: h.tensor_copy(out=gg[:, j * 512:(j + 1) * 512], in_=pb[j % 2][:]))(),
                         reads=[("pb", j % 2)], writes=["gg"])
            P.barrier()
            if stop == "A0":
                return "stop"
            with ExitStack() as mix:
                ogT = sb(mix, "ogT", [128, 4, T], BF16)
                yfT = sb(mix, "yfT", [128, 4, T], BF16)
                with ExitStack() as inp:
                    hT = sb(inp, "hT", [128, 8, T], BF16)
                    with ExitStack() as pa:
                        norm_to_T(pa, l, 0, lambda k, g: (hT[:, k, g * 256:(g + 1) * 256], ("hT", g // 2, k)), "A", first=(l == 0))
                    P.barrier()
                    if stop == "A":
                        return "stop"
                    if "hT" in dbg_out and l == 0:
                        with ExitStack() as dd:
                            tmpd = sb(dd, "tmpd", [128, T], F32)
                            for k in range(8):
                                P.op("dve", (lambda k=k: lambda h: h.tensor_copy(out=tmpd[:], in_=hT[:, k, :]))(), reads=[("hT", i, kk) for i in range(4) for kk in range(8)], writes=["tmpd"])
                                dbg_store("hT", tmpd[:], dbg_out["hT"][k * 128:(k + 1) * 128, :], ["tmpd"])
                            P.barrier()
                    rot = [0]

                    def nextbank():
                        b = rot[0] % 3
                        rot[0] += 1
                        return b
                    winr = [("win", k) for k in range(8)]
                    with ExitStack() as pbx:
                        win = sb(pbx, "winF", [128, 8, 512], BF16)
                        zfT = sb(pbx, "zfT", [128, 4, T], BF16)
                        ZCS = sb(pbx, "ZCS", [128, NTC, 1024], BF16)
                        WOFF = 0
                        for k in range(8):
                            P.op("pool", (lambda k=k, win=win: lambda h: h.dma_start(out=win[:, k, :], in_=w_in[l, k * 128:(k + 1) * 128, 0:512]))(),
                                 writes=[("win", k)], dma=True)

                        def fm_mm(bank, col0, m, tb, win=win, WOFF=WOFF):
                            def f(h):
                                ins = None
                                for k in range(8):
                                    ins = h.matmul(pb[bank][0:m, :], lhsT=win[:, k, col0 - WOFF:col0 - WOFF + m], rhs=hT[:, k, tb * 512:(tb + 1) * 512],
                                                   start=(k == 0), stop=(k == 7))
                                return ins
                            return f
                        for tb in range(4):
                            c0 = tb * 512
                            hk = [("hT", tb, k) for k in range(8)]
                            for fc in range(4):
                                b = nextbank()
                                P.op("pe", fm_mm(b, fc * 128, 128, tb), reads=winr + hk, writes=[("pb", b)])
                                P.op("dve", (lambda b=b, fc=fc, c0=c0: lambda h: h.tensor_copy(out=zfT[:, fc, c0:c0 + 512], in_=pb[b][:]))(),
                                     reads=[("pb", b)], writes=[("zfT", tb)])
                        P.barrier()
                        for tc in range(NTC):
                            t0 = tc * 128
                            b0 = (tc % 2) * 2

                            def zmm(h, t0=t0, b0=b0):
                                ins = None
                                for hd in range(4):
                                    ins = h.matmul(pb[b0 + hd // 2][:, (hd % 2) * 256:(hd % 2 + 1) * 256], lhsT=zfT[:, hd, t0:t0 + 128], rhs=cc[:],
                                                   start=True, stop=True)
                                return ins
                            P.op("pe", zmm, reads=[("zfT", tc // 4), "cc"], writes=[("pb", b0), ("pb", b0 + 1)])
                            P.op("act", (lambda tc=tc, b0=b0: lambda h: h.activation(out=ZCS[:, tc, 0:512], in_=pb[b0][:], func=AF.Copy))(),
                                 reads=[("pb", b0)], writes=[("ZCS", tc)])
                            P.op("dve", (lambda tc=tc, b0=b0: lambda h: h.tensor_copy(out=ZCS[:, tc, 512:1024], in_=pb[b0 + 1][:]))(),
                                 reads=[("pb", b0 + 1)], writes=[("ZCS", tc)])
                        cring = [sb(pbx, "cring%d" % i, [128, 1024], BF16) for i in range(4)]
                        ci = 0
                        for tpb in range(4):
                            for tc in range(NTC):
                                r = ci % 4
                                ci += 1
                                load("sp", cring[r][:], cst_in[tpb * 16 + tc], ("cring", r))

                                def ymm(h, tc=tc, r=r):
                                    ins = None
                                    for hd in range(4):
                                        h.matmul(pb[4 + hd][:], lhsT=ZCS[:, tc, hd * 256:hd * 256 + 128], rhs=cring[r][:, 0:512],
                                                 start=(tc == 0), stop=False)
                                        ins = h.matmul(pb[4 + hd][:], lhsT=ZCS[:, tc, hd * 256 + 128:hd * 256 + 256], rhs=cring[r][:, 512:1024],
                                                       start=False, stop=(tc == NTC - 1))
                                    return ins
                                P.op("pe", ymm, reads=[("ZCS", tc), ("cring", r)], writes=[("pb", 4 + hd) for hd in range(4)])
                                if MOD_IL and l + 1 < n_layers and (ci % 2 == 0):
                                    mcb = ci // 2 - 1
                                    if mcb == 0:
                                        mod_dma(l + 1, 0)
                                        mod_dma(l + 1, 1)
                                    if mcb < 24:
                                        if mcb + 2 < 24:
                                            mod_dma(l + 1, mcb + 2)
                                        mod_mm(l + 1, mcb)
                                    if mcb == 24:
                                        mod_finish(l + 1)
                            for hd in range(4):
                                eng = "act" if hd % 2 == 0 else "dve"
                                if eng == "act":
                                    fn = (lambda hd=hd, tpb=tpb: lambda h: h.activation(out=yfT[:, hd, tpb * 512:(tpb + 1) * 512], in_=pb[4 + hd][:], func=AF.Copy))()
                                else:
                                    fn = (lambda hd=hd, tpb=tpb: lambda h: h.tensor_copy(out=yfT[:, hd, tpb * 512:(tpb + 1) * 512], in_=pb[4 + hd][:]))()
                                P.op(eng, fn, reads=[("pb", 4 + hd)], writes=[("yfT", tpb)])
                    P.barrier()
                    if stop == "C":
                        return "stop"
                    QK = sb(inp, "QK", [128, 8, T], BF16)
                    V = sb(inp, "V", [128, NTC, 512], BF16)
                    sgT = sb(inp, "sgT", [128, 4, T], BF16)
                    dec = sb(inp, "dec", [128, 4, NTC], F32)
                    with ExitStack() as pbx:
                        win = sb(pbx, "winG", [128, 8, IN_COLS - 512], BF16)
                        WOFF = 512
                        wg = sb(pbx, "wg", [33, 512], BF16)
                        zab = sb(pbx, "zab", [33, 1024], BF16)
                        etmp = [sb(pbx, "etmp%d" % i, [128, 512], F32) for i in range(2)]
                        lhi = sb(pbx, "lhi", [128, 512], BF16)
                        llo = sb(pbx, "llo", [128, 512], BF16)
                        Ep = sb(pbx, "Ep", [128, 4, 512], F32)
                        Em = sb(pbx, "Em", [128, 4, 512], F32)
                        for k in range(8):
                            P.op("pool", (lambda k=k, win=win: lambda h: h.dma_start(out=win[:, k, :], in_=w_in[l, k * 128:(k + 1) * 128, 512:IN_COLS]))(),
                                 writes=[("win", k)], dma=True)
                        P.op("pool", lambda h: h.dma_start(out=wg[:], in_=wgate[l]), writes=["wg"], dma=True)
                        P.op("dve", lambda h: h.memset(zab[32:33, :], 1.0), writes=["zab1"])

                        def fm_mm(bank, col0, m, tb, win=win, WOFF=WOFF):
                            def f(h):
                                ins = None
                                for k in range(8):
                                    ins = h.matmul(pb[bank][0:m, :], lhsT=win[:, k, col0 - WOFF:col0 - WOFF + m], rhs=hT[:, k, tb * 512:(tb + 1) * 512],
                                                   start=(k == 0), stop=(k == 7))
                                return ins
                            return f
                        for tb in range(4):
                            c0 = tb * 512
                            hk = [("hT", tb, k) for k in range(8)]
                            zt = zab[:, (tb % 2) * 512:(tb % 2 + 1) * 512]
                            zkey = ("zab", tb % 2)
                            P.op("pe", fm_mm(7, 1536, 32, tb), reads=winr + hk, writes=[("pb", 7)])
                            P.op("act", (lambda zt=zt: lambda h: h.activation(out=zt[0:32, :], in_=pb[7][0:32, :], func=AF.Copy))(),
                                 reads=[("pb", 7)], writes=[zkey])

                            def gate_a(j, tb=tb, zt=zt, zkey=zkey):
                                xb = 3 + (j % 2)
                                et = etmp[j % 2]
                                P.op("pe", lambda h: h.matmul(pb[xb][:], lhsT=zt[0:33, j * 128:(j + 1) * 128], rhs=wg[0:33, :], start=True, stop=True),
                                     reads=[zkey, "zab1", "wg"], writes=[("pb", xb)])
                                P.op("act", lambda h: h.activation(out=et[:], in_=pb[xb][:], func=AF.Exp, scale=-1.0), reads=[("pb", xb)], writes=[("etmp", j % 2)])
                                P.op("act", lambda h: h.activation(out=et[:], in_=et[:], func=AF.Ln, bias=1.0), reads=[("etmp", j % 2)], writes=[("etmp", j % 2)])

                            def gate_a2(j):
                                et = etmp[j % 2]
                                P.op("dve", lambda h: h.tensor_copy(out=lhi[:], in_=et[:]), reads=[("etmp", j % 2)], writes=["lhi"])
                                P.op("dve", lambda h: h.tensor_tensor(out=llo[:], in0=et[:], in1=lhi[:], op=ALU.subtract), reads=[("etmp", j % 2), "lhi"], writes=["llo"])

                            def gate_b(j):
                                def cums(h):
                                    ins = None
                                    for q in range(4):
                                        h.matmul(pb[5][:, q * 128:(q + 1) * 128], lhsT=lhi[:, q * 128:(q + 1) * 128],
                                                 rhs=uub[:, (q // 2) * 128:(q // 2 + 1) * 128], start=True, stop=False)
                                        ins = h.matmul(pb[5][:, q * 128:(q + 1) * 128], lhsT=llo[:, q * 128:(q + 1) * 128],
                                                       rhs=uub[:, (q // 2) * 128:(q // 2 + 1) * 128], start=False, stop=True)
                                    return ins
                                P.op("pe", cums, reads=["lhi", "llo", "uub"], writes=[("pb", 5)])
                                pv4 = pb[5][:].rearrange("p (q c) -> p q c", c=128)
                                P.op("act", lambda h: h.activation(out=Ep[:, :, j * 128:(j + 1) * 128], in_=pv4, func=AF.Exp),
                                     reads=[("pb", 5)], writes=[("Ep", j)])
                                P.op("act", lambda h: h.activation(out=Em[:, :, j * 128:(j + 1) * 128], in_=pv4, func=AF.Exp, scale=-1.0),
                                     reads=[("pb", 5)], writes=[("Em", j)])

                            def v_step(j, tb=tb, win=win, WOFF=WOFF):
                                tc = tb * 4 + j
                                t0 = tc * 128
                                vb = 6 + (j % 2)

                                def vmm(h):
                                    ins = None
                                    for k in range(8):
                                        ins = h.matmul(pb[vb][:], lhsT=hT[:, k, t0:t0 + 128], rhs=win[:, k, 1024 - WOFF:1536 - WOFF], start=(k == 0), stop=(k == 7))
                                    return ins
                                P.op("pe", vmm, reads=winr + hk, writes=[("pb", vb)])
                                P.op("dve", lambda h: h.tensor_copy(out=V[:, tc, :], in_=pb[vb][:]), reads=[("pb", vb)], writes=[("V", tc)])
                            gate_a(0)
                            gate_a(1)
                            gate_a2(0)
                            v_step(0)
                            gate_b(0)
                            gate_a(2)
                            gate_a2(1)
                            v_step(1)
                            gate_b(1)
                            gate_a(3)
                            gate_a2(2)
                            v_step(2)
                            gate_b(2)
                            gate_a2(3)
                            v_step(3)
                            gate_b(3)
                            Epk = [("Ep", j) for j in range(4)]
                            Emk = [("Em", j) for j in range(4)]
                            Epv = Ep[:].rearrange("p q (j c) -> p q j c", c=128)
                            P.op("dve", (lambda tb=tb, Epv=Epv: lambda h: h.tensor_copy(out=dec[:, 0:2, tb * 4:(tb + 1) * 4], in_=Epv[:, 0:2, :, 127]))(),
                                 reads=Epk, writes=[("dec", tb)])
                            P.op("dve", (lambda tb=tb, Epv=Epv: lambda h: h.tensor_copy(out=dec[:, 2:4, tb * 4:(tb + 1) * 4], in_=Epv[:, 2:4, :, 0]))(),
                                 reads=Epk, writes=[("dec", tb)])
                            for p in range(2):
                                b = nextbank()
                                P.op("pe", fm_mm(b, 512 + p * 128, 128, tb), reads=winr + hk, writes=[("pb", b)])
                                for d in range(2):
                                    P.op("dve", (lambda b=b, p=p, d=d, c0=c0: lambda h: h.scalar_tensor_tensor(
                                        out=QK[:, d * 2 + p, c0:c0 + 512], in0=pb[b][:], scalar=0.125, in1=Ep[:, d * 2 + p, :],
                                        op0=ALU.mult, op1=ALU.mult))(),
                                        reads=[("pb", b)] + Epk, writes=[("QK", d * 2 + p, tb)])
                            for p in range(2):
                                b = nextbank()
                                P.op("pe", fm_mm(b, 768 + p * 128, 128, tb), reads=winr + hk, writes=[("pb", b)])
                                for d in range(2):
                                    P.op("dve", (lambda b=b, p=p, d=d, c0=c0: lambda h: h.tensor_tensor(
                                        out=QK[:, 4 + d * 2 + p, c0:c0 + 512], in0=pb[b][:], in1=Em[:, d * 2 + p, :], op=ALU.mult))(),
                                        reads=[("pb", b)] + Emk, writes=[("QK", 4 + d * 2 + p, tb)])
                            for fc in range(4):
                                b = nextbank()
                                P.op("pe", fm_mm(b, 1568 + fc * 128, 128, tb), reads=winr + hk, writes=[("pb", b)])
                                P.op("act", (lambda b=b, fc=fc, c0=c0: lambda h: h.activation(out=sgT[:, fc, c0:c0 + 512], in_=pb[b][:], func=AF.Silu))(),
                                     reads=[("pb", b)], writes=[("sgT", tb)])
                    P.barrier()
                    if stop == "B2":
                        return "stop"
                    with ExitStack() as gl:
                        S = [[sb(gl, "S%d_%d" % (q, i), [128, 256], F32) for i in range(2)] for q in range(4)]
                        Sst = hT[:].rearrange("p k t -> p (k t)").rearrange("p (q c v) -> p q c v", q=4, c=NTC)
                        kTm = [sb(gl, "kTm%d" % i, [128, 512], BF16) for i in range(2)]
                        kvs = [sb(gl, "kvs%d" % i, [128, 4, 256], F32) for i in range(2)]
                        deckp = sb(gl, "deckp", [128, 4, NTC], F32)
                        deck = [("dec", i) for i in range(4)]
                        P.op("dve", lambda h: h.tensor_copy(out=deckp[:], in_=dec[:]), reads=deck, writes=["deckp"])
                        dkv = deckp[:].rearrange("p q (a b) -> p q a b", b=2)
                        P.op("dve", lambda h: h.tensor_scalar(out=dkv[:, 0:2, 1:8, 0], in0=dkv[:, 0:2, 1:8, 0], scalar1=kp[:, 0:1], scalar2=None, op0=ALU.mult),
                             reads=["deckp", "kp"], writes=["deckp"])
                        P.op("dve", lambda h: h.tensor_scalar(out=dkv[:, 2:4, 0:7, 1], in0=dkv[:, 2:4, 0:7, 1], scalar1=kp[:, 0:1], scalar2=None, op0=ALU.mult),
                             reads=["deckp", "kp"], writes=["deckp"])
                        for q in range(4):
                            P.op("dve", (lambda q=q: lambda h: h.memset(S[q][0][:], 0.0))(), writes=[("S", q, 0)])
                            d_, p_ = q // 2, q % 2
                            for e in range(2):
                                P.op("sp", (lambda q=q, d_=d_, p_=p_, e=e: lambda h: h.dma_start(
                                    out=S[q][0][e * 64:(e + 1) * 64, e * 128:(e + 1) * 128], in_=s0_in[l, d_, 2 * p_ + e]))(),
                                    reads=[], writes=[("S", q, 0)], dma=True)

                        def chunks_of(step):
                            return [step, step, NTC - 1 - step, NTC - 1 - step]

                        def d1_prep(step):
                            chunk_of = chunks_of(step)
                            pT = pb[step % 2]

                            def ktr(h):
                                ins = None
                                for q in range(4):
                                    c = chunk_of[q]
                                    ins = h.transpose(pT[:].bitcast(BF16)[:, q * 128:(q + 1) * 128], QK[:, 4 + q, c * 128:(c + 1) * 128], identb[:])
                                return ins
                            P.op("pe", ktr, reads=[("QK", 4 + q, chunk_of[q] // 4) for q in range(4)] + ["identb"], writes=[("pb", step % 2)])
                            km = kTm[step % 2]
                            P.op("act", lambda h: h.activation(out=km[:], in_=pT[:].bitcast(BF16)[:, 0:512], func=AF.Copy),
                                 reads=[("pb", step % 2)], writes=[("kTm", step % 2)])
                            kb0 = 2 + (step % 2) * 2

                            def kvmm(h):
                                ins = None
                                for q in range(4):
                                    c = chunk_of[q]
                                    p_ = q % 2
                                    ins = h.matmul(pb[kb0 + q // 2][:, (q % 2) * 256:(q % 2 + 1) * 256], lhsT=km[:, q * 128:(q + 1) * 128],
                                                   rhs=V[:, c, p_ * 256:(p_ + 1) * 256], start=True, stop=True)
                                return ins
                            P.op("pe", kvmm, reads=[("kTm", step % 2)] + [("V", chunk_of[q]) for q in range(4)], writes=[("pb", kb0), ("pb", kb0 + 1)])
                            kv = kvs[step % 2]
                            for q in range(4):
                                c = chunk_of[q]
                                P.op("act", (lambda q=q, c=c: lambda h: h.activation(
                                    out=kv[:, q, :], in_=pb[kb0 + q // 2][:, (q % 2) * 256:(q % 2 + 1) * 256], func=AF.Copy, scale=dec[:, q, c:c + 1]))(),
                                    reads=[("pb", kb0 + q // 2), ("dec", c // 4)], writes=[("kvs", step % 2, q)])

                        def d1_main(step):
                            chunk_of = chunks_of(step)
                            cur, nxt = step % 2, (step + 1) % 2
                            kv = kvs[step % 2]
                            for q in range(4):
                                c = chunk_of[q]
                                d_ = q // 2
                                seg_start = (c % 2 == 0 and c > 0) if d_ == 0 else (c % 2 == 1 and c < NTC - 1)
                                src = S[q][cur][:]
                                dst = Sst[:, q, c, :]
                                if d_ == 0:
                                    sc_ = kp[:, 0:1] if seg_start else 1.0
                                    fn = (lambda src=src, dst=dst, sc_=sc_: lambda h: h.activation(out=dst, in_=src, func=AF.Copy, scale=sc_))()
                                    P.op("act", fn, reads=[("S", q, cur), "kp"], writes=[("Sst", q, c)])
                                else:
                                    if seg_start:
                                        fn = (lambda src=src, dst=dst: lambda h: h.tensor_scalar(out=dst, in0=src, scalar1=kp[:, 0:1], scalar2=None, op0=ALU.mult))()
                                    else:
                                        fn = (lambda src=src, dst=dst: lambda h: h.tensor_copy(out=dst, in_=src))()
                                    P.op("dve", fn, reads=[("S", q, cur), "kp"], writes=[("Sst", q, c)])
                            for q in range(4):
                                c = chunk_of[q]
                                P.op("dve", (lambda q=q, c=c: lambda h: h.scalar_tensor_tensor(
                                    out=S[q][nxt][:], in0=S[q][cur][:], scalar=deckp[:, q, c:c + 1], in1=kv[:, q, :], op0=ALU.mult, op1=ALU.add))(),
                                    reads=[("S", q, cur), "deckp", ("kvs", step % 2, q)], writes=[("S", q, nxt)])
                                d_, p_ = q // 2, q % 2
                                seg_end = (c % 2 == 1) if d_ == 0 else (c % 2 == 0)
                                if seg_end:
                                    seg = c // 2
                                    for e in range(2):
                                        key = ("ns", l, seg, d_, 2 * p_ + e)
                                        P.op("sp", (lambda q=q, seg=seg, d_=d_, p_=p_, e=e: lambda h: h.dma_start(
                                            out=ns_out[l, seg, d_, 2 * p_ + e], in_=S[q][nxt][e * 64:(e + 1) * 64, e * 128:(e + 1) * 128]))(),
                                            reads=[("S", q, nxt)], writes=[key], dma=True)
                                        out_keys.append(key)
                        d1_prep(0)
                        for step in range(NTC):
                            if step + 1 < NTC:
                                d1_prep(step + 1)
                            d1_main(step)
                        P.barrier()
                        if stop == "D1":
                            return "stop"
                        attm = [sb(gl, "attm%d" % i, [128, 1024], BF16) for i in range(3)]
                        qzt = [sb(gl, "qzt%d" % i, [128, 4, 2, 128], BF16) for i in range(3)]
                        for i in range(3):
                            P.op("pool", (lambda i=i: lambda h: h.memset(qzt[i][:], 0.0))(), writes=[("qz", i)])
                        osq = [sb(gl, "osq%d" % i, [128, 512], BF16) for i in range(2)]
                        lno = [sb(gl, "lno%d" % i, [128, 512], F32) for i in range(2)]

                        def d2_a1(c):
                            t0 = c * 128
                            qz = qzt[c % 3]
                            for e in range(2):
                                P.op("act", (lambda e=e: lambda h: h.activation(
                                    out=qz[e * 64:(e + 1) * 64, :, e, :], in_=QK[e * 64:(e + 1) * 64, 0:4, t0:t0 + 128], func=AF.Copy))(),
                                    reads=[("QK", qq, c // 4) for qq in range(4)], writes=[("qz", c % 3)])

                        def d2_a2(c):
                            t0 = c * 128
                            a0 = (c % 2) * 2
                            am = attm[c % 3]
                            qz = qzt[c % 3]

                            def attmm(h):
                                ins = None
                                for d_ in range(2):
                                    for hd in range(4):
                                        e, p_ = hd % 2, hd // 2
                                        ins = h.matmul(pb[a0 + d_][:, hd * 128:(hd + 1) * 128],
                                                       lhsT=QK[:, 4 + d_ * 2 + p_, t0:t0 + 128],
                                                       rhs=qz[:, d_ * 2 + p_, e, :], start=True, stop=True)
                                return ins
                            P.op("pe", attmm, reads=[("QK", i, c // 4) for i in range(4, 8)] + [("qz", c % 3)], writes=[("pb", a0), ("pb", a0 + 1)])
                            for d_ in range(2):
                                P.op("dve", (lambda d_=d_: lambda h: h.tensor_tensor(
                                    out=am[:, d_ * 512:(d_ + 1) * 512], in0=pb[a0 + d_][:], in1=mk[:, d_ * 512:(d_ + 1) * 512], op=ALU.mult))(),
                                    reads=[("pb", a0 + d_), "mk"], writes=[("attm", c % 3, d_)])

                        def d2_b(c):
                            t0 = c * 128
                            po = 4 + (c % 2)
                            am = attm[c % 3]
                            qz = qzt[c % 3]

                            def omm(h):
                                ins = None
                                for hd in range(4):
                                    e, p_ = hd % 2, hd // 2
                                    o_ap = pb[po][:, hd * 128:(hd + 1) * 128]
                                    h.matmul(o_ap, lhsT=V[:, c, hd * 128:(hd + 1) * 128], rhs=am[:, hd * 128:(hd + 1) * 128], start=True, stop=False)
                                    h.matmul(o_ap, lhsT=V[:, c, hd * 128:(hd + 1) * 128], rhs=am[:, 512 + hd * 128:512 + (hd + 1) * 128], start=False, stop=False)
                                    h.matmul(o_ap, lhsT=Sst[:, 0 + p_, c, e * 128:(e + 1) * 128], rhs=qz[:, 0 + p_, e, :], start=False, stop=False)
                                    ins = h.matmul(o_ap, lhsT=Sst[:, 2 + p_, c, e * 128:(e + 1) * 128], rhs=qz[:, 2 + p_, e, :], start=False, stop=True)
                                return ins
                            P.op("pe", omm, reads=[("V", c), ("attm", c % 3, 0), ("attm", c % 3, 1)] + [("Sst", q, c) for q in range(4)] + [("qz", c % 3)],
                                 writes=[("pb", po)])
                            oq = osq[c % 2]
                            P.op("act", lambda h: h.activation(out=oq[:], in_=pb[po][:], func=AF.Square), reads=[("pb", po)], writes=[("osq", c % 2)])

                        def d2_c1(c):
                            pss = 6
                            oq = osq[c % 2]
                            ln_ = lno[c % 2]
                            P.op("pe", lambda h: h.matmul(pb[pss][:], lhsT=onesb[:], rhs=oq[:], start=True, stop=True),
                                 reads=[("osq", c % 2), "onesb"], writes=[("pb", pss)])
                            P.op("act", lambda h: h.activation(out=ln_[:], in_=pb[pss][:], func=AF.Ln, scale=1.0 / 128, bias=EPS),
                                 reads=[("pb", pss)], writes=[("lno", c % 2)])
                            P.op("act", lambda h: h.activation(out=ln_[:], in_=ln_[:], func=AF.Exp, scale=-0.5), reads=[("lno", c % 2)], writes=[("lno", c % 2)])

                        def d2_c2(c):
                            t0 = c * 128
                            po = 4 + (c % 2)
                            ln_ = lno[c % 2]
                            P.op("dve", lambda h: h.scalar_tensor_tensor(out=ln_[:], in0=pb[po][:], scalar=VTT[:, base + 256:base + 257],
                                                                         in1=ln_[:], op0=ALU.mult, op1=ALU.mult),
                                 reads=[("pb", po), ("lno", c % 2), "VTT"], writes=[("lno", c % 2)])
                            og1v = ln_[:].rearrange("p (q c) -> p q c", c=128)
                            P.op("dve", lambda h: h.tensor_tensor(out=ogT[:, :, t0:t0 + 128], in0=og1v, in1=sgT[:, :, t0:t0 + 128], op=ALU.mult),
                                 reads=[("lno", c % 2), ("sgT", c // 4)], writes=[("ogT", c)])
                        for it in range(NTC + 3):
                            if it < NTC:
                                d2_a1(it)
                            if 0 <= it - 2 < NTC:
                                d2_b(it - 2)
                            if 0 <= it - 3 < NTC:
                                d2_c1(it - 3)
                            if it < NTC:
                                d2_a2(it)
                            if 0 <= it - 3 < NTC:
                                d2_c2(it - 3)
                P.barrier()
                if l == 0:
                    with ExitStack() as dd:
                        tmpd = sb(dd, "tmpd2", [128, T], F32)
                        for nm, src in (("yfT", yfT), ("ogT", ogT)):
                            if nm in dbg_out:
                                for k in range(4):
                                    P.op("dve", (lambda k=k, src=src: lambda h: h.tensor_copy(out=tmpd[:], in_=src[:, k, :]))(), reads=list(P.keys), writes=["tmpd2"])
                                    dbg_store(nm, tmpd[:], dbg_out[nm][k * 128:(k + 1) * 128, :], ["tmpd2"])
                        P.barrier()
                if stop == "D2":
                    return "stop"
                with ExitStack() as pe_:
                    wout = sb(pe_, "wout", [128, 8, D], BF16)
                    xrE = [sb(pe_, "xrE%d" % i, [128, D], F32) for i in range(4)]
                    tmpE = [sb(pe_, "tmpE%d" % i, [128, D], F32) for i in range(4)]
                    junkE = sb(pe_, "junkE", [128, 512], BF16)
                    ss2 = sb(pe_, "ss2E", [128, 2 * NTC], F32)
                    lnv2 = sb(pe_, "lnv2E", [128, NTC], F32)
                    rstd2 = sb(pe_, "rstd2E", [128, NTC], F32)
                    for k in range(8):
                        P.op("pool", (lambda k=k: lambda h: h.dma_start(out=wout[:, k, :], in_=w_out[l, k * 128:(k + 1) * 128, :]))(),
                             writes=[("wout", k)], dma=True)
                    for tc in range(NTC):
                        t0 = tc * 128
                        b0 = (tc % 4) * 2

                        def outmm(h, t0=t0, b0=b0):
                            ins = None
                            for cb in range(2):
                                for k in range(8):
                                    src = yfT[:, k, t0:t0 + 128] if k < 4 else ogT[:, k - 4, t0:t0 + 128]
                                    ins = h.matmul(pb[b0 + cb][:], lhsT=src, rhs=wout[:, k, cb * 512:(cb + 1) * 512], start=(k == 0), stop=(k == 7))
                            return ins
                        P.op("pe", outmm, reads=[("wout", k) for k in range(8)] + [("yfT", tc // 4), ("ogT", tc)], writes=[("pb", b0), ("pb", b0 + 1)])
                        post_norm_1(tc, [pb[b0], pb[b0 + 1]], [("pb", b0), ("pb", b0 + 1)], junkE, "junkE", ss2, lnv2, "E")
                        if tc > 0:
                            pc = tc - 1
                            pb0 = (pc % 4) * 2
                            post_norm_2(l, 0, pc, [pb[pb0], pb[pb0 + 1]], [("pb", pb0), ("pb", pb0 + 1)], xrE[pc % 4], ("xrE", pc % 4),
                                        tmpE[pc % 4], ("tmpE", pc % 4), lnv2, rstd2, "E", False)
                    pc = NTC - 1
                    pb0 = (pc % 4) * 2
                    post_norm_2(l, 0, pc, [pb[pb0], pb[pb0 + 1]], [("pb", pb0), ("pb", pb0 + 1)], xrE[pc % 4], ("xrE", pc % 4),
                                tmpE[pc % 4], ("tmpE", pc % 4), lnv2, rstd2, "E", False)
            P.barrier()
            if l == 0 and "xmix" in dbg_out:
                with ExitStack() as dd:
                    tmpd = sb(dd, "tmpd3", [128, D], F32)
                    for tc in range(NTC):
                        P.op("sp", (lambda tc=tc: lambda h: h.dma_start(out=tmpd[:], in_=xs[tc * 128:(tc + 1) * 128, :]))(), reads=[("xs", tc)], writes=["tmpd3"], dma=True)
                        dbg_store("xmix", tmpd[:], dbg_out["xmix"][tc * 128:(tc + 1) * 128, :], ["tmpd3"])
                    P.barrier()
            if stop == "E":
                return "stop"
            with ExitStack() as ffn:
                h2x = sb(ffn, "h2x", [128, 8, 32, 66], BF16)
                P.op("dve", lambda h: h.memset(h2x[:], 0.0), writes=[("h2x", g, k) for g in range(8) for k in range(8)] + ["h2xhalo"])
                with ExitStack() as pf:
                    def dstf(k, g):
                        return h2x[:, k, g * 4:(g + 1) * 4, 1:65], ("h2x", g, k)
                    norm_to_T(pf, l, 1, dstf, "F")
                P.barrier()
                h2k = [("h2x", g, k) for g in range(8) for k in range(8)]
                P.op("dve", lambda h: h.tensor_tensor(out=h2x[:, :, 1:32, 0], in0=h2x[:, :, 0:31, 64],
                                                      in1=hm[:, 1:32].unsqueeze(1).to_broadcast([128, 8, 31]), op=ALU.mult),
                     reads=h2k + ["hm"], writes=["h2xhalo"])
                P.op("dve", lambda h: h.tensor_tensor(out=h2x[:, :, 0:31, 65], in0=h2x[:, :, 1:32, 1],
                                                      in1=hm[:, 32:63].unsqueeze(1).to_broadcast([128, 8, 31]), op=ALU.mult),
                     reads=h2k + ["hm"], writes=["h2xhalo"])
                h2xf = h2x[:].rearrange("p k s c -> p k (s c)")
                wdn = sb(ffn, "wdn", [128, NPAIR, D], BF16)
                aT = sb(ffn, "aT", [128, NPAIR, 1024], BF16)
                wup = [sb(ffn, "wup%d" % i, [128, 8, 512], BF16) for i in range(3)]
                NB = int(os.environ.get('FFN_NB', '4'))
                t1 = [sb(ffn, "t1_%d" % i, [128, 6, 64], F32) for i in range(NB)]
                g1 = [sb(ffn, "g1_%d" % i, [128, 6, 64], F32) for i in range(NB)]
                xrG = [sb(ffn, "xrG%d" % i, [128, D], F32) for i in range(2)]
                tmpG = [sb(ffn, "tmpG%d" % i, [128, D], F32) for i in range(2)]
                junkG = sb(ffn, "junkG", [128, 512], BF16)
                ss2g = sb(ffn, "ss2G", [128, 2 * NTC], F32)
                lnv2g = sb(ffn, "lnv2G", [128, NTC], F32)
                rstd2g = sb(ffn, "rstd2G", [128, NTC], F32)
                wupsrc = w_up[l].rearrange("(k p) c -> p k c", p=128)
                WU = [(hf, u) for hf in range(2) for u in range(NPAIR // 2)]

                def wup_dma(wi, part=None):
                    hf, u = WU[wi]
                    wr = wup[wi % 3]
                    wkey = ("wup", wi % 3)
                    for pt in (range(4) if part is None else [part]):
                        half, kq = pt // 2, pt % 2
                        c_src = (DFF if half else 0) + u * 256
                        P.op("pool", (lambda half=half, kq=kq, c_src=c_src: lambda h: h.dma_start(
                            out=wr[:, kq * 4:(kq + 1) * 4, half * 256:(half + 1) * 256],
                            in_=wupsrc[:, kq * 4:(kq + 1) * 4, c_src:c_src + 256]))(), writes=[wkey], dma=True)

                def wdn_dma():
                    for i in range(NPAIR):
                        P.op("pool", (lambda i=i: lambda h: h.dma_start(out=wdn[:, i, :], in_=w_down[l, i * 128:(i + 1) * 128, :]))(),
                             writes=[("wdn", i)], dma=True)
                wup_dma(0)
                wup_dma(1)
                BLKS = [(0, 6), (6, 5), (11, 5)] if os.environ.get('FFN_BLK', '655') == '655' else [(0, 4), (4, 4), (8, 4), (12, 4)]
                cnt2 = 0
                for hf in range(2):
                    for u in range(NPAIR // 2):
                        wi = hf * (NPAIR // 2) + u
                        if wi == 1:
                            wdn_dma()
                        wr = wup[wi % 3]
                        wkey = ("wup", wi % 3)
                        uidx = 0
                        for (sl0, nsg) in BLKS:
                            sg0 = hf * 16 + sl0
                            ncol = nsg * 66
                            for ii in range(2):
                                i = 2 * u + ii
                                if wi + 2 < len(WU) and uidx < 4:
                                    wup_dma(wi + 2, uidx)
                                uidx += 1
                                r2 = cnt2 % NB
                                bv = r2 * 2
                                bg = bv + 1
                                cnt2 += 1

                                def upmm(h, wr=wr, ii=ii, sg0=sg0, ncol=ncol, bv=bv, bg=bg):
                                    ins = None
                                    rhsv = [h2xf[:, k, sg0 * 66:sg0 * 66 + ncol] for k in range(8)]
                                    for k in range(8):
                                        h.matmul(pb[bv][:, 0:ncol], lhsT=wr[:, k, ii * 128:(ii + 1) * 128], rhs=rhsv[k], start=(k == 0), stop=(k == 7))
                                    for k in range(8):
                                        ins = h.matmul(pb[bg][:, 0:ncol], lhsT=wr[:, k, 256 + ii * 128:256 + (ii + 1) * 128], rhs=rhsv[k], start=(k == 0), stop=(k == 7))
                                    return ins
                                P.op("pe", upmm, reads=[wkey, "h2xhalo"] + h2k, writes=[("pb", bv), ("pb", bg)])
                                chains = []
                                for (bank, dstt, dkey, coff) in ((bv, t1[r2], ("t1", r2), i), (bg, g1[r2], ("g1", r2), NPAIR + i)):
                                    pvw = pb[bank][:, 0:ncol].rearrange("p (s c) -> p s c", c=66)
                                    dv = dstt[:, 0:nsg, :]
                                    cws = [VTT[:, base + j * 44 + coff:base + j * 44 + coff + 1] for j in range(3)]
                                    cbb = VTT[:, base + 132 + coff:base + 132 + coff + 1]
                                    chains.append((bank, pvw, dv, dkey, cws, cbb))
                                for (bank, pvw, dv, dkey, cws, cbb) in chains:
                                    P.op("act", (lambda pvw=pvw, dv=dv, cws=cws, cbb=cbb: lambda h: h.activation(
                                        out=dv, in_=pvw[:, :, 1:65], func=AF.Identity, scale=cws[1], bias=cbb))(),
                                        reads=[("pb", bank), "VTT"], writes=[dkey])
                                for tap, lo in ((0, 0), (2, 2)):
                                    for (bank, pvw, dv, dkey, cws, cbb) in chains:
                                        P.op("dve", (lambda pvw=pvw, dv=dv, cws=cws, tap=tap, lo=lo: lambda h: h.scalar_tensor_tensor(
                                            out=dv, in0=pvw[:, :, lo:lo + 64], scalar=cws[tap], in1=dv, op0=ALU.mult, op1=ALU.add))(),
                                            reads=[("pb", bank), "VTT", dkey], writes=[dkey])
                                sv = chains[1][2]
                                P.op("act", (lambda sv=sv: lambda h: h.activation(out=sv, in_=sv, func=AF.Silu))(),
                                     reads=[("g1", r2)], writes=[("g1", r2)])
                                a_dst = aT[:, i, sl0 * 64:(sl0 + nsg) * 64].rearrange("p (s c) -> p s c", c=64)
                                P.op("pool", (lambda sv=sv, tv=chains[0][2], a_dst=a_dst: lambda h: h.tensor_tensor(out=a_dst, in0=sv, in1=tv, op=ALU.mult))(),
                                     reads=[("g1", r2), ("t1", r2)], writes=[("aT", i)])
                    P.barrier()
                    for tcl in range(8):
                        tc = hf * 8 + tcl
                        b0 = (tcl % 4) * 2

                        def dnmm(h, tcl=tcl, b0=b0):
                            ins = None
                            for cb in range(2):
                                for i in range(NPAIR):
                                    ins = h.matmul(pb[b0 + cb][:], lhsT=aT[:, i, tcl * 128:(tcl + 1) * 128], rhs=wdn[:, i, cb * 512:(cb + 1) * 512],
                                                   start=(i == 0), stop=(i == NPAIR - 1))
                            return ins
                        P.op("pe", dnmm, reads=[("wdn", i) for i in range(NPAIR)] + [("aT", i) for i in range(NPAIR)],
                             writes=[("pb", b0), ("pb", b0 + 1)])
                        post_norm_1(tc, [pb[b0], pb[b0 + 1]], [("pb", b0), ("pb", b0 + 1)], junkG, "junkG", ss2g, lnv2g, "G")
                        if tcl > 0:
                            pc = tc - 1
                            pb0 = ((tcl - 1) % 4) * 2
                            post_norm_2(l, 1, pc, [pb[pb0], pb[pb0 + 1]], [("pb", pb0), ("pb", pb0 + 1)], xrG[pc % 2], ("xrG", pc % 2),
                                        tmpG[pc % 2], ("tmpG", pc % 2), lnv2g, rstd2g, "G", last)
                    pc = hf * 8 + 7
                    pb0 = (7 % 4) * 2
                    post_norm_2(l, 1, pc, [pb[pb0], pb[pb0 + 1]], [("pb", pb0), ("pb", pb0 + 1)], xrG[pc % 2], ("xrG", pc % 2),
                                tmpG[pc % 2], ("tmpG", pc % 2), lnv2g, rstd2g, "G", last)
                    P.barrier()
        for l_ in range(n_layers):
            if stop == "stage0" or do_layer(l_) == "stop":
                break
        P.op("sp", None, reads=list(dict.fromkeys(out_keys)))
        run_prog(nc, P)
    return nc, P


_CACHE = {}


def _consts():
    if "c" in _CACHE:
        return _CACHE["c"]
    bf = ml_dtypes.bfloat16

    def dft(n):
        k = np.arange(n)
        ang = 2.0 * np.pi * ((np.outer(k, k) % n).astype(np.float64)) / n
        return np.cos(ang), np.sin(ang)
    c2048, s2048 = dft(2048)
    c256, s256 = dft(256)
    c128, s128 = dft(128)
    samp_c = c2048 / np.sqrt(2048.0)
    samp_s = -s2048 / np.sqrt(2048.0)
    pr_c = np.zeros((2048, 2048))
    pr_s = np.zeros((2048, 2048))
    for i in range(8):
        pr_c[i * 256:(i + 1) * 256, i * 256:(i + 1) * 256] = c256 / 16.0
        pr_s[i * 256:(i + 1) * 256, i * 256:(i + 1) * 256] = -s256 / 16.0

    def tiles(cm, sm):
        out = np.zeros((64, 128, 1024), np.float32)
        for tpb in range(4):
            for tc in range(16):
                out[tpb * 16 + tc, :, 0:512] = cm[tc * 128:(tc + 1) * 128, tpb * 512:(tpb + 1) * 512]
                out[tpb * 16 + tc, :, 512:1024] = sm[tc * 128:(tc + 1) * 128, tpb * 512:(tpb + 1) * 512]
        return out.astype(bf)
    cst_s = tiles(samp_c, samp_s)
    cst_p = tiles(pr_c, pr_s)
    cc = np.concatenate([c128, s128], axis=1) / np.sqrt(128.0)
    j = np.arange(128)[:, None]
    i = np.arange(128)[None, :]
    mf = (j <= i).astype(np.float32)
    mb = (j >= i).astype(np.float32)
    mk = np.concatenate([np.tile(mf, (1, 4)), np.tile(mb, (1, 4))], axis=1).astype(bf)
    u = np.concatenate([mf, mb], axis=1).astype(np.float32) * (-1.0 / 16.0)
    q = 256
    omega = (1.0 / (10000.0 ** (np.arange(q, dtype=np.float32) / q))).astype(np.float32)
    er = np.arange(32, dtype=np.float32)[:, None] * omega
    ec = np.arange(64, dtype=np.float32)[:, None] * omega
    prr = np.concatenate([np.sin(er), np.cos(er)], axis=-1)
    pcc = np.concatenate([np.sin(ec), np.cos(ec)], axis=-1)
    pe = np.concatenate([np.broadcast_to(prr[:, None], (32, 64, 512)), np.broadcast_to(pcc[None], (32, 64, 512))], axis=-1)
    pe = np.ascontiguousarray(pe.reshape(2048, 1024).astype(np.float32))
    hm_s = np.zeros((128, 64), np.float32)
    hm_p = np.zeros((128, 64), np.float32)
    for s in range(32):
        hm_p[:, s] = 0.0 if s % 4 == 0 else 1.0
        hm_p[:, 32 + s] = 0.0 if s % 4 == 3 else 1.0
    c = dict(cst_s=cst_s, cst_p=cst_p, cc=cc.astype(bf), mk=mk, u=u, pe=pe, pe0=np.zeros_like(pe),
             hm_s=hm_s, hm_p=hm_p, idf=np.eye(128, dtype=np.float32), idb=np.eye(128).astype(bf))
    _CACHE["c"] = c
    return c


def _in_maps(inp):
    c = _consts()
    f = lambda a: np.ascontiguousarray(np.asarray(a, dtype=np.float32))
    x_prompt, x_sample = f(inp["x_prompt"]), f(inp["x_sample"])
    state = f(inp["state_gla"])
    cvs = f(inp["c"])
    cctx = f(inp["c_ctx"])
    wgate = np.zeros((2, 33, 512), np.float32)
    wgate[:, 0:16, 0:256] = f(inp["w_gate_f"])
    wgate[:, 16:32, 256:512] = f(inp["w_gate_b"])
    wgate[:, 32, 0:256] = f(inp["b_gate_f"])
    wgate[:, 32, 256:512] = f(inp["b_gate_b"])

    def vt_for(cvec):
        vt = np.zeros((VT_ROWS, 128), np.float32)
        vt[0:8] = cvec.reshape(8, 128)
        for l in range(2):
            b = LBASE(l)
            vt[b:b + 132] = f(inp["conv_w"])[l].reshape(132, 128)
            vt[b + 132:b + 176] = f(inp["conv_b"])[l].reshape(44, 128)
            vt[b + 176:b + 224] = f(inp["b_ada"])[l].reshape(48, 128)
            vt[b + 224:b + 232] = f(inp["g_pre_mix"])[l].reshape(8, 128)
            vt[b + 232:b + 240] = f(inp["g_post_mix"])[l].reshape(8, 128)
            vt[b + 240:b + 248] = f(inp["g_pre_ffn"])[l].reshape(8, 128)
            vt[b + 248:b + 256] = f(inp["g_post_ffn"])[l].reshape(8, 128)
            vt[b + 256] = f(inp["g_gla"])[l]
        return vt
    shared = dict(w_ada=f(inp["w_ada"]), w_in=f(inp["w_in"]), wgate=wgate, w_out=f(inp["w_out"]), w_up=f(inp["w_up"]),
                  w_down=f(inp["w_down"]), cc=c["cc"], mk=c["mk"], u=c["u"], idf=c["idf"], idb=c["idb"])
    maps = []
    for core in range(8):
        m = dict(shared)
        if core < 4:
            b = core
            m["x"] = x_sample[b]
            m["pe"] = c["pe"]
            m["vt"] = vt_for(cvs[b])
            m["s0"] = np.ascontiguousarray(state[b])
            m["kp"] = np.ones((128, 1), np.float32)
            m["hm"] = c["hm_s"]
            m["cst"] = c["cst_s"]
        else:
            j = core - 4
            m["x"] = np.ascontiguousarray(x_prompt[8 * j:8 * j + 8].reshape(T, D))
            m["pe"] = c["pe0"]
            m["vt"] = vt_for(cctx)
            m["s0"] = np.zeros((2, 2, 4, 64, 128), np.float32)
            m["kp"] = np.zeros((128, 1), np.float32)
            m["hm"] = c["hm_p"]
            m["cst"] = c["cst_p"]
        maps.append(m)
    return maps


def kernel(**inputs):
    if "nc" not in _CACHE:
        _CACHE["nc"] = build_nc()[0]
    nc = _CACHE["nc"]
    maps = _in_maps(inputs)
    res = run_bass_kernel_spmd(nc, maps, core_ids=list(range(8)))
    r = res.results
    y_sample = np.stack([r[b]["y"] for b in range(4)], axis=0).astype(np.float32)
    y_prompt = np.concatenate([r[4 + j]["y"].reshape(8, 256, D) for j in range(4)], axis=0).astype(np.float32)
    ns = np.concatenate([np.transpose(r[4 + j]["ns"], (1, 0, 2, 3, 4, 5)) for j in range(4)], axis=0).astype(np.float32)
    return y_prompt, y_sample, ns
```

```python
import os
import numpy as np
import ml_dtypes
from contextlib import ExitStack
import concourse.bass as bass
import concourse.mybir as mybir
from concourse.bass_utils import run_bass_kernel_spmd

F32 = mybir.dt.float32
BF16 = mybir.dt.bfloat16
AF = mybir.ActivationFunctionType
ALU = mybir.AluOpType

ENGS = ("pe", "act", "dve", "pool", "sp")
N_DMA_SEMS = 16

T = 2048
D = 1024
NK = 8
NTC = 16
IN_COLS = 2080
DFF = 2816
NPAIR = 22
EPS = 1e-6
VT_ROWS = 640
LBASE = lambda l: 8 + l * 257


class Prog:
    def __init__(self):
        self.ops = []
        self.keys = set()

    def op(self, eng, fn, reads=(), writes=(), dma=False):
        self.ops.append((eng, fn, tuple(reads), tuple(writes), dma))
        self.keys.update(reads)
        self.keys.update(writes)

    def barrier(self):
        allk = tuple(self.keys)
        self.op("sp", lambda h: h.nop(), reads=(), writes=allk + ("__bar",))
        for e in ("pe", "act", "dve", "pool"):
            self.op(e, None, reads=("__bar",))

    def analyze(self):
        ops = self.ops
        n = len(ops)
        last_writer = {}
        readers = {}
        need = [None] * n
        signal = [False] * n
        for i, (eng, fn, reads, writes, dma) in enumerate(ops):
            raw = set()
            other = set()
            for r in reads:
                j = last_writer.get(r)
                if j is not None:
                    raw.add(j)
            for w in writes:
                j = last_writer.get(w)
                if j is not None:
                    other.add(j)
                for j in readers.get(w, ()):
                    other.add(j)
            other -= raw
            other.discard(i)
            raw.discard(i)
            keep = {}
            for j, is_raw in [(j, True) for j in raw] + [(j, False) for j in other]:
                ej, _, _, _, dj = ops[j]
                if dj:
                    keep[("d", j)] = j
                    continue
                if ej == eng and not dma:
                    if eng == "pe" or not is_raw:
                        continue
                k = ("e", ej)
                if k not in keep or keep[k] < j:
                    keep[k] = j
            need[i] = sorted(keep.values())
            for j in need[i]:
                signal[j] = True
            for r in reads:
                readers.setdefault(r, []).append(i)
            for w in writes:
                last_writer[w] = i
                readers[w] = []
        cnt = {e: 0 for e in ENGS}
        rr = {e: 0 for e in ENGS}
        dcnt = {}
        dprev = {}
        sig = [None] * n
        waits = [None] * n
        seen = {e: {} for e in ENGS}
        for i, (eng, fn, reads, writes, dma) in enumerate(ops):
            w = []
            for j in need[i]:
                w.append((sig[j][0], sig[j][1]))
            if dma:
                s = ("d", eng, rr[eng] % N_DMA_SEMS)
                rr[eng] += 1
                if s in dprev:
                    w.append((s, dprev[s]))
                dcnt[s] = dcnt.get(s, 0) + 16
                sig[i] = (s, dcnt[s], 16)
                dprev[s] = dcnt[s]
            elif signal[i]:
                cnt[eng] += 1
                sig[i] = (("e", eng), cnt[eng], 1)
            m = {}
            for (k, v) in w:
                if seen[eng].get(k, 0) >= v:
                    continue
                m[k] = max(m.get(k, 0), v)
            for k, v in m.items():
                seen[eng][k] = v
            waits[i] = list(m.items())
        self.sig = sig
        self.waits = waits
        self.semkeys = sorted({s[0] for s in sig if s is not None} |
                              {k for w in waits for (k, v) in w}, key=str)
        self.stats = dict(n_ops=n, signals=dict(cnt), n_waits=sum(len(w) for w in waits),
                          per_eng={e: sum(1 for o in ops if o[0] == e) for e in ENGS})

    def emit_engine(self, eng, h, sems):
        for i, (e, fn, reads, writes, dma) in enumerate(self.ops):
            if e != eng:
                continue
            for (k, v) in self.waits[i]:
                h.wait_ge(sems[k], v)
            if fn is None:
                if self.sig[i] is not None:
                    h.nop().then_inc(sems[self.sig[i][0]], self.sig[i][2])
                continue
            ins = fn(h)
            if self.sig[i] is not None:
                assert ins is not None, ("op must return an instruction", i, e)
                ins.then_inc(sems[self.sig[i][0]], self.sig[i][2])


def run_prog(nc, prog):
    prog.analyze()
    with ExitStack() as st:
        sems = {}
        for k in prog.semkeys:
            sems[k] = st.enter_context(nc.semaphore("s_" + "_".join(str(x) for x in k)))
        block = st.enter_context(nc.Block())

        @block.tensor
        def _(h):
            prog.emit_engine("pe", h, sems)

        @block.scalar
        def _(h):
            prog.emit_engine("act", h, sems)

        @block.vector
        def _(h):
            prog.emit_engine("dve", h, sems)

        @block.gpsimd
        def _(h):
            prog.emit_engine("pool", h, sems)

        @block.sync
        def _(h):
            prog.emit_engine("sp", h, sems)


class _Stop(Exception):
    pass


def build_nc(n_layers=2, dbg=None, stop=None):
    nc = bass.Bass("TRN2", target_bir_lowering=False)
    dt_in = lambda name, shape, dt=F32: nc.dram_tensor(name, list(shape), dt, kind="ExternalInput").ap()
    x_in = dt_in("x", [T, D])
    pe_in = dt_in("pe", [T, D])
    vt_in = dt_in("vt", [VT_ROWS, 128])
    s0_in = dt_in("s0", [2, 2, 4, 64, 128])
    kp_in = dt_in("kp", [128, 1])
    hm_in = dt_in("hm", [128, 64])
    w_ada = dt_in("w_ada", [2, D, 6 * D])
    w_in = dt_in("w_in", [2, D, IN_COLS])
    wgate = dt_in("wgate", [2, 33, 512])
    w_out = dt_in("w_out", [2, D, D])
    w_up = dt_in("w_up", [2, D, 2 * DFF])
    w_down = dt_in("w_down", [2, DFF, D])
    cst_in = dt_in("cst", [64, 128, 1024], BF16)
    cc_in = dt_in("cc", [128, 256], BF16)
    mk_in = dt_in("mk", [128, 1024], BF16)
    u_in = dt_in("u", [128, 256])
    idf_in = dt_in("idf", [128, 128])
    idb_in = dt_in("idb", [128, 128], BF16)
    y_out = nc.dram_tensor("y", [T, D], F32, kind="ExternalOutput").ap()
    ns_out = nc.dram_tensor("ns", [2, 8, 2, 4, 64, 128], F32, kind="ExternalOutput").ap()
    xs = nc.dram_tensor("xs", [T, D], F32, kind="Internal").ap()
    modscr = nc.dram_tensor("modscr", [2, 16, 128], F32, kind="Internal").ap()
    dbg_out = {}
    if dbg:
        for name, shape in dbg.items():
            dbg_out[name] = nc.dram_tensor("dbg_" + name, list(shape), F32, kind="ExternalOutput").ap()

    P = Prog()
    out_keys = []

    with ExitStack() as top:
        uid = [0]

        def sb(st, name, shape, dt):
            uid[0] += 1
            return st.enter_context(nc.sbuf_tensor("sb%d_%s" % (uid[0], name), list(shape), dt))

        def ps(st, name, shape, dt=F32):
            uid[0] += 1
            return st.enter_context(nc.psum_tensor("ps%d_%s" % (uid[0], name), list(shape), dt))

        identb = sb(top, "identb", [128, 128], BF16)
        identf = sb(top, "identf", [128, 128], F32)
        onesb = sb(top, "onesb", [128, 128], BF16)
        cc = sb(top, "cc", [128, 256], BF16)
        mk = sb(top, "mk", [128, 1024], BF16)
        uu = sb(top, "uu", [128, 256], F32)
        uub = sb(top, "uub", [128, 256], BF16)
        hm = sb(top, "hm", [128, 64], F32)
        kp = sb(top, "kp", [128, 1], F32)
        VTT = sb(top, "VTT", [128, VT_ROWS], F32)
        scb = sb(top, "scb", [128, 8], BF16)
        modT = sb(top, "modT", [128, 2, 48], F32)
        gs = sb(top, "gs", [128, 2, 16], F32)
        gt = sb(top, "gt", [128, 2, 16], F32)
        gtT = sb(top, "gtT", [16, 128], F32)
        gg = sb(top, "gg", [128, 2048], F32)
        onesf = sb(top, "onesf", [1, 128], F32)
        wada = [sb(top, "wada%d" % i, [128, 8, 256], BF16) for i in range(3)]
        pb = [ps(top, "pb%d" % i, [128, 512]) for i in range(8)]
        pbm = pb[3][:, 256:512]

        def load(eng, dst, src, key, reads=()):
            P.op(eng, lambda h: h.dma_start(out=dst, in_=src), reads=reads, writes=[key], dma=True)

        load("sp", identb[:], idb_in[:, :], "identb")
        load("sp", identf[:], idf_in[:, :], "identf")
        load("sp", cc[:], cc_in[:, :], "cc")
        load("sp", mk[:], mk_in[:, :], "mk")
        load("sp", uu[:], u_in[:, :], "uu")
        load("sp", hm[:], hm_in[:, :], "hm")
        load("sp", kp[:], kp_in[:, :], "kp")
        P.op("dve", lambda h: h.memset(onesb[:], 1.0), writes=["onesb"])
        P.op("dve", lambda h: h.memset(onesf[:], 1.0), writes=["onesf"])
        P.op("dve", lambda h: h.tensor_copy(out=uub[:], in_=uu[:]), reads=["uu"], writes=["uub"])

        with ExitStack() as s0s:
            vtt = [sb(s0s, "vtt%d" % i, [128, 128], F32) for i in range(2)]
            for i in range(VT_ROWS // 128):
                t = vtt[i % 2]
                load("sp", t[:], vt_in[i * 128:(i + 1) * 128, :], ("vtt", i % 2))
                P.op("pe", (lambda t=t, i=i: lambda h: h.matmul(pb[i % 2][:, 0:128], lhsT=t[:], rhs=identf[:], start=True, stop=True))(),
                     reads=[("vtt", i % 2), "identf"], writes=[("pb", i % 2)])
                P.op("dve", (lambda i=i: lambda h: h.tensor_copy(out=VTT[:, i * 128:(i + 1) * 128], in_=pb[i % 2][:, 0:128]))(),
                     reads=[("pb", i % 2)], writes=["VTT"])
            P.op("act", lambda h: h.activation(out=scb[:], in_=VTT[:, 0:8], func=AF.Silu), reads=["VTT"], writes=["scb"])

        modcnt = [0]

        def mod_dma(l, cb):
            wsrc = w_ada[l].rearrange("(k p) c -> p k c", p=128)
            r = (l * 24 + cb) % 3
            t = wada[r]
            P.op("pool", lambda h: h.dma_start(out=t[:], in_=wsrc[:, :, cb * 256:(cb + 1) * 256]), writes=[("wada", r)], dma=True)

        def mod_mm(l, cb):
            r = (l * 24 + cb) % 3
            t = wada[r]

            def mmod(h):
                ins = None
                for j in range(2):
                    fc = cb * 2 + j
                    for k in range(8):
                        ins = h.matmul(pbm[:, l * 48 + fc:l * 48 + fc + 1], lhsT=t[:, k, j * 128:(j + 1) * 128],
                                       rhs=scb[:, k:k + 1], start=(k == 0), stop=(k == 7))
                return ins
            P.op("pe", mmod, reads=[("wada", r), "scb"], writes=[("pb", 3)])

        def mod_finish(l):
            base = LBASE(l)
            P.op("dve", lambda h: h.tensor_tensor(out=modT[:, l, :], in0=pbm[:, l * 48:(l + 1) * 48],
                                                  in1=VTT[:, base + 176:base + 224], op=ALU.add),
                 reads=[("pb", 3), "VTT"], writes=[("modT", l)])
            for j, (sc0, g0) in enumerate([(8, 224), (32, 240)]):
                P.op("dve", (lambda j=j, sc0=sc0, g0=g0: lambda h: h.scalar_tensor_tensor(
                    out=gs[:, l, j * 8:(j + 1) * 8], in0=modT[:, l, sc0:sc0 + 8], scalar=1.0,
                    in1=VTT[:, base + g0:base + g0 + 8], op0=ALU.add, op1=ALU.mult))(),
                    reads=[("modT", l), "VTT"], writes=[("gs", l)])
            for j, (sc0, g0) in enumerate([(16, 232), (40, 248)]):
                P.op("dve", (lambda j=j, sc0=sc0, g0=g0: lambda h: h.tensor_tensor(
                    out=gt[:, l, j * 8:(j + 1) * 8], in0=modT[:, l, sc0:sc0 + 8],
                    in1=VTT[:, base + g0:base + g0 + 8], op=ALU.mult))(),
                    reads=[("modT", l), "VTT"], writes=[("gt", l)])
            P.op("pe", lambda h: h.matmul(pbm[0:16, 128:256], lhsT=gt[:, l, :], rhs=identf[:], start=True, stop=True),
                 reads=[("gt", l), "identf"], writes=[("pb", 3)])
            P.op("dve", lambda h: h.tensor_copy(out=gtT[:], in_=pbm[0:16, 128:256]), reads=[("pb", 3)], writes=["gtT"])
            P.op("sp", lambda h: h.dma_start(out=modscr[l], in_=gtT[:]), reads=["gtT"], writes=[("modscr", l)], dma=True)

        mod_dma(0, 0)
        mod_dma(0, 1)
        for cb in range(24):
            if cb + 2 < 24:
                mod_dma(0, cb + 2)
            mod_mm(0, cb)
        mod_finish(0)
        import os
        MOD_IL = os.environ.get("MOD_IL", "1") == "1"
        if not MOD_IL and n_layers > 1:
            mod_dma(1, 0)
            mod_dma(1, 1)
            for cb in range(24):
                if cb + 2 < 24:
                    mod_dma(1, cb + 2)
                mod_mm(1, cb)
            mod_finish(1)
        P.barrier()

        def norm_to_T(st, l, which, dst_fn, tag, first=False):
            xr = [sb(st, tag + "xr%d" % i, [128, D], F32) for i in range(4)]
            xn = [sb(st, tag + "xn%d" % i, [128, D], BF16) for i in range(4)]
            per = [sb(st, tag + "per%d" % i, [128, D], F32) for i in range(4)] if first else None
            junk = sb(st, tag + "junk", [128, D], BF16)
            ss = sb(st, tag + "ss", [128, NTC], F32)
            lnv = sb(st, tag + "lnv", [128, NTC], F32)
            rstd = sb(st, tag + "rstd", [128, NTC], F32)
            gcol = which * 8
            shcol = 0 if which == 0 else 24
            def stage1(g):
                pT = [pb[(g % 4) * 2], pb[(g % 4) * 2 + 1]]
                pkeys = [("pb", (g % 4) * 2), ("pb", (g % 4) * 2 + 1)]
                for j in range(2):
                    tc = 2 * g + j
                    r = tc % 4
                    if first:
                        load("sp", xr[r][:], x_in[tc * 128:(tc + 1) * 128, :], (tag + "xr", r))
                        load("sp", per[r][:], pe_in[tc * 128:(tc + 1) * 128, :], (tag + "per", r))
                        P.op("dve", (lambda r=r: lambda h: h.tensor_tensor(out=xr[r][:], in0=xr[r][:], in1=per[r][:], op=ALU.add))(),
                             reads=[(tag + "xr", r), (tag + "per", r)], writes=[(tag + "xr", r)])
                        P.op("pool", (lambda r=r, tc=tc: lambda h: h.dma_start(out=xs[tc * 128:(tc + 1) * 128, :], in_=xr[r][:]))(),
                             reads=[(tag + "xr", r)], writes=[("xs", tc)], dma=True)
                    else:
                        load("sp", xr[r][:], xs[tc * 128:(tc + 1) * 128, :], (tag + "xr", r), reads=[("xs", tc)])
                    P.op("act", (lambda r=r, tc=tc: lambda h: h.activation(out=junk[:], in_=xr[r][:], func=AF.Square, accum_out=ss[:, tc:tc + 1]))(),
                         reads=[(tag + "xr", r)], writes=[tag + "junk", (tag + "ss", tc)])
                    P.op("act", (lambda tc=tc: lambda h: h.activation(out=lnv[:, tc:tc + 1], in_=ss[:, tc:tc + 1], func=AF.Ln, scale=1.0 / D, bias=EPS))(),
                         reads=[(tag + "ss", tc)], writes=[(tag + "lnv", tc)])
                    P.op("act", (lambda tc=tc: lambda h: h.activation(out=rstd[:, tc:tc + 1], in_=lnv[:, tc:tc + 1], func=AF.Exp, scale=-0.5))(),
                         reads=[(tag + "lnv", tc)], writes=[(tag + "rstd", tc)])
                    P.op("dve", (lambda r=r, tc=tc: lambda h: h.tensor_scalar(out=xn[tc % 4][:], in0=xr[r][:], scalar1=rstd[:, tc:tc + 1], scalar2=None, op0=ALU.mult))(),
                         reads=[(tag + "xr", r), (tag + "rstd", tc)], writes=[(tag + "xn", tc % 4)])

                    def tr(h, tc=tc, j=j, pT=pT):
                        ins = None
                        for k in range(8):
                            bank = pT[k // 4]
                            dst = bank[:].bitcast(BF16)[:, (k % 4) * 256 + j * 128:(k % 4) * 256 + (j + 1) * 128]
                            ins = h.transpose(dst, xn[tc % 4][:, k * 128:(k + 1) * 128], identb[:])
                        return ins
                    P.op("pe", tr, reads=[(tag + "xn", tc % 4), "identb"], writes=pkeys)

            def stage2(g):
                pT = [pb[(g % 4) * 2], pb[(g % 4) * 2 + 1]]
                pkeys = [("pb", (g % 4) * 2), ("pb", (g % 4) * 2 + 1)]
                for k in range(8):
                    bank = pT[k // 4]
                    src = bank[:].bitcast(BF16)[:, (k % 4) * 256:(k % 4 + 1) * 256]
                    dst, dkey = dst_fn(k, g)
                    if len(dst.shape) == 3:
                        src = src.rearrange("p (s c) -> p s c", c=dst.shape[2])
                    if k < 4:
                        P.op("act", (lambda src=src, dst=dst, k=k: lambda h: h.activation(
                            out=dst, in_=src, func=AF.Identity, scale=gs[:, l, gcol + k:gcol + k + 1],
                            bias=modT[:, l, shcol + k:shcol + k + 1]))(),
                            reads=pkeys + [("gs", l), ("modT", l)], writes=[dkey])
                    else:
                        P.op("dve", (lambda src=src, dst=dst, k=k: lambda h: h.tensor_scalar(
                            out=dst, in0=src, scalar1=gs[:, l, gcol + k:gcol + k + 1],
                            scalar2=modT[:, l, shcol + k:shcol + k + 1], op0=ALU.mult, op1=ALU.add))(),
                            reads=pkeys + [("gs", l), ("modT", l)], writes=[dkey])
            stage1(0)
            for g in range(8):
                if g + 1 < 8:
                    stage1(g + 1)
                stage2(g)

        def post_norm_1(tc, py, pykeys, junk, junkkey, ss2, lnv2, tag, xr=None, xrkey=None):
            if xr is not None:
                load("sp", xr[:], xs[tc * 128:(tc + 1) * 128, :], xrkey, reads=[("xs", tc)])
            for cb in range(2):
                P.op("act", (lambda cb=cb: lambda h: h.activation(out=junk[:, 0:512], in_=py[cb][:], func=AF.Square, accum_out=ss2[:, 2 * tc + cb:2 * tc + cb + 1]))(),
                     reads=[pykeys[cb]], writes=[junkkey, (tag + "ss2", tc, cb)])
            P.op("dve", lambda h: h.tensor_tensor(out=lnv2[:, tc:tc + 1], in0=ss2[:, 2 * tc:2 * tc + 1], in1=ss2[:, 2 * tc + 1:2 * tc + 2], op=ALU.add),
                 reads=[(tag + "ss2", tc, 0), (tag + "ss2", tc, 1)], writes=[(tag + "lnv2", tc)])

        def post_norm_2(l, which, tc, py, pykeys, xr, xrkey, tmp, tmpkey, lnv2, rstd2, tag, final):
            P.op("act", lambda h: h.activation(out=lnv2[:, tc:tc + 1], in_=lnv2[:, tc:tc + 1], func=AF.Ln, scale=1.0 / D, bias=EPS),
                 reads=[(tag + "lnv2", tc)], writes=[(tag + "lnv2", tc)])
            P.op("act", lambda h: h.activation(out=rstd2[:, tc:tc + 1], in_=lnv2[:, tc:tc + 1], func=AF.Exp, scale=-0.5),
                 reads=[(tag + "lnv2", tc)], writes=[(tag + "rstd2", tc)])
            for cb in range(2):
                P.op("dve", (lambda cb=cb: lambda h: h.scalar_tensor_tensor(
                    out=tmp[:, cb * 512:(cb + 1) * 512], in0=py[cb][:], scalar=rstd2[:, tc:tc + 1],
                    in1=gg[:, which * 1024 + cb * 512:which * 1024 + (cb + 1) * 512], op0=ALU.mult, op1=ALU.mult))(),
                    reads=[pykeys[cb], (tag + "rstd2", tc), "gg"], writes=[tmpkey])
            P.op("dve", lambda h: h.tensor_tensor(out=xr[:], in0=xr[:], in1=tmp[:], op=ALU.add),
                 reads=[xrkey, tmpkey], writes=[xrkey])
            if final:
                P.op("pool", lambda h: h.dma_start(out=y_out[tc * 128:(tc + 1) * 128, :], in_=xr[:]),
                     reads=[xrkey], writes=[("yout", tc)], dma=True)
                out_keys.append(("yout", tc))
            else:
                P.op("pool", lambda h: h.dma_start(out=xs[tc * 128:(tc + 1) * 128, :], in_=xr[:]),
                     reads=[xrkey], writes=[("xs", tc)], dma=True)

        def dbg_store(name, src_ap, dst_ap, rkeys):
            if name in dbg_out:
                P.op("sp", lambda h: h.dma_start(out=dst_ap, in_=src_ap), reads=rkeys, writes=[("dbg", name, id(dst_ap))], dma=True)
                out_keys.append(("dbg", name, id(dst_ap)))

        def do_layer(l):
            base = LBASE(l)
            last = (l == n_layers - 1)
            with ExitStack() as gsc:
                ggrow = sb(gsc, "ggrow", [1, 2048], F32)
                P.op("sp", lambda h: h.dma_start(out=ggrow[:], in_=modscr[l:l + 1].rearrange("o a b -> o (a b)")),
                     reads=[("modscr", l)], writes=["ggrow"], dma=True)
                for j in range(4):
                    P.op("pe", (lambda j=j: lambda h: h.matmul(pb[j % 2][:], lhsT=onesf[:], rhs=ggrow[:, j * 512:(j + 1) * 512], start=True, stop=True))(),
                         reads=["onesf", "ggrow"], writes=[("pb", j % 2)])
                    P.op("dve", (lambda j=j: lambda h: h.tensor_copy(out=gg[:, j * 512:(j + 1) * 512], in_=pb[j % 2][:]))(),
                         reads=[("pb", j % 2)], writes=["gg"])
            P.barrier()
            if stop == "A0":
                return "stop"
            with ExitStack() as mix:
                ogT = sb(mix, "ogT", [128, 4, T], BF16)
                yfT = sb(mix, "yfT", [128, 4, T], BF16)
                with ExitStack() as inp:
                    hT = sb(inp, "hT", [128, 8, T], BF16)
                    with ExitStack() as pa:
                        norm_to_T(pa, l, 0, lambda k, g: (hT[:, k, g * 256:(g + 1) * 256], ("hT", g // 2, k)), "A", first=(l == 0))
                    P.barrier()
                    if stop == "A":
                        return "stop"
                    if "hT" in dbg_out and l == 0:
                        with ExitStack() as dd:
                            tmpd = sb(dd, "tmpd", [128, T], F32)
                            for k in range(8):
                                P.op("dve", (lambda k=k: lambda h: h.tensor_copy(out=tmpd[:], in_=hT[:, k, :]))(), reads=[("hT", i, kk) for i in range(4) for kk in range(8)], writes=["tmpd"])
                                dbg_store("hT", tmpd[:], dbg_out["hT"][k * 128:(k + 1) * 128, :], ["tmpd"])
                            P.barrier()
                    rot = [0]

                    def nextbank():
                        b = rot[0] % 3
                        rot[0] += 1
                        return b
                    winr = [("win", k) for k in range(8)]
                    with ExitStack() as pbx:
                        win = sb(pbx, "winF", [128, 8, 512], BF16)
                        zfT = sb(pbx, "zfT", [128, 4, T], BF16)
                        ZCS = sb(pbx, "ZCS", [128, NTC, 1024], BF16)
                        WOFF = 0
                        for k in range(8):
                            P.op("pool", (lambda k=k, win=win: lambda h: h.dma_start(out=win[:, k, :], in_=w_in[l, k * 128:(k + 1) * 128, 0:512]))(),
                                 writes=[("win", k)], dma=True)

                        def fm_mm(bank, col0, m, tb, win=win, WOFF=WOFF):
                            def f(h):
                                ins = None
                                for k in range(8):
                                    ins = h.matmul(pb[bank][0:m, :], lhsT=win[:, k, col0 - WOFF:col0 - WOFF + m], rhs=hT[:, k, tb * 512:(tb + 1) * 512],
                                                   start=(k == 0), stop=(k == 7))
                                return ins
                            return f
                        for tb in range(4):
                            c0 = tb * 512
                            hk = [("hT", tb, k) for k in range(8)]
                            for fc in range(4):
                                b = nextbank()
                                P.op("pe", fm_mm(b, fc * 128, 128, tb), reads=winr + hk, writes=[("pb", b)])
                                P.op("dve", (lambda b=b, fc=fc, c0=c0: lambda h: h.tensor_copy(out=zfT[:, fc, c0:c0 + 512], in_=pb[b][:]))(),
                                     reads=[("pb", b)], writes=[("zfT", tb)])
                        P.barrier()
                        for tc in range(NTC):
                            t0 = tc * 128
                            b0 = (tc % 2) * 2

                            def zmm(h, t0=t0, b0=b0):
                                ins = None
                                for hd in range(4):
                                    ins = h.matmul(pb[b0 + hd // 2][:, (hd % 2) * 256:(hd % 2 + 1) * 256], lhsT=zfT[:, hd, t0:t0 + 128], rhs=cc[:],
                                                   start=True, stop=True)
                                return ins
                            P.op("pe", zmm, reads=[("zfT", tc // 4), "cc"], writes=[("pb", b0), ("pb", b0 + 1)])
                            P.op("act", (lambda tc=tc, b0=b0: lambda h: h.activation(out=ZCS[:, tc, 0:512], in_=pb[b0][:], func=AF.Copy))(),
                                 reads=[("pb", b0)], writes=[("ZCS", tc)])
                            P.op("dve", (lambda tc=tc, b0=b0: lambda h: h.tensor_copy(out=ZCS[:, tc, 512:1024], in_=pb[b0 + 1][:]))(),
                                 reads=[("pb", b0 + 1)], writes=[("ZCS", tc)])
                        cring = [sb(pbx, "cring%d" % i, [128, 1024], BF16) for i in range(4)]
                        ci = 0
                        for tpb in range(4):
                            for tc in range(NTC):
                                r = ci % 4
                                ci += 1
                                load("sp", cring[r][:], cst_in[tpb * 16 + tc], ("cring", r))

                                def ymm(h, tc=tc, r=r):
                                    ins = None
                                    for hd in range(4):
                                        h.matmul(pb[4 + hd][:], lhsT=ZCS[:, tc, hd * 256:hd * 256 + 128], rhs=cring[r][:, 0:512],
                                                 start=(tc == 0), stop=False)
                                        ins = h.matmul(pb[4 + hd][:], lhsT=ZCS[:, tc, hd * 256 + 128:hd * 256 + 256], rhs=cring[r][:, 512:1024],
                                                       start=False, stop=(tc == NTC - 1))
                                    return ins
                                P.op("pe", ymm, reads=[("ZCS", tc), ("cring", r)], writes=[("pb", 4 + hd) for hd in range(4)])
                                if MOD_IL and l + 1 < n_layers and (ci % 2 == 0):
                                    mcb = ci // 2 - 1
                                    if mcb == 0:
                                        mod_dma(l + 1, 0)
                                        mod_dma(l + 1, 1)
                                    if mcb < 24:
                                        if mcb + 2 < 24:
                                            mod_dma(l + 1, mcb + 2)
                                        mod_mm(l + 1, mcb)
                                    if mcb == 24:
                                        mod_finish(l + 1)
                            for hd in range(4):
                                eng = "act" if hd % 2 == 0 else "dve"
                                if eng == "act":
                                    fn = (lambda hd=hd, tpb=tpb: lambda h: h.activation(out=yfT[:, hd, tpb * 512:(tpb + 1) * 512], in_=pb[4 + hd][:], func=AF.Copy))()
                                else:
                                    fn = (lambda hd=hd, tpb=tpb: lambda h: h.tensor_copy(out=yfT[:, hd, tpb * 512:(tpb + 1) * 512], in_=pb[4 + hd][:]))()
                                P.op(eng, fn, reads=[("pb", 4 + hd)], writes=[("yfT", tpb)])
                    P.barrier()
                    if stop == "C":
                        return "stop"
                    QK = sb(inp, "QK", [128, 8, T], BF16)
                    V = sb(inp, "V", [128, NTC, 512], BF16)
                    sgT = sb(inp, "sgT", [128, 4, T], BF16)
                    dec = sb(inp, "dec", [128, 4, NTC], F32)
                    with ExitStack() as pbx:
                        win = sb(pbx, "winG", [128, 8, IN_COLS - 512], BF16)
                        WOFF = 512
                        wg = sb(pbx, "wg", [33, 512], BF16)
                        zab = sb(pbx, "zab", [33, 1024], BF16)
                        etmp = [sb(pbx, "etmp%d" % i, [128, 512], F32) for i in range(2)]
                        lhi = sb(pbx, "lhi", [128, 512], BF16)
                        llo = sb(pbx, "llo", [128, 512], BF16)
                        Ep = sb(pbx, "Ep", [128, 4, 512], F32)
                        Em = sb(pbx, "Em", [128, 4, 512], F32)
                        for k in range(8):
                            P.op("pool", (lambda k=k, win=win: lambda h: h.dma_start(out=win[:, k, :], in_=w_in[l, k * 128:(k + 1) * 128, 512:IN_COLS]))(),
                                 writes=[("win", k)], dma=True)
                        P.op("pool", lambda h: h.dma_start(out=wg[:], in_=wgate[l]), writes=["wg"], dma=True)
                        P.op("dve", lambda h: h.memset(zab[32:33, :], 1.0), writes=["zab1"])

                        def fm_mm(bank, col0, m, tb, win=win, WOFF=WOFF):
                            def f(h):
                                ins = None
                                for k in range(8):
                                    ins = h.matmul(pb[bank][0:m, :], lhsT=win[:, k, col0 - WOFF:col0 - WOFF + m], rhs=hT[:, k, tb * 512:(tb + 1) * 512],
                                                   start=(k == 0), stop=(k == 7))
                                return ins
                            return f
                        for tb in range(4):
                            c0 = tb * 512
                            hk = [("hT", tb, k) for k in range(8)]
                            zt = zab[:, (tb % 2) * 512:(tb % 2 + 1) * 512]
                            zkey = ("zab", tb % 2)
                            P.op("pe", fm_mm(7, 1536, 32, tb), reads=winr + hk, writes=[("pb", 7)])
                            P.op("act", (lambda zt=zt: lambda h: h.activation(out=zt[0:32, :], in_=pb[7][0:32, :], func=AF.Copy))(),
                                 reads=[("pb", 7)], writes=[zkey])

                            def gate_a(j, tb=tb, zt=zt, zkey=zkey):
                                xb = 3 + (j % 2)
                                et = etmp[j % 2]
                                P.op("pe", lambda h: h.matmul(pb[xb][:], lhsT=zt[0:33, j * 128:(j + 1) * 128], rhs=wg[0:33, :], start=True, stop=True),
                                     reads=[zkey, "zab1", "wg"], writes=[("pb", xb)])
                                P.op("act", lambda h: h.activation(out=et[:], in_=pb[xb][:], func=AF.Exp, scale=-1.0), reads=[("pb", xb)], writes=[("etmp", j % 2)])
                                P.op("act", lambda h: h.activation(out=et[:], in_=et[:], func=AF.Ln, bias=1.0), reads=[("etmp", j % 2)], writes=[("etmp", j % 2)])

                            def gate_a2(j):
                                et = etmp[j % 2]
                                P.op("dve", lambda h: h.tensor_copy(out=lhi[:], in_=et[:]), reads=[("etmp", j % 2)], writes=["lhi"])
                                P.op("dve", lambda h: h.tensor_tensor(out=llo[:], in0=et[:], in1=lhi[:], op=ALU.subtract), reads=[("etmp", j % 2), "lhi"], writes=["llo"])

                            def gate_b(j):
                                def cums(h):
                                    ins = None
                                    for q in range(4):
                                        h.matmul(pb[5][:, q * 128:(q + 1) * 128], lhsT=lhi[:, q * 128:(q + 1) * 128],
                                                 rhs=uub[:, (q // 2) * 128:(q // 2 + 1) * 128], start=True, stop=False)
                                        ins = h.matmul(pb[5][:, q * 128:(q + 1) * 128], lhsT=llo[:, q * 128:(q + 1) * 128],
                                                       rhs=uub[:, (q // 2) * 128:(q // 2 + 1) * 128], start=False, stop=True)
                                    return ins
                                P.op("pe", cums, reads=["lhi", "llo", "uub"], writes=[("pb", 5)])
                                pv4 = pb[5][:].rearrange("p (q c) -> p q c", c=128)
                                P.op("act", lambda h: h.activation(out=Ep[:, :, j * 128:(j + 1) * 128], in_=pv4, func=AF.Exp),
                                     reads=[("pb", 5)], writes=[("Ep", j)])
                                P.op("act", lambda h: h.activation(out=Em[:, :, j * 128:(j + 1) * 128], in_=pv4, func=AF.Exp, scale=-1.0),
                                     reads=[("pb", 5)], writes=[("Em", j)])

                            def v_step(j, tb=tb, win=win, WOFF=WOFF):
                                tc = tb * 4 + j
                                t0 = tc * 128
                                vb = 6 + (j % 2)

                                def vmm(h):
                                    ins = None
                                    for k in range(8):
                                        ins = h.matmul(pb[vb][:], lhsT=hT[:, k, t0:t0 + 128], rhs=win[:, k, 1024 - WOFF:1536 - WOFF], start=(k == 0), stop=(k == 7))
                                    return ins
                                P.op("pe", vmm, reads=winr + hk, writes=[("pb", vb)])
                                P.op("dve", lambda h: h.tensor_copy(out=V[:, tc, :], in_=pb[vb][:]), reads=[("pb", vb)], writes=[("V", tc)])
                            gate_a(0)
                            gate_a(1)
                            gate_a2(0)
                            v_step(0)
                            gate_b(0)
                            gate_a(2)
                            gate_a2(1)
                            v_step(1)
                            gate_b(1)
                            gate_a(3)
                            gate_a2(2)
                            v_step(2)
                            gate_b(2)
                            gate_a2(3)
                            v_step(3)
                            gate_b(3)
                            Epk = [("Ep", j) for j in range(4)]
                            Emk = [("Em", j) for j in range(4)]
                            Epv = Ep[:].rearrange("p q (j c) -> p q j c", c=128)
                            P.op("dve", (lambda tb=tb, Epv=Epv: lambda h: h.tensor_copy(out=dec[:, 0:2, tb * 4:(tb + 1) * 4], in_=Epv[:, 0:2, :, 127]))(),
                                 reads=Epk, writes=[("dec", tb)])
                            P.op("dve", (lambda tb=tb, Epv=Epv: lambda h: h.tensor_copy(out=dec[:, 2:4, tb * 4:(tb + 1) * 4], in_=Epv[:, 2:4, :, 0]))(),
                                 reads=Epk, writes=[("dec", tb)])
                            for p in range(2):
                                b = nextbank()
                                P.op("pe", fm_mm(b, 512 + p * 128, 128, tb), reads=winr + hk, writes=[("pb", b)])
                                for d in range(2):
                                    P.op("dve", (lambda b=b, p=p, d=d, c0=c0: lambda h: h.scalar_tensor_tensor(
                                        out=QK[:, d * 2 + p, c0:c0 + 512], in0=pb[b][:], scalar=0.125, in1=Ep[:, d * 2 + p, :],
                                        op0=ALU.mult, op1=ALU.mult))(),
                                        reads=[("pb", b)] + Epk, writes=[("QK", d * 2 + p, tb)])
                            for p in range(2):
                                b = nextbank()
                                P.op("pe", fm_mm(b, 768 + p * 128, 128, tb), reads=winr + hk, writes=[("pb", b)])
                                for d in range(2):
                                    P.op("dve", (lambda b=b, p=p, d=d, c0=c0: lambda h: h.tensor_tensor(
                                        out=QK[:, 4 + d * 2 + p, c0:c0 + 512], in0=pb[b][:], in1=Em[:, d * 2 + p, :], op=ALU.mult))(),
                                        reads=[("pb", b)] + Emk, writes=[("QK", 4 + d * 2 + p, tb)])
                            for fc in range(4):
                                b = nextbank()
                                P.op("pe", fm_mm(b, 1568 + fc * 128, 128, tb), reads=winr + hk, writes=[("pb", b)])
                                P.op("act", (lambda b=b, fc=fc, c0=c0: lambda h: h.activation(out=sgT[:, fc, c0:c0 + 512], in_=pb[b][:], func=AF.Silu))(),
                                     reads=[("pb", b)], writes=[("sgT", tb)])
                    P.barrier()
                    if stop == "B2":
                        return "stop"
                    with ExitStack() as gl:
                        S = [[sb(gl, "S%d_%d" % (q, i), [128, 256], F32) for i in range(2)] for q in range(4)]
                        Sst = hT[:].rearrange("p k t -> p (k t)").rearrange("p (q c v) -> p q c v", q=4, c=NTC)
                        kTm = [sb(gl, "kTm%d" % i, [128, 512], BF16) for i in range(2)]
                        kvs = [sb(gl, "kvs%d" % i, [128, 4, 256], F32) for i in range(2)]
                        deckp = sb(gl, "deckp", [128, 4, NTC], F32)
                        deck = [("dec", i) for i in range(4)]
                        P.op("dve", lambda h: h.tensor_copy(out=deckp[:], in_=dec[:]), reads=deck, writes=["deckp"])
                        dkv = deckp[:].rearrange("p q (a b) -> p q a b", b=2)
                        P.op("dve", lambda h: h.tensor_scalar(out=dkv[:, 0:2, 1:8, 0], in0=dkv[:, 0:2, 1:8, 0], scalar1=kp[:, 0:1], scalar2=None, op0=ALU.mult),
                             reads=["deckp", "kp"], writes=["deckp"])
                        P.op("dve", lambda h: h.tensor_scalar(out=dkv[:, 2:4, 0:7, 1], in0=dkv[:, 2:4, 0:7, 1], scalar1=kp[:, 0:1], scalar2=None, op0=ALU.mult),
                             reads=["deckp", "kp"], writes=["deckp"])
                        for q in range(4):
                            P.op("dve", (lambda q=q: lambda h: h.memset(S[q][0][:], 0.0))(), writes=[("S", q, 0)])
                            d_, p_ = q // 2, q % 2
                            for e in range(2):
                                P.op("sp", (lambda q=q, d_=d_, p_=p_, e=e: lambda h: h.dma_start(
                                    out=S[q][0][e * 64:(e + 1) * 64, e * 128:(e + 1) * 128], in_=s0_in[l, d_, 2 * p_ + e]))(),
                                    reads=[], writes=[("S", q, 0)], dma=True)

                        def chunks_of(step):
                            return [step, step, NTC - 1 - step, NTC - 1 - step]

                        def d1_prep(step):
                            chunk_of = chunks_of(step)
                            pT = pb[step % 2]

                            def ktr(h):
                                ins = None
                                for q in range(4):
                                    c = chunk_of[q]
                                    ins = h.transpose(pT[:].bitcast(BF16)[:, q * 128:(q + 1) * 128], QK[:, 4 + q, c * 128:(c + 1) * 128], identb[:])
                                return ins
                            P.op("pe", ktr, reads=[("QK", 4 + q, chunk_of[q] // 4) for q in range(4)] + ["identb"], writes=[("pb", step % 2)])
                            km = kTm[step % 2]
                            P.op("act", lambda h: h.activation(out=km[:], in_=pT[:].bitcast(BF16)[:, 0:512], func=AF.Copy),
                                 reads=[("pb", step % 2)], writes=[("kTm", step % 2)])
                            kb0 = 2 + (step % 2) * 2

                            def kvmm(h):
                                ins = None
                                for q in range(4):
                                    c = chunk_of[q]
                                    p_ = q % 2
                                    ins = h.matmul(pb[kb0 + q // 2][:, (q % 2) * 256:(q % 2 + 1) * 256], lhsT=km[:, q * 128:(q + 1) * 128],
                                                   rhs=V[:, c, p_ * 256:(p_ + 1) * 256], start=True, stop=True)
                                return ins
                            P.op("pe", kvmm, reads=[("kTm", step % 2)] + [("V", chunk_of[q]) for q in range(4)], writes=[("pb", kb0), ("pb", kb0 + 1)])
                            kv = kvs[step % 2]
                            for q in range(4):
                                c = chunk_of[q]
                                P.op("act", (lambda q=q, c=c: lambda h: h.activation(
                                    out=kv[:, q, :], in_=pb[kb0 + q // 2][:, (q % 2) * 256:(q % 2 + 1) * 256], func=AF.Copy, scale=dec[:, q, c:c + 1]))(),
                                    reads=[("pb", kb0 + q // 2), ("dec", c // 4)], writes=[("kvs", step % 2, q)])

                        def d1_main(step):
                            chunk_of = chunks_of(step)
                            cur, nxt = step % 2, (step + 1) % 2
                            kv = kvs[step % 2]
                            for q in range(4):
                                c = chunk_of[q]
                                d_ = q // 2
                                seg_start = (c % 2 == 0 and c > 0) if d_ == 0 else (c % 2 == 1 and c < NTC - 1)
                                src = S[q][cur][:]
                                dst = Sst[:, q, c, :]
                                if d_ == 0:
                                    sc_ = kp[:, 0:1] if seg_start else 1.0
                                    fn = (lambda src=src, dst=dst, sc_=sc_: lambda h: h.activation(out=dst, in_=src, func=AF.Copy, scale=sc_))()
                                    P.op("act", fn, reads=[("S", q, cur), "kp"], writes=[("Sst", q, c)])
                                else:
                                    if seg_start:
                                        fn = (lambda src=src, dst=dst: lambda h: h.tensor_scalar(out=dst, in0=src, scalar1=kp[:, 0:1], scalar2=None, op0=ALU.mult))()
                                    else:
                                        fn = (lambda src=src, dst=dst: lambda h: h.tensor_copy(out=dst, in_=src))()
                                    P.op("dve", fn, reads=[("S", q, cur), "kp"], writes=[("Sst", q, c)])
                            for q in range(4):
                                c = chunk_of[q]
                                P.op("dve", (lambda q=q, c=c: lambda h: h.scalar_tensor_tensor(
                                    out=S[q][nxt][:], in0=S[q][cur][:], scalar=deckp[:, q, c:c + 1], in1=kv[:, q, :], op0=ALU.mult, op1=ALU.add))(),
                                    reads=[("S", q, cur), "deckp", ("kvs", step % 2, q)], writes=[("S", q, nxt)])
                                d_, p_ = q // 2, q % 2
                                seg_end = (c % 2 == 1) if d_ == 0 else (c % 2 == 0)
                                if seg_end:
                                    seg = c // 2
                                    for e in range(2):
                                        key = ("ns", l, seg, d_, 2 * p_ + e)
                                        P.op("sp", (lambda q=q, seg=seg, d_=d_, p_=p_, e=e: lambda h: h.dma_start(
                                            out=ns_out[l, seg, d_, 2 * p_ + e], in_=S[q][nxt][e * 64:(e + 1) * 64, e * 128:(e + 1) * 128]))(),
                                            reads=[("S", q, nxt)], writes=[key], dma=True)
                                        out_keys.append(key)
                        d1_prep(0)
                        for step in range(NTC):
                            if step + 1 < NTC:
                                d1_prep(step + 1)
                            d1_main(step)
                        P.barrier()
                        if stop == "D1":
                            return "stop"
                        attm = [sb(gl, "attm%d" % i, [128, 1024], BF16) for i in range(3)]
                        qzt = [sb(gl, "qzt%d" % i, [128, 4, 2, 128], BF16) for i in range(3)]
                        for i in range(3):
                            P.op("pool", (lambda i=i: lambda h: h.memset(qzt[i][:], 0.0))(), writes=[("qz", i)])
                        osq = [sb(gl, "osq%d" % i, [128, 512], BF16) for i in range(2)]
                        lno = [sb(gl, "lno%d" % i, [128, 512], F32) for i in range(2)]

                        def d2_a1(c):
                            t0 = c * 128
                            qz = qzt[c % 3]
                            for e in range(2):
                                P.op("act", (lambda e=e: lambda h: h.activation(
                                    out=qz[e * 64:(e + 1) * 64, :, e, :], in_=QK[e * 64:(e + 1) * 64, 0:4, t0:t0 + 128], func=AF.Copy))(),
                                    reads=[("QK", qq, c // 4) for qq in range(4)], writes=[("qz", c % 3)])

                        def d2_a2(c):
                            t0 = c * 128
                            a0 = (c % 2) * 2
                            am = attm[c % 3]
                            qz = qzt[c % 3]

                            def attmm(h):
                                ins = None
                                for d_ in range(2):
                                    for hd in range(4):
                                        e, p_ = hd % 2, hd // 2
                                        ins = h.matmul(pb[a0 + d_][:, hd * 128:(hd + 1) * 128],
                                                       lhsT=QK[:, 4 + d_ * 2 + p_, t0:t0 + 128],
                                                       rhs=qz[:, d_ * 2 + p_, e, :], start=True, stop=True)
                                return ins
                            P.op("pe", attmm, reads=[("QK", i, c // 4) for i in range(4, 8)] + [("qz", c % 3)], writes=[("pb", a0), ("pb", a0 + 1)])
                            for d_ in range(2):
                                P.op("dve", (lambda d_=d_: lambda h: h.tensor_tensor(
                                    out=am[:, d_ * 512:(d_ + 1) * 512], in0=pb[a0 + d_][:], in1=mk[:, d_ * 512:(d_ + 1) * 512], op=ALU.mult))(),
                                    reads=[("pb", a0 + d_), "mk"], writes=[("attm", c % 3, d_)])

                        def d2_b(c):
                            t0 = c * 128
                            po = 4 + (c % 2)
                            am = attm[c % 3]
                            qz = qzt[c % 3]

                            def omm(h):
                                ins = None
                                for hd in range(4):
                                    e, p_ = hd % 2, hd // 2
                                    o_ap = pb[po][:, hd * 128:(hd + 1) * 128]
                                    h.matmul(o_ap, lhsT=V[:, c, hd * 128:(hd + 1) * 128], rhs=am[:, hd * 128:(hd + 1) * 128], start=True, stop=False)
                                    h.matmul(o_ap, lhsT=V[:, c, hd * 128:(hd + 1) * 128], rhs=am[:, 512 + hd * 128:512 + (hd + 1) * 128], start=False, stop=False)
                                    h.matmul(o_ap, lhsT=Sst[:, 0 + p_, c, e * 128:(e + 1) * 128], rhs=qz[:, 0 + p_, e, :], start=False, stop=False)
                                    ins = h.matmul(o_ap, lhsT=Sst[:, 2 + p_, c, e * 128:(e + 1) * 128], rhs=qz[:, 2 + p_, e, :], start=False, stop=True)
                                return ins
                            P.op("pe", omm, reads=[("V", c), ("attm", c % 3, 0), ("attm", c % 3, 1)] + [("Sst", q, c) for q in range(4)] + [("qz", c % 3)],
                                 writes=[("pb", po)])
                            oq = osq[c % 2]
                            P.op("act", lambda h: h.activation(out=oq[:], in_=pb[po][:], func=AF.Square), reads=[("pb", po)], writes=[("osq", c % 2)])

                        def d2_c1(c):
                            pss = 6
                            oq = osq[c % 2]
                            ln_ = lno[c % 2]
                            P.op("pe", lambda h: h.matmul(pb[pss][:], lhsT=onesb[:], rhs=oq[:], start=True, stop=True),
                                 reads=[("osq", c % 2), "onesb"], writes=[("pb", pss)])
                            P.op("act", lambda h: h.activation(out=ln_[:], in_=pb[pss][:], func=AF.Ln, scale=1.0 / 128, bias=EPS),
                                 reads=[("pb", pss)], writes=[("lno", c % 2)])
                            P.op("act", lambda h: h.activation(out=ln_[:], in_=ln_[:], func=AF.Exp, scale=-0.5), reads=[("lno", c % 2)], writes=[("lno", c % 2)])

                        def d2_c2(c):
                            t0 = c * 128
                            po = 4 + (c % 2)
                            ln_ = lno[c % 2]
                            P.op("dve", lambda h: h.scalar_tensor_tensor(out=ln_[:], in0=pb[po][:], scalar=VTT[:, base + 256:base + 257],
                                                                         in1=ln_[:], op0=ALU.mult, op1=ALU.mult),
                                 reads=[("pb", po), ("lno", c % 2), "VTT"], writes=[("lno", c % 2)])
                            og1v = ln_[:].rearrange("p (q c) -> p q c", c=128)
                            P.op("dve", lambda h: h.tensor_tensor(out=ogT[:, :, t0:t0 + 128], in0=og1v, in1=sgT[:, :, t0:t0 + 128], op=ALU.mult),
                                 reads=[("lno", c % 2), ("sgT", c // 4)], writes=[("ogT", c)])
                        for it in range(NTC + 3):
                            if it < NTC:
                                d2_a1(it)
                            if 0 <= it - 2 < NTC:
                                d2_b(it - 2)
                            if 0 <= it - 3 < NTC:
                                d2_c1(it - 3)
                            if it < NTC:
                                d2_a2(it)
                            if 0 <= it - 3 < NTC:
                                d2_c2(it - 3)
                P.barrier()
                if l == 0:
                    with ExitStack() as dd:
                        tmpd = sb(dd, "tmpd2", [128, T], F32)
                        for nm, src in (("yfT", yfT), ("ogT", ogT)):
                            if nm in dbg_out:
                                for k in range(4):
                                    P.op("dve", (lambda k=k, src=src: lambda h: h.tensor_copy(out=tmpd[:], in_=src[:, k, :]))(), reads=list(P.keys), writes=["tmpd2"])
                                    dbg_store(nm, tmpd[:], dbg_out[nm][k * 128:(k + 1) * 128, :], ["tmpd2"])
                        P.barrier()
                if stop == "D2":
                    return "stop"
                with ExitStack() as pe_:
                    wout = sb(pe_, "wout", [128, 8, D], BF16)
                    xrE = [sb(pe_, "xrE%d" % i, [128, D], F32) for i in range(4)]
                    tmpE = [sb(pe_, "tmpE%d" % i, [128, D], F32) for i in range(4)]
                    junkE = sb(pe_, "junkE", [128, 512], BF16)
                    ss2 = sb(pe_, "ss2E", [128, 2 * NTC], F32)
                    lnv2 = sb(pe_, "lnv2E", [128, NTC], F32)
                    rstd2 = sb(pe_, "rstd2E", [128, NTC], F32)
                    for k in range(8):
                        P.op("pool", (lambda k=k: lambda h: h.dma_start(out=wout[:, k, :], in_=w_out[l, k * 128:(k + 1) * 128, :]))(),
                             writes=[("wout", k)], dma=True)
                    for tc in range(NTC):
                        t0 = tc * 128
                        b0 = (tc % 4) * 2

                        def outmm(h, t0=t0, b0=b0):
                            ins = None
                            for cb in range(2):
                                for k in range(8):
                                    src = yfT[:, k, t0:t0 + 128] if k < 4 else ogT[:, k - 4, t0:t0 + 128]
                                    ins = h.matmul(pb[b0 + cb][:], lhsT=src, rhs=wout[:, k, cb * 512:(cb + 1) * 512], start=(k == 0), stop=(k == 7))
                            return ins
                        P.op("pe", outmm, reads=[("wout", k) for k in range(8)] + [("yfT", tc // 4), ("ogT", tc)], writes=[("pb", b0), ("pb", b0 + 1)])
                        post_norm_1(tc, [pb[b0], pb[b0 + 1]], [("pb", b0), ("pb", b0 + 1)], junkE, "junkE", ss2, lnv2, "E", xrE[tc % 4], ("xrE", tc % 4))
                        if tc > 0:
                            pc = tc - 1
                            pb0 = (pc % 4) * 2
                            post_norm_2(l, 0, pc, [pb[pb0], pb[pb0 + 1]], [("pb", pb0), ("pb", pb0 + 1)], xrE[pc % 4], ("xrE", pc % 4),
                                        tmpE[pc % 4], ("tmpE", pc % 4), lnv2, rstd2, "E", False)
                    pc = NTC - 1
                    pb0 = (pc % 4) * 2
                    post_norm_2(l, 0, pc, [pb[pb0], pb[pb0 + 1]], [("pb", pb0), ("pb", pb0 + 1)], xrE[pc % 4], ("xrE", pc % 4),
                                tmpE[pc % 4], ("tmpE", pc % 4), lnv2, rstd2, "E", False)
            P.barrier()
            if l == 0 and "xmix" in dbg_out:
                with ExitStack() as dd:
                    tmpd = sb(dd, "tmpd3", [128, D], F32)
                    for tc in range(NTC):
                        P.op("sp", (lambda tc=tc: lambda h: h.dma_start(out=tmpd[:], in_=xs[tc * 128:(tc + 1) * 128, :]))(), reads=[("xs", tc)], writes=["tmpd3"], dma=True)
                        dbg_store("xmix", tmpd[:], dbg_out["xmix"][tc * 128:(tc + 1) * 128, :], ["tmpd3"])
                    P.barrier()
            if stop == "E":
                return "stop"
            with ExitStack() as ffn:
                h2x = sb(ffn, "h2x", [128, 8, 32, 66], BF16)
                P.op("dve", lambda h: h.memset(h2x[:], 0.0), writes=[("h2x", g, k) for g in range(8) for k in range(8)] + ["h2xhalo"])
                with ExitStack() as pf:
                    def dstf(k, g):
                        return h2x[:, k, g * 4:(g + 1) * 4, 1:65], ("h2x", g, k)
                    norm_to_T(pf, l, 1, dstf, "F")
                P.barrier()
                h2k = [("h2x", g, k) for g in range(8) for k in range(8)]
                P.op("dve", lambda h: h.tensor_tensor(out=h2x[:, :, 1:32, 0], in0=h2x[:, :, 0:31, 64],
                                                      in1=hm[:, 1:32].unsqueeze(1).to_broadcast([128, 8, 31]), op=ALU.mult),
                     reads=h2k + ["hm"], writes=["h2xhalo"])
                P.op("dve", lambda h: h.tensor_tensor(out=h2x[:, :, 0:31, 65], in0=h2x[:, :, 1:32, 1],
                                                      in1=hm[:, 32:63].unsqueeze(1).to_broadcast([128, 8, 31]), op=ALU.mult),
                     reads=h2k + ["hm"], writes=["h2xhalo"])
                h2xf = h2x[:].rearrange("p k s c -> p k (s c)")
                wdn = sb(ffn, "wdn", [128, NPAIR, D], BF16)
                aT = sb(ffn, "aT", [128, NPAIR, 1024], BF16)
                wup = [sb(ffn, "wup%d" % i, [128, 8, 512], BF16) for i in range(3)]
                NB = int(os.environ.get('FFN_NB', '4'))
                t1 = [sb(ffn, "t1_%d" % i, [128, 6, 64], F32) for i in range(NB)]
                g1 = [sb(ffn, "g1_%d" % i, [128, 6, 64], F32) for i in range(NB)]
                xrG = [sb(ffn, "xrG%d" % i, [128, D], F32) for i in range(2)]
                tmpG = [sb(ffn, "tmpG%d" % i, [128, D], F32) for i in range(2)]
                junkG = sb(ffn, "junkG", [128, 512], BF16)
                ss2g = sb(ffn, "ss2G", [128, 2 * NTC], F32)
                lnv2g = sb(ffn, "lnv2G", [128, NTC], F32)
                rstd2g = sb(ffn, "rstd2G", [128, NTC], F32)
                wupsrc = w_up[l].rearrange("(k p) c -> p k c", p=128)
                WU = [(hf, u) for hf in range(2) for u in range(NPAIR // 2)]

                def wup_dma(wi, part=None):
                    hf, u = WU[wi]
                    wr = wup[wi % 3]
                    wkey = ("wup", wi % 3)
                    for pt in (range(4) if part is None else [part]):
                        half, kq = pt // 2, pt % 2
                        c_src = (DFF if half else 0) + u * 256
                        P.op("pool", (lambda half=half, kq=kq, c_src=c_src: lambda h: h.dma_start(
                            out=wr[:, kq * 4:(kq + 1) * 4, half * 256:(half + 1) * 256],
                            in_=wupsrc[:, kq * 4:(kq + 1) * 4, c_src:c_src + 256]))(), writes=[wkey], dma=True)

                def wdn_dma():
                    for i in range(NPAIR):
                        P.op("pool", (lambda i=i: lambda h: h.dma_start(out=wdn[:, i, :], in_=w_down[l, i * 128:(i + 1) * 128, :]))(),
                             writes=[("wdn", i)], dma=True)
                wup_dma(0)
                wup_dma(1)
                BLKS = [(0, 6), (6, 5), (11, 5)] if os.environ.get('FFN_BLK', '655') == '655' else [(0, 4), (4, 4), (8, 4), (12, 4)]
                cnt2 = 0
                for hf in range(2):
                    for u in range(NPAIR // 2):
                        wi = hf * (NPAIR // 2) + u
                        if wi == 1:
                            wdn_dma()
                        wr = wup[wi % 3]
                        wkey = ("wup", wi % 3)
                        uidx = 0
                        for (sl0, nsg) in BLKS:
                            sg0 = hf * 16 + sl0
                            ncol = nsg * 66
                            for ii in range(2):
                                i = 2 * u + ii
                                if wi + 2 < len(WU) and uidx < 4:
                                    wup_dma(wi + 2, uidx)
                                uidx += 1
                                r2 = cnt2 % NB
                                bv = r2 * 2
                                bg = bv + 1
                                cnt2 += 1

                                def upmm(h, wr=wr, ii=ii, sg0=sg0, ncol=ncol, bv=bv, bg=bg):
                                    ins = None
                                    rhsv = [h2xf[:, k, sg0 * 66:sg0 * 66 + ncol] for k in range(8)]
                                    for k in range(8):
                                        h.matmul(pb[bv][:, 0:ncol], lhsT=wr[:, k, ii * 128:(ii + 1) * 128], rhs=rhsv[k], start=(k == 0), stop=(k == 7))
                                    for k in range(8):
                                        ins = h.matmul(pb[bg][:, 0:ncol], lhsT=wr[:, k, 256 + ii * 128:256 + (ii + 1) * 128], rhs=rhsv[k], start=(k == 0), stop=(k == 7))
                                    return ins
                                P.op("pe", upmm, reads=[wkey, "h2xhalo"] + h2k, writes=[("pb", bv), ("pb", bg)])
                                chains = []
                                for (bank, dstt, dkey, coff) in ((bv, t1[r2], ("t1", r2), i), (bg, g1[r2], ("g1", r2), NPAIR + i)):
                                    pvw = pb[bank][:, 0:ncol].rearrange("p (s c) -> p s c", c=66)
                                    dv = dstt[:, 0:nsg, :]
                                    cws = [VTT[:, base + j * 44 + coff:base + j * 44 + coff + 1] for j in range(3)]
                                    cbb = VTT[:, base + 132 + coff:base + 132 + coff + 1]
                                    chains.append((bank, pvw, dv, dkey, cws, cbb))
                                for (bank, pvw, dv, dkey, cws, cbb) in chains:
                                    P.op("act", (lambda pvw=pvw, dv=dv, cws=cws, cbb=cbb: lambda h: h.activation(
                                        out=dv, in_=pvw[:, :, 1:65], func=AF.Identity, scale=cws[1], bias=cbb))(),
                                        reads=[("pb", bank), "VTT"], writes=[dkey])
                                for tap, lo in ((0, 0), (2, 2)):
                                    for (bank, pvw, dv, dkey, cws, cbb) in chains:
                                        P.op("dve", (lambda pvw=pvw, dv=dv, cws=cws, tap=tap, lo=lo: lambda h: h.scalar_tensor_tensor(
                                            out=dv, in0=pvw[:, :, lo:lo + 64], scalar=cws[tap], in1=dv, op0=ALU.mult, op1=ALU.add))(),
                                            reads=[("pb", bank), "VTT", dkey], writes=[dkey])
                                sv = chains[1][2]
                                P.op("act", (lambda sv=sv: lambda h: h.activation(out=sv, in_=sv, func=AF.Silu))(),
                                     reads=[("g1", r2)], writes=[("g1", r2)])
                                a_dst = aT[:, i, sl0 * 64:(sl0 + nsg) * 64].rearrange("p (s c) -> p s c", c=64)
                                P.op("pool", (lambda sv=sv, tv=chains[0][2], a_dst=a_dst: lambda h: h.tensor_tensor(out=a_dst, in0=sv, in1=tv, op=ALU.mult))(),
                                     reads=[("g1", r2), ("t1", r2)], writes=[("aT", i)])
                    P.barrier()
                    for tcl in range(8):
                        tc = hf * 8 + tcl
                        b0 = (tcl % 4) * 2

                        def dnmm(h, tcl=tcl, b0=b0):
                            ins = None
                            for cb in range(2):
                                for i in range(NPAIR):
                                    ins = h.matmul(pb[b0 + cb][:], lhsT=aT[:, i, tcl * 128:(tcl + 1) * 128], rhs=wdn[:, i, cb * 512:(cb + 1) * 512],
                                                   start=(i == 0), stop=(i == NPAIR - 1))
                            return ins
                        P.op("pe", dnmm, reads=[("wdn", i) for i in range(NPAIR)] + [("aT", i) for i in range(NPAIR)],
                             writes=[("pb", b0), ("pb", b0 + 1)])
                        post_norm_1(tc, [pb[b0], pb[b0 + 1]], [("pb", b0), ("pb", b0 + 1)], junkG, "junkG", ss2g, lnv2g, "G", xrG[tc % 2], ("xrG", tc % 2))
                        if tcl > 0:
                            pc = tc - 1
                            pb0 = ((tcl - 1) % 4) * 2
                            post_norm_2(l, 1, pc, [pb[pb0], pb[pb0 + 1]], [("pb", pb0), ("pb", pb0 + 1)], xrG[pc % 2], ("xrG", pc % 2),
                                        tmpG[pc % 2], ("tmpG", pc % 2), lnv2g, rstd2g, "G", last)
                    pc = hf * 8 + 7
                    pb0 = (7 % 4) * 2
                    post_norm_2(l, 1, pc, [pb[pb0], pb[pb0 + 1]], [("pb", pb0), ("pb", pb0 + 1)], xrG[pc % 2], ("xrG", pc % 2),
                                tmpG[pc % 2], ("tmpG", pc % 2), lnv2g, rstd2g, "G", last)
                    P.barrier()
        for l_ in range(n_layers):
            if stop == "stage0" or do_layer(l_) == "stop":
                break
        P.op("sp", None, reads=list(dict.fromkeys(out_keys)))
        run_prog(nc, P)
    return nc, P


_CACHE = {}


def _consts():
    if "c" in _CACHE:
        return _CACHE["c"]
    bf = ml_dtypes.bfloat16

    def dft(n):
        k = np.arange(n)
        ang = 2.0 * np.pi * ((np.outer(k, k) % n).astype(np.float64)) / n
        return np.cos(ang), np.sin(ang)
    c2048, s2048 = dft(2048)
    c256, s256 = dft(256)
    c128, s128 = dft(128)
    samp_c = c2048 / np.sqrt(2048.0)
    samp_s = -s2048 / np.sqrt(2048.0)
    pr_c = np.zeros((2048, 2048))
    pr_s = np.zeros((2048, 2048))
    for i in range(8):
        pr_c[i * 256:(i + 1) * 256, i * 256:(i + 1) * 256] = c256 / 16.0
        pr_s[i * 256:(i + 1) * 256, i * 256:(i + 1) * 256] = -s256 / 16.0

    def tiles(cm, sm):
        out = np.zeros((64, 128, 1024), np.float32)
        for tpb in range(4):
            for tc in range(16):
                out[tpb * 16 + tc, :, 0:512] = cm[tc * 128:(tc + 1) * 128, tpb * 512:(tpb + 1) * 512]
                out[tpb * 16 + tc, :, 512:1024] = sm[tc * 128:(tc + 1) * 128, tpb * 512:(tpb + 1) * 512]
        return out.astype(bf)
    cst_s = tiles(samp_c, samp_s)
    cst_p = tiles(pr_c, pr_s)
    cc = np.concatenate([c128, s128], axis=1) / np.sqrt(128.0)
    j = np.arange(128)[:, None]
    i = np.arange(128)[None, :]
    mf = (j <= i).astype(np.float32)
    mb = (j >= i).astype(np.float32)
    mk = np.concatenate([np.tile(mf, (1, 4)), np.tile(mb, (1, 4))], axis=1).astype(bf)
    u = np.concatenate([mf, mb], axis=1).astype(np.float32) * (-1.0 / 16.0)
    q = 256
    omega = (1.0 / (10000.0 ** (np.arange(q, dtype=np.float32) / q))).astype(np.float32)
    er = np.arange(32, dtype=np.float32)[:, None] * omega
    ec = np.arange(64, dtype=np.float32)[:, None] * omega
    prr = np.concatenate([np.sin(er), np.cos(er)], axis=-1)
    pcc = np.concatenate([np.sin(ec), np.cos(ec)], axis=-1)
    pe = np.concatenate([np.broadcast_to(prr[:, None], (32, 64, 512)), np.broadcast_to(pcc[None], (32, 64, 512))], axis=-1)
    pe = np.ascontiguousarray(pe.reshape(2048, 1024).astype(np.float32))
    hm_s = np.zeros((128, 64), np.float32)
    hm_p = np.zeros((128, 64), np.float32)
    for s in range(32):
        hm_p[:, s] = 0.0 if s % 4 == 0 else 1.0
        hm_p[:, 32 + s] = 0.0 if s % 4 == 3 else 1.0
    c = dict(cst_s=cst_s, cst_p=cst_p, cc=cc.astype(bf), mk=mk, u=u, pe=pe, pe0=np.zeros_like(pe),
             hm_s=hm_s, hm_p=hm_p, idf=np.eye(128, dtype=np.float32), idb=np.eye(128).astype(bf))
    _CACHE["c"] = c
    return c


def _in_maps(inp):
    c = _consts()
    f = lambda a: np.ascontiguousarray(np.asarray(a, dtype=np.float32))
    x_prompt, x_sample = f(inp["x_prompt"]), f(inp["x_sample"])
    state = f(inp["state_gla"])
    cvs = f(inp["c"])
    cctx = f(inp["c_ctx"])
    wgate = np.zeros((2, 33, 512), np.float32)
    wgate[:, 0:16, 0:256] = f(inp["w_gate_f"])
    wgate[:, 16:32, 256:512] = f(inp["w_gate_b"])
    wgate[:, 32, 0:256] = f(inp["b_gate_f"])
    wgate[:, 32, 256:512] = f(inp["b_gate_b"])

    def vt_for(cvec):
        vt = np.zeros((VT_ROWS, 128), np.float32)
        vt[0:8] = cvec.reshape(8, 128)
        for l in range(2):
            b = LBASE(l)
            vt[b:b + 132] = f(inp["conv_w"])[l].reshape(132, 128)
            vt[b + 132:b + 176] = f(inp["conv_b"])[l].reshape(44, 128)
            vt[b + 176:b + 224] = f(inp["b_ada"])[l].reshape(48, 128)
            vt[b + 224:b + 232] = f(inp["g_pre_mix"])[l].reshape(8, 128)
            vt[b + 232:b + 240] = f(inp["g_post_mix"])[l].reshape(8, 128)
            vt[b + 240:b + 248] = f(inp["g_pre_ffn"])[l].reshape(8, 128)
            vt[b + 248:b + 256] = f(inp["g_post_ffn"])[l].reshape(8, 128)
            vt[b + 256] = f(inp["g_gla"])[l]
        return vt
    shared = dict(w_ada=f(inp["w_ada"]), w_in=f(inp["w_in"]), wgate=wgate, w_out=f(inp["w_out"]), w_up=f(inp["w_up"]),
                  w_down=f(inp["w_down"]), cc=c["cc"], mk=c["mk"], u=c["u"], idf=c["idf"], idb=c["idb"])
    maps = []
    for core in range(8):
        m = dict(shared)
        if core < 4:
            b = core
            m["x"] = x_sample[b]
            m["pe"] = c["pe"]
            m["vt"] = vt_for(cvs[b])
            m["s0"] = np.ascontiguousarray(state[b])
            m["kp"] = np.ones((128, 1), np.float32)
            m["hm"] = c["hm_s"]
            m["cst"] = c["cst_s"]
        else:
            j = core - 4
            m["x"] = np.ascontiguousarray(x_prompt[8 * j:8 * j + 8].reshape(T, D))
            m["pe"] = c["pe0"]
            m["vt"] = vt_for(cctx)
            m["s0"] = np.zeros((2, 2, 4, 64, 128), np.float32)
            m["kp"] = np.zeros((128, 1), np.float32)
            m["hm"] = c["hm_p"]
            m["cst"] = c["cst_p"]
        maps.append(m)
    return maps


def kernel(**inputs):
    if "nc" not in _CACHE:
        _CACHE["nc"] = build_nc()[0]
    nc = _CACHE["nc"]
    maps = _in_maps(inputs)
    res = run_bass_kernel_spmd(nc, maps, core_ids=list(range(8)))
    r = res.results
    y_sample = np.stack([r[b]["y"] for b in range(4)], axis=0).astype(np.float32)
    y_prompt = np.concatenate([r[4 + j]["y"].reshape(8, 256, D) for j in range(4)], axis=0).astype(np.float32)
    ns = np.concatenate([np.transpose(r[4 + j]["ns"], (1, 0, 2, 3, 4, 5)) for j in range(4)], axis=0).astype(np.float32)
    return y_prompt, y_sample, ns
```

```python
import os
import numpy as np
import ml_dtypes
from contextlib import ExitStack
import concourse.bass as bass
import concourse.mybir as mybir
from concourse.bass_utils import run_bass_kernel_spmd

F32 = mybir.dt.float32
BF16 = mybir.dt.bfloat16
AF = mybir.ActivationFunctionType
ALU = mybir.AluOpType

ENGS = ("pe", "act", "dve", "pool", "sp")
N_DMA_SEMS = 16

T = 2048
D = 1024
NK = 8
NTC = 16
IN_COLS = 2080
DFF = 2816
NPAIR = 22
EPS = 1e-6
VT_ROWS = 640
LBASE = lambda l: 8 + l * 257


class Prog:
    def __init__(self):
        self.ops = []
        self.keys = set()

    def op(self, eng, fn, reads=(), writes=(), dma=False):
        self.ops.append((eng, fn, tuple(reads), tuple(writes), dma))
        self.keys.update(reads)
        self.keys.update(writes)

    def barrier(self):
        allk = tuple(self.keys)
        self.op("sp", lambda h: h.nop(), reads=(), writes=allk + ("__bar",))
        for e in ("pe", "act", "dve", "pool"):
            self.op(e, None, reads=("__bar",))

    def analyze(self):
        ops = self.ops
        n = len(ops)
        last_writer = {}
        readers = {}
        need = [None] * n
        signal = [False] * n
        for i, (eng, fn, reads, writes, dma) in enumerate(ops):
            raw = set()
            other = set()
            for r in reads:
                j = last_writer.get(r)
                if j is not None:
                    raw.add(j)
            for w in writes:
                j = last_writer.get(w)
                if j is not None:
                    other.add(j)
                for j in readers.get(w, ()):
                    other.add(j)
            other -= raw
            other.discard(i)
            raw.discard(i)
            keep = {}
            for j, is_raw in [(j, True) for j in raw] + [(j, False) for j in other]:
                ej, _, _, _, dj = ops[j]
                if dj:
                    keep[("d", j)] = j
                    continue
                if ej == eng and not dma:
                    if eng == "pe" or not is_raw:
                        continue
                k = ("e", ej)
                if k not in keep or keep[k] < j:
                    keep[k] = j
            need[i] = sorted(keep.values())
            for j in need[i]:
                signal[j] = True
            for r in reads:
                readers.setdefault(r, []).append(i)
            for w in writes:
                last_writer[w] = i
                readers[w] = []
        cnt = {e: 0 for e in ENGS}
        rr = {e: 0 for e in ENGS}
        dcnt = {}
        dprev = {}
        sig = [None] * n
        waits = [None] * n
        seen = {e: {} for e in ENGS}
        for i, (eng, fn, reads, writes, dma) in enumerate(ops):
            w = []
            for j in need[i]:
                w.append((sig[j][0], sig[j][1]))
            if dma:
                s = ("d", eng, rr[eng] % N_DMA_SEMS)
                rr[eng] += 1
                if s in dprev:
                    w.append((s, dprev[s]))
                dcnt[s] = dcnt.get(s, 0) + 16
                sig[i] = (s, dcnt[s], 16)
                dprev[s] = dcnt[s]
            elif signal[i]:
                cnt[eng] += 1
                sig[i] = (("e", eng), cnt[eng], 1)
            m = {}
            for (k, v) in w:
                if seen[eng].get(k, 0) >= v:
                    continue
                m[k] = max(m.get(k, 0), v)
            for k, v in m.items():
                seen[eng][k] = v
            waits[i] = list(m.items())
        self.sig = sig
        self.waits = waits
        self.semkeys = sorted({s[0] for s in sig if s is not None} |
                              {k for w in waits for (k, v) in w}, key=str)
        self.stats = dict(n_ops=n, signals=dict(cnt), n_waits=sum(len(w) for w in waits),
                          per_eng={e: sum(1 for o in ops if o[0] == e) for e in ENGS})

    def emit_engine(self, eng, h, sems):
        for i, (e, fn, reads, writes, dma) in enumerate(self.ops):
            if e != eng:
                continue
            for (k, v) in self.waits[i]:
                h.wait_ge(sems[k], v)
            if fn is None:
                if self.sig[i] is not None:
                    h.nop().then_inc(sems[self.sig[i][0]], self.sig[i][2])
                continue
            ins = fn(h)
            if self.sig[i] is not None:
                assert ins is not None, ("op must return an instruction", i, e)
                ins.then_inc(sems[self.sig[i][0]], self.sig[i][2])


def run_prog(nc, prog):
    prog.analyze()
    with ExitStack() as st:
        sems = {}
        for k in prog.semkeys:
            sems[k] = st.enter_context(nc.semaphore("s_" + "_".join(str(x) for x in k)))
        block = st.enter_context(nc.Block())

        @block.tensor
        def _(h):
            prog.emit_engine("pe", h, sems)

        @block.scalar
        def _(h):
            prog.emit_engine("act", h, sems)

        @block.vector
        def _(h):
            prog.emit_engine("dve", h, sems)

        @block.gpsimd
        def _(h):
            prog.emit_engine("pool", h, sems)

        @block.sync
        def _(h):
            prog.emit_engine("sp", h, sems)


class _Stop(Exception):
    pass


def build_nc(n_layers=2, dbg=None, stop=None):
    nc = bass.Bass("TRN2", target_bir_lowering=False)
    dt_in = lambda name, shape, dt=F32: nc.dram_tensor(name, list(shape), dt, kind="ExternalInput").ap()
    x_in = dt_in("x", [T, D])
    pe_in = dt_in("pe", [T, D])
    vt_in = dt_in("vt", [VT_ROWS, 128])
    s0_in = dt_in("s0", [2, 2, 4, 64, 128])
    kp_in = dt_in("kp", [128, 1])
    hm_in = dt_in("hm", [128, 64])
    w_ada = dt_in("w_ada", [2, D, 6 * D])
    w_in = dt_in("w_in", [2, D, IN_COLS])
    wgate = dt_in("wgate", [2, 33, 512])
    w_out = dt_in("w_out", [2, D, D])
    w_up = dt_in("w_up", [2, D, 2 * DFF])
    w_down = dt_in("w_down", [2, DFF, D])
    cst_in = dt_in("cst", [64, 128, 1024], BF16)
    cc_in = dt_in("cc", [128, 256], BF16)
    mk_in = dt_in("mk", [128, 1024], BF16)
    u_in = dt_in("u", [128, 256])
    idf_in = dt_in("idf", [128, 128])
    idb_in = dt_in("idb", [128, 128], BF16)
    y_out = nc.dram_tensor("y", [T, D], F32, kind="ExternalOutput").ap()
    ns_out = nc.dram_tensor("ns", [2, 8, 2, 4, 64, 128], F32, kind="ExternalOutput").ap()
    xs = nc.dram_tensor("xs", [T, D], F32, kind="Internal").ap()
    modscr = nc.dram_tensor("modscr", [2, 16, 128], F32, kind="Internal").ap()
    dbg_out = {}
    if dbg:
        for name, shape in dbg.items():
            dbg_out[name] = nc.dram_tensor("dbg_" + name, list(shape), F32, kind="ExternalOutput").ap()

    P = Prog()
    out_keys = []

    with ExitStack() as top:
        uid = [0]

        def sb(st, name, shape, dt):
            uid[0] += 1
            return st.enter_context(nc.sbuf_tensor("sb%d_%s" % (uid[0], name), list(shape), dt))

        def ps(st, name, shape, dt=F32):
            uid[0] += 1
            return st.enter_context(nc.psum_tensor("ps%d_%s" % (uid[0], name), list(shape), dt))

        identb = sb(top, "identb", [128, 128], BF16)
        identf = sb(top, "identf", [128, 128], F32)
        onesb = sb(top, "onesb", [128, 128], BF16)
        cc = sb(top, "cc", [128, 256], BF16)
        mk = sb(top, "mk", [128, 1024], BF16)
        uu = sb(top, "uu", [128, 256], F32)
        uub = sb(top, "uub", [128, 256], BF16)
        hm = sb(top, "hm", [128, 64], F32)
        kp = sb(top, "kp", [128, 1], F32)
        VTT = sb(top, "VTT", [128, VT_ROWS], F32)
        scb = sb(top, "scb", [128, 8], BF16)
        modT = sb(top, "modT", [128, 2, 48], F32)
        gs = sb(top, "gs", [128, 2, 16], F32)
        gt = sb(top, "gt", [128, 2, 16], F32)
        gtT = sb(top, "gtT", [16, 128], F32)
        gg = sb(top, "gg", [128, 2048], F32)
        onesf = sb(top, "onesf", [1, 128], F32)
        wada = [sb(top, "wada%d" % i, [128, 8, 256], BF16) for i in range(3)]
        pb = [ps(top, "pb%d" % i, [128, 512]) for i in range(8)]
        pbm = pb[3][:, 256:512]

        def load(eng, dst, src, key, reads=()):
            P.op(eng, lambda h: h.dma_start(out=dst, in_=src), reads=reads, writes=[key], dma=True)

        load("sp", identb[:], idb_in[:, :], "identb")
        load("sp", identf[:], idf_in[:, :], "identf")
        load("sp", cc[:], cc_in[:, :], "cc")
        load("sp", mk[:], mk_in[:, :], "mk")
        load("sp", uu[:], u_in[:, :], "uu")
        load("sp", hm[:], hm_in[:, :], "hm")
        load("sp", kp[:], kp_in[:, :], "kp")
        P.op("dve", lambda h: h.memset(onesb[:], 1.0), writes=["onesb"])
        P.op("dve", lambda h: h.memset(onesf[:], 1.0), writes=["onesf"])
        P.op("dve", lambda h: h.tensor_copy(out=uub[:], in_=uu[:]), reads=["uu"], writes=["uub"])

        with ExitStack() as s0s:
            vtt = [sb(s0s, "vtt%d" % i, [128, 128], F32) for i in range(2)]
            for i in range(VT_ROWS // 128):
                t = vtt[i % 2]
                load("sp", t[:], vt_in[i * 128:(i + 1) * 128, :], ("vtt", i % 2))
                P.op("pe", (lambda t=t, i=i: lambda h: h.matmul(pb[i % 2][:, 0:128], lhsT=t[:], rhs=identf[:], start=True, stop=True))(),
                     reads=[("vtt", i % 2), "identf"], writes=[("pb", i % 2)])
                P.op("dve", (lambda i=i: lambda h: h.tensor_copy(out=VTT[:, i * 128:(i + 1) * 128], in_=pb[i % 2][:, 0:128]))(),
                     reads=[("pb", i % 2)], writes=["VTT"])
            P.op("act", lambda h: h.activation(out=scb[:], in_=VTT[:, 0:8], func=AF.Silu), reads=["VTT"], writes=["scb"])

        modcnt = [0]

        def mod_dma(l, cb):
            wsrc = w_ada[l].rearrange("(k p) c -> p k c", p=128)
            r = (l * 24 + cb) % 3
            t = wada[r]
            P.op("pool", lambda h: h.dma_start(out=t[:], in_=wsrc[:, :, cb * 256:(cb + 1) * 256]), writes=[("wada", r)], dma=True)

        def mod_mm(l, cb):
            r = (l * 24 + cb) % 3
            t = wada[r]

            def mmod(h):
                ins = None
                for j in range(2):
                    fc = cb * 2 + j
                    for k in range(8):
                        ins = h.matmul(pbm[:, l * 48 + fc:l * 48 + fc + 1], lhsT=t[:, k, j * 128:(j + 1) * 128],
                                       rhs=scb[:, k:k + 1], start=(k == 0), stop=(k == 7))
                return ins
            P.op("pe", mmod, reads=[("wada", r), "scb"], writes=[("pb", 3)])

        def mod_finish(l):
            base = LBASE(l)
            P.op("dve", lambda h: h.tensor_tensor(out=modT[:, l, :], in0=pbm[:, l * 48:(l + 1) * 48],
                                                  in1=VTT[:, base + 176:base + 224], op=ALU.add),
                 reads=[("pb", 3), "VTT"], writes=[("modT", l)])
            for j, (sc0, g0) in enumerate([(8, 224), (32, 240)]):
                P.op("dve", (lambda j=j, sc0=sc0, g0=g0: lambda h: h.scalar_tensor_tensor(
                    out=gs[:, l, j * 8:(j + 1) * 8], in0=modT[:, l, sc0:sc0 + 8], scalar=1.0,
                    in1=VTT[:, base + g0:base + g0 + 8], op0=ALU.add, op1=ALU.mult))(),
                    reads=[("modT", l), "VTT"], writes=[("gs", l)])
            for j, (sc0, g0) in enumerate([(16, 232), (40, 248)]):
                P.op("dve", (lambda j=j, sc0=sc0, g0=g0: lambda h: h.tensor_tensor(
                    out=gt[:, l, j * 8:(j + 1) * 8], in0=modT[:, l, sc0:sc0 + 8],
                    in1=VTT[:, base + g0:base + g0 + 8], op=ALU.mult))(),
                    reads=[("modT", l), "VTT"], writes=[("gt", l)])
            P.op("pe", lambda h: h.matmul(pbm[0:16, 128:256], lhsT=gt[:, l, :], rhs=identf[:], start=True, stop=True),
                 reads=[("gt", l), "identf"], writes=[("pb", 3)])
            P.op("dve", lambda h: h.tensor_copy(out=gtT[:], in_=pbm[0:16, 128:256]), reads=[("pb", 3)], writes=["gtT"])
            P.op("sp", lambda h: h.dma_start(out=modscr[l], in_=gtT[:]), reads=["gtT"], writes=[("modscr", l)], dma=True)

        mod_dma(0, 0)
        mod_dma(0, 1)
        for cb in range(24):
            if cb + 2 < 24:
                mod_dma(0, cb + 2)
            mod_mm(0, cb)
        mod_finish(0)
        import os
        MOD_IL = os.environ.get("MOD_IL", "1") == "1"
        if not MOD_IL and n_layers > 1:
            mod_dma(1, 0)
            mod_dma(1, 1)
            for cb in range(24):
                if cb + 2 < 24:
                    mod_dma(1, cb + 2)
                mod_mm(1, cb)
            mod_finish(1)
        P.barrier()

        def norm_to_T(st, l, which, dst_fn, tag, first=False):
            xr = [sb(st, tag + "xr%d" % i, [128, D], F32) for i in range(4)]
            xn = [sb(st, tag + "xn%d" % i, [128, D], BF16) for i in range(4)]
            per = [sb(st, tag + "per%d" % i, [128, D], F32) for i in range(4)] if first else None
            junk = sb(st, tag + "junk", [128, D], BF16)
            ss = sb(st, tag + "ss", [128, NTC], F32)
            lnv = sb(st, tag + "lnv", [128, NTC], F32)
            rstd = sb(st, tag + "rstd", [128, NTC], F32)
            gcol = which * 8
            shcol = 0 if which == 0 else 24
            def stage1(g):
                pT = [pb[(g % 4) * 2], pb[(g % 4) * 2 + 1]]
                pkeys = [("pb", (g % 4) * 2), ("pb", (g % 4) * 2 + 1)]
                for j in range(2):
                    tc = 2 * g + j
                    r = tc % 4
                    if first:
                        load("sp", xr[r][:], x_in[tc * 128:(tc + 1) * 128, :], (tag + "xr", r))
                        load("sp", per[r][:], pe_in[tc * 128:(tc + 1) * 128, :], (tag + "per", r))
                        P.op("dve", (lambda r=r: lambda h: h.tensor_tensor(out=xr[r][:], in0=xr[r][:], in1=per[r][:], op=ALU.add))(),
                             reads=[(tag + "xr", r), (tag + "per", r)], writes=[(tag + "xr", r)])
                        P.op("pool", (lambda r=r, tc=tc: lambda h: h.dma_start(out=xs[tc * 128:(tc + 1) * 128, :], in_=xr[r][:]))(),
                             reads=[(tag + "xr", r)], writes=[("xs", tc)], dma=True)
                    else:
                        load("sp", xr[r][:], xs[tc * 128:(tc + 1) * 128, :], (tag + "xr", r), reads=[("xs", tc)])
                    P.op("act", (lambda r=r, tc=tc: lambda h: h.activation(out=junk[:], in_=xr[r][:], func=AF.Square, accum_out=ss[:, tc:tc + 1]))(),
                         reads=[(tag + "xr", r)], writes=[tag + "junk", (tag + "ss", tc)])
                    P.op("act", (lambda tc=tc: lambda h: h.activation(out=lnv[:, tc:tc + 1], in_=ss[:, tc:tc + 1], func=AF.Ln, scale=1.0 / D, bias=EPS))(),
                         reads=[(tag + "ss", tc)], writes=[(tag + "lnv", tc)])
                    P.op("act", (lambda tc=tc: lambda h: h.activation(out=rstd[:, tc:tc + 1], in_=lnv[:, tc:tc + 1], func=AF.Exp, scale=-0.5))(),
                         reads=[(tag + "lnv", tc)], writes=[(tag + "rstd", tc)])
                    P.op("dve", (lambda r=r, tc=tc: lambda h: h.tensor_scalar(out=xn[tc % 4][:], in0=xr[r][:], scalar1=rstd[:, tc:tc + 1], scalar2=None, op0=ALU.mult))(),
                         reads=[(tag + "xr", r), (tag + "rstd", tc)], writes=[(tag + "xn", tc % 4)])

                    def tr(h, tc=tc, j=j, pT=pT):
                        ins = None
                        for k in range(8):
                            bank = pT[k // 4]
                            dst = bank[:].bitcast(BF16)[:, (k % 4) * 256 + j * 128:(k % 4) * 256 + (j + 1) * 128]
                            ins = h.transpose(dst, xn[tc % 4][:, k * 128:(k + 1) * 128], identb[:])
                        return ins
                    P.op("pe", tr, reads=[(tag + "xn", tc % 4), "identb"], writes=pkeys)

            def stage2(g):
                pT = [pb[(g % 4) * 2], pb[(g % 4) * 2 + 1]]
                pkeys = [("pb", (g % 4) * 2), ("pb", (g % 4) * 2 + 1)]
                for k in range(8):
                    bank = pT[k // 4]
                    src = bank[:].bitcast(BF16)[:, (k % 4) * 256:(k % 4 + 1) * 256]
                    dst, dkey = dst_fn(k, g)
                    if len(dst.shape) == 3:
                        src = src.rearrange("p (s c) -> p s c", c=dst.shape[2])
                    if k < 4:
                        P.op("act", (lambda src=src, dst=dst, k=k: lambda h: h.activation(
                            out=dst, in_=src, func=AF.Identity, scale=gs[:, l, gcol + k:gcol + k + 1],
                            bias=modT[:, l, shcol + k:shcol + k + 1]))(),
                            reads=pkeys + [("gs", l), ("modT", l)], writes=[dkey])
                    else:
                        P.op("dve", (lambda src=src, dst=dst, k=k: lambda h: h.tensor_scalar(
                            out=dst, in0=src, scalar1=gs[:, l, gcol + k:gcol + k + 1],
                            scalar2=modT[:, l, shcol + k:shcol + k + 1], op0=ALU.mult, op1=ALU.add))(),
                            reads=pkeys + [("gs", l), ("modT", l)], writes=[dkey])
            stage1(0)
            for g in range(8):
                if g + 1 < 8:
                    stage1(g + 1)
                stage2(g)

        def post_norm_1(tc, py, pykeys, junk, junkkey, ss2, lnv2, tag, xr=None, xrkey=None):
            if xr is not None:
                load("sp", xr[:], xs[tc * 128:(tc + 1) * 128, :], xrkey, reads=[("xs", tc)])
            for cb in range(2):
                P.op("act", (lambda cb=cb: lambda h: h.activation(out=junk[:, 0:512], in_=py[cb][:], func=AF.Square, accum_out=ss2[:, 2 * tc + cb:2 * tc + cb + 1]))(),
                     reads=[pykeys[cb]], writes=[junkkey, (tag + "ss2", tc, cb)])
            P.op("dve", lambda h: h.tensor_tensor(out=lnv2[:, tc:tc + 1], in0=ss2[:, 2 * tc:2 * tc + 1], in1=ss2[:, 2 * tc + 1:2 * tc + 2], op=ALU.add),
                 reads=[(tag + "ss2", tc, 0), (tag + "ss2", tc, 1)], writes=[(tag + "lnv2", tc)])

        def post_norm_2(l, which, tc, py, pykeys, xr, xrkey, tmp, tmpkey, lnv2, rstd2, tag, final):
            P.op("act", lambda h: h.activation(out=lnv2[:, tc:tc + 1], in_=lnv2[:, tc:tc + 1], func=AF.Ln, scale=1.0 / D, bias=EPS),
                 reads=[(tag + "lnv2", tc)], writes=[(tag + "lnv2", tc)])
            P.op("act", lambda h: h.activation(out=rstd2[:, tc:tc + 1], in_=lnv2[:, tc:tc + 1], func=AF.Exp, scale=-0.5),
                 reads=[(tag + "lnv2", tc)], writes=[(tag + "rstd2", tc)])
            for cb in range(2):
                P.op("dve", (lambda cb=cb: lambda h: h.scalar_tensor_tensor(
                    out=tmp[:, cb * 512:(cb + 1) * 512], in0=py[cb][:], scalar=rstd2[:, tc:tc + 1],
                    in1=gg[:, which * 1024 + cb * 512:which * 1024 + (cb + 1) * 512], op0=ALU.mult, op1=ALU.mult))(),
                    reads=[pykeys[cb], (tag + "rstd2", tc), "gg"], writes=[tmpkey])
            P.op("dve", lambda h: h.tensor_tensor(out=xr[:], in0=xr[:], in1=tmp[:], op=ALU.add),
                 reads=[xrkey, tmpkey], writes=[xrkey])
            if final:
                P.op("pool", lambda h: h.dma_start(out=y_out[tc * 128:(tc + 1) * 128, :], in_=xr[:]),
                     reads=[xrkey], writes=[("yout", tc)], dma=True)
                out_keys.append(("yout", tc))
            else:
                P.op("pool", lambda h: h.dma_start(out=xs[tc * 128:(tc + 1) * 128, :], in_=xr[:]),
                     reads=[xrkey], writes=[("xs", tc)], dma=True)

        def dbg_store(name, src_ap, dst_ap, rkeys):
            if name in dbg_out:
                P.op("sp", lambda h: h.dma_start(out=dst_ap, in_=src_ap), reads=rkeys, writes=[("dbg", name, id(dst_ap))], dma=True)
                out_keys.append(("dbg", name, id(dst_ap)))

        def do_layer(l):
            base = LBASE(l)
            last = (l == n_layers - 1)
            with ExitStack() as gsc:
                ggrow = sb(gsc, "ggrow", [1, 2048], F32)
                P.op("sp", lambda h: h.dma_start(out=ggrow[:], in_=modscr[l:l + 1].rearrange("o a b -> o (a b)")),
                     reads=[("modscr", l)], writes=["ggrow"], dma=True)
                for j in range(4):
                    P.op("pe", (lambda j=j: lambda h: h.matmul(pb[j % 2][:], lhsT=onesf[:], rhs=ggrow[:, j * 512:(j + 1) * 512], start=True, stop=True))(),
                         reads=["onesf", "ggrow"], writes=[("pb", j % 2)])
                    P.op("dve", (lambda j=j: lambda h: h.tensor_copy(out=gg[:, j * 512:(j + 1) * 512], in_=pb[j % 2][:]))(),
                         reads=[("pb", j % 2)], writes=["gg"])
            P.barrier()
            if stop == "A0":
                return "stop"
            with ExitStack() as mix:
                ogT = sb(mix, "ogT", [128, 4, T], BF16)
                yfT = sb(mix, "yfT", [128, 4, T], BF16)
                with ExitStack() as inp:
                    hT = sb(inp, "hT", [128, 8, T], BF16)
                    winF = sb(inp, "winF", [128, 8, 512], BF16)
                    for k in range(8):
                        P.op("pool", (lambda k=k: lambda h: h.dma_start(out=winF[:, k, :], in_=w_in[l, k * 128:(k + 1) * 128, 0:512]))(),
                             writes=[("win", k)], dma=True)
                    with ExitStack() as pa:
                        norm_to_T(pa, l, 0, lambda k, g: (hT[:, k, g * 256:(g + 1) * 256], ("hT", g // 2, k)), "A", first=(l == 0))
                    P.barrier()
                    if stop == "A":
                        return "stop"
                    if "hT" in dbg_out and l == 0:
                        with ExitStack() as dd:
                            tmpd = sb(dd, "tmpd", [128, T], F32)
                            for k in range(8):
                                P.op("dve", (lambda k=k: lambda h: h.tensor_copy(out=tmpd[:], in_=hT[:, k, :]))(), reads=[("hT", i, kk) for i in range(4) for kk in range(8)], writes=["tmpd"])
                                dbg_store("hT", tmpd[:], dbg_out["hT"][k * 128:(k + 1) * 128, :], ["tmpd"])
                            P.barrier()
                    rot = [0]

                    def nextbank():
                        b = rot[0] % 3
                        rot[0] += 1
                        return b
                    winr = [("win", k) for k in range(8)]
                    with ExitStack() as pbx:
                        win = winF
                        zfT = sb(pbx, "zfT", [128, 4, T], BF16)
                        ZCS = sb(pbx, "ZCS", [128, NTC, 1024], BF16)
                        WOFF = 0

                        def fm_mm(bank, col0, m, tb, win=win, WOFF=WOFF):
                            def f(h):
                                ins = None
                                for k in range(8):
                                    ins = h.matmul(pb[bank][0:m, :], lhsT=win[:, k, col0 - WOFF:col0 - WOFF + m], rhs=hT[:, k, tb * 512:(tb + 1) * 512],
                                                   start=(k == 0), stop=(k == 7))
                                return ins
                            return f
                        for tb in range(4):
                            c0 = tb * 512
                            hk = [("hT", tb, k) for k in range(8)]
                            for fc in range(4):
                                b = nextbank()
                                P.op("pe", fm_mm(b, fc * 128, 128, tb), reads=winr + hk, writes=[("pb", b)])
                                P.op("dve", (lambda b=b, fc=fc, c0=c0: lambda h: h.tensor_copy(out=zfT[:, fc, c0:c0 + 512], in_=pb[b][:]))(),
                                     reads=[("pb", b)], writes=[("zfT", tb)])
                        P.barrier()
                        for tc in range(NTC):
                            t0 = tc * 128
                            b0 = (tc % 2) * 2

                            def zmm(h, t0=t0, b0=b0):
                                ins = None
                                for hd in range(4):
                                    ins = h.matmul(pb[b0 + hd // 2][:, (hd % 2) * 256:(hd % 2 + 1) * 256], lhsT=zfT[:, hd, t0:t0 + 128], rhs=cc[:],
                                                   start=True, stop=True)
                                return ins
                            P.op("pe", zmm, reads=[("zfT", tc // 4), "cc"], writes=[("pb", b0), ("pb", b0 + 1)])
                            P.op("act", (lambda tc=tc, b0=b0: lambda h: h.activation(out=ZCS[:, tc, 0:512], in_=pb[b0][:], func=AF.Copy))(),
                                 reads=[("pb", b0)], writes=[("ZCS", tc)])
                            P.op("dve", (lambda tc=tc, b0=b0: lambda h: h.tensor_copy(out=ZCS[:, tc, 512:1024], in_=pb[b0 + 1][:]))(),
                                 reads=[("pb", b0 + 1)], writes=[("ZCS", tc)])
                        cring = [sb(pbx, "cring%d" % i, [128, 1024], BF16) for i in range(4)]
                        ci = 0
                        for tpb in range(4):
                            for tc in range(NTC):
                                r = ci % 4
                                ci += 1
                                load("sp", cring[r][:], cst_in[tpb * 16 + tc], ("cring", r))

                                def ymm(h, tc=tc, r=r):
                                    ins = None
                                    for hd in range(4):
                                        h.matmul(pb[4 + hd][:], lhsT=ZCS[:, tc, hd * 256:hd * 256 + 128], rhs=cring[r][:, 0:512],
                                                 start=(tc == 0), stop=False)
                                        ins = h.matmul(pb[4 + hd][:], lhsT=ZCS[:, tc, hd * 256 + 128:hd * 256 + 256], rhs=cring[r][:, 512:1024],
                                                       start=False, stop=(tc == NTC - 1))
                                    return ins
                                P.op("pe", ymm, reads=[("ZCS", tc), ("cring", r)], writes=[("pb", 4 + hd) for hd in range(4)])
                                if MOD_IL and l + 1 < n_layers and (ci % 2 == 0):
                                    mcb = ci // 2 - 1
                                    if mcb == 0:
                                        mod_dma(l + 1, 0)
                                        mod_dma(l + 1, 1)
                                    if mcb < 24:
                                        if mcb + 2 < 24:
                                            mod_dma(l + 1, mcb + 2)
                                        mod_mm(l + 1, mcb)
                                    if mcb == 24:
                                        mod_finish(l + 1)
                            for hd in range(4):
                                eng = "act" if hd % 2 == 0 else "dve"
                                if eng == "act":
                                    fn = (lambda hd=hd, tpb=tpb: lambda h: h.activation(out=yfT[:, hd, tpb * 512:(tpb + 1) * 512], in_=pb[4 + hd][:], func=AF.Copy))()
                                else:
                                    fn = (lambda hd=hd, tpb=tpb: lambda h: h.tensor_copy(out=yfT[:, hd, tpb * 512:(tpb + 1) * 512], in_=pb[4 + hd][:]))()
                                P.op(eng, fn, reads=[("pb", 4 + hd)], writes=[("yfT", tpb)])
                    P.barrier()
                    if stop == "C":
                        return "stop"
                    QK = sb(inp, "QK", [128, 8, T], BF16)
                    V = sb(inp, "V", [128, NTC, 512], BF16)
                    sgT = sb(inp, "sgT", [128, 4, T], BF16)
                    dec = sb(inp, "dec", [128, 4, NTC], F32)
                    with ExitStack() as pbx:
                        win = sb(pbx, "winG", [128, 8, IN_COLS - 512], BF16)
                        WOFF = 512
                        wg = sb(pbx, "wg", [33, 512], BF16)
                        zab = sb(pbx, "zab", [33, 1024], BF16)
                        etmp = [sb(pbx, "etmp%d" % i, [128, 512], F32) for i in range(2)]
                        lhi = sb(pbx, "lhi", [128, 512], BF16)
                        llo = sb(pbx, "llo", [128, 512], BF16)
                        Ep = sb(pbx, "Ep", [128, 4, 512], F32)
                        Em = winF[:].rearrange("p k c -> p (k c)").bitcast(F32).rearrange("p (q c) -> p q c", c=512)
                        for k in range(8):
                            P.op("pool", (lambda k=k, win=win: lambda h: h.dma_start(out=win[:, k, :], in_=w_in[l, k * 128:(k + 1) * 128, 512:IN_COLS]))(),
                                 writes=[("win", k)], dma=True)
                        P.op("pool", lambda h: h.dma_start(out=wg[:], in_=wgate[l]), writes=["wg"], dma=True)
                        P.op("dve", lambda h: h.memset(zab[32:33, :], 1.0), writes=["zab1"])

                        def fm_mm(bank, col0, m, tb, win=win, WOFF=WOFF):
                            def f(h):
                                ins = None
                                for k in range(8):
                                    ins = h.matmul(pb[bank][0:m, :], lhsT=win[:, k, col0 - WOFF:col0 - WOFF + m], rhs=hT[:, k, tb * 512:(tb + 1) * 512],
                                                   start=(k == 0), stop=(k == 7))
                                return ins
                            return f
                        for tb in range(4):
                            c0 = tb * 512
                            hk = [("hT", tb, k) for k in range(8)]
                            zt = zab[:, (tb % 2) * 512:(tb % 2 + 1) * 512]
                            zkey = ("zab", tb % 2)
                            P.op("pe", fm_mm(7, 1536, 32, tb), reads=winr + hk, writes=[("pb", 7)])
                            P.op("act", (lambda zt=zt: lambda h: h.activation(out=zt[0:32, :], in_=pb[7][0:32, :], func=AF.Copy))(),
                                 reads=[("pb", 7)], writes=[zkey])

                            def gate_a(j, tb=tb, zt=zt, zkey=zkey):
                                xb = 3 + (j % 2)
                                et = etmp[j % 2]
                                P.op("pe", lambda h: h.matmul(pb[xb][:], lhsT=zt[0:33, j * 128:(j + 1) * 128], rhs=wg[0:33, :], start=True, stop=True),
                                     reads=[zkey, "zab1", "wg"], writes=[("pb", xb)])
                                P.op("act", lambda h: h.activation(out=et[:], in_=pb[xb][:], func=AF.Exp, scale=-1.0), reads=[("pb", xb)], writes=[("etmp", j % 2)])
                                P.op("act", lambda h: h.activation(out=et[:], in_=et[:], func=AF.Ln, bias=1.0), reads=[("etmp", j % 2)], writes=[("etmp", j % 2)])

                            def gate_a2(j):
                                et = etmp[j % 2]
                                P.op("dve", lambda h: h.tensor_copy(out=lhi[:], in_=et[:]), reads=[("etmp", j % 2)], writes=["lhi"])
                                P.op("dve", lambda h: h.tensor_tensor(out=llo[:], in0=et[:], in1=lhi[:], op=ALU.subtract), reads=[("etmp", j % 2), "lhi"], writes=["llo"])

                            def gate_b(j):
                                def cums(h):
                                    ins = None
                                    for q in range(4):
                                        h.matmul(pb[5][:, q * 128:(q + 1) * 128], lhsT=lhi[:, q * 128:(q + 1) * 128],
                                                 rhs=uub[:, (q // 2) * 128:(q // 2 + 1) * 128], start=True, stop=False)
                                        ins = h.matmul(pb[5][:, q * 128:(q + 1) * 128], lhsT=llo[:, q * 128:(q + 1) * 128],
                                                       rhs=uub[:, (q // 2) * 128:(q // 2 + 1) * 128], start=False, stop=True)
                                    return ins
                                P.op("pe", cums, reads=["lhi", "llo", "uub"], writes=[("pb", 5)])
                                pv4 = pb[5][:].rearrange("p (q c) -> p q c", c=128)
                                P.op("act", lambda h: h.activation(out=Ep[:, :, j * 128:(j + 1) * 128], in_=pv4, func=AF.Exp),
                                     reads=[("pb", 5)], writes=[("Ep", j)])
                                P.op("act", lambda h: h.activation(out=Em[:, :, j * 128:(j + 1) * 128], in_=pv4, func=AF.Exp, scale=-1.0),
                                     reads=[("pb", 5)], writes=[("Em", j)])

                            def v_step(j, tb=tb, win=win, WOFF=WOFF):
                                tc = tb * 4 + j
                                t0 = tc * 128
                                vb = 6 + (j % 2)

                                def vmm(h):
                                    ins = None
                                    for k in range(8):
                                        ins = h.matmul(pb[vb][:], lhsT=hT[:, k, t0:t0 + 128], rhs=win[:, k, 1024 - WOFF:1536 - WOFF], start=(k == 0), stop=(k == 7))
                                    return ins
                                P.op("pe", vmm, reads=winr + hk, writes=[("pb", vb)])
                                P.op("dve", lambda h: h.tensor_copy(out=V[:, tc, :], in_=pb[vb][:]), reads=[("pb", vb)], writes=[("V", tc)])
                            gate_a(0)
                            gate_a(1)
                            gate_a2(0)
                            v_step(0)
                            gate_b(0)
                            gate_a(2)
                            gate_a2(1)
                            v_step(1)
                            gate_b(1)
                            gate_a(3)
                            gate_a2(2)
                            v_step(2)
                            gate_b(2)
                            gate_a2(3)
                            v_step(3)
                            gate_b(3)
                            Epk = [("Ep", j) for j in range(4)]
                            Emk = [("Em", j) for j in range(4)]
                            Epv = Ep[:].rearrange("p q (j c) -> p q j c", c=128)
                            P.op("dve", (lambda tb=tb, Epv=Epv: lambda h: h.tensor_copy(out=dec[:, 0:2, tb * 4:(tb + 1) * 4], in_=Epv[:, 0:2, :, 127]))(),
                                 reads=Epk, writes=[("dec", tb)])
                            P.op("dve", (lambda tb=tb, Epv=Epv: lambda h: h.tensor_copy(out=dec[:, 2:4, tb * 4:(tb + 1) * 4], in_=Epv[:, 2:4, :, 0]))(),
                                 reads=Epk, writes=[("dec", tb)])
                            for p in range(2):
                                b = nextbank()
                                P.op("pe", fm_mm(b, 512 + p * 128, 128, tb), reads=winr + hk, writes=[("pb", b)])
                                for d in range(2):
                                    P.op("dve", (lambda b=b, p=p, d=d, c0=c0: lambda h: h.scalar_tensor_tensor(
                                        out=QK[:, d * 2 + p, c0:c0 + 512], in0=pb[b][:], scalar=0.125, in1=Ep[:, d * 2 + p, :],
                                        op0=ALU.mult, op1=ALU.mult))(),
                                        reads=[("pb", b)] + Epk, writes=[("QK", d * 2 + p, tb)])
                            for p in range(2):
                                b = nextbank()
                                P.op("pe", fm_mm(b, 768 + p * 128, 128, tb), reads=winr + hk, writes=[("pb", b)])
                                for d in range(2):
                                    P.op("dve", (lambda b=b, p=p, d=d, c0=c0: lambda h: h.tensor_tensor(
                                        out=QK[:, 4 + d * 2 + p, c0:c0 + 512], in0=pb[b][:], in1=Em[:, d * 2 + p, :], op=ALU.mult))(),
                                        reads=[("pb", b)] + Emk, writes=[("QK", 4 + d * 2 + p, tb)])
                            for fc in range(4):
                                b = nextbank()
                                P.op("pe", fm_mm(b, 1568 + fc * 128, 128, tb), reads=winr + hk, writes=[("pb", b)])
                                P.op("act", (lambda b=b, fc=fc, c0=c0: lambda h: h.activation(out=sgT[:, fc, c0:c0 + 512], in_=pb[b][:], func=AF.Silu))(),
                                     reads=[("pb", b)], writes=[("sgT", tb)])
                    P.barrier()
                    if stop == "B2":
                        return "stop"
                    with ExitStack() as gl:
                        S = [[sb(gl, "S%d_%d" % (q, i), [128, 256], F32) for i in range(2)] for q in range(4)]
                        Sst = hT[:].rearrange("p k t -> p (k t)").rearrange("p (q c v) -> p q c v", q=4, c=NTC)
                        kTm = [sb(gl, "kTm%d" % i, [128, 512], BF16) for i in range(2)]
                        kvs = [sb(gl, "kvs%d" % i, [128, 4, 256], F32) for i in range(2)]
                        deckp = sb(gl, "deckp", [128, 4, NTC], F32)
                        deck = [("dec", i) for i in range(4)]
                        P.op("dve", lambda h: h.tensor_copy(out=deckp[:], in_=dec[:]), reads=deck, writes=["deckp"])
                        dkv = deckp[:].rearrange("p q (a b) -> p q a b", b=2)
                        P.op("dve", lambda h: h.tensor_scalar(out=dkv[:, 0:2, 1:8, 0], in0=dkv[:, 0:2, 1:8, 0], scalar1=kp[:, 0:1], scalar2=None, op0=ALU.mult),
                             reads=["deckp", "kp"], writes=["deckp"])
                        P.op("dve", lambda h: h.tensor_scalar(out=dkv[:, 2:4, 0:7, 1], in0=dkv[:, 2:4, 0:7, 1], scalar1=kp[:, 0:1], scalar2=None, op0=ALU.mult),
                             reads=["deckp", "kp"], writes=["deckp"])
                        for q in range(4):
                            P.op("dve", (lambda q=q: lambda h: h.memset(S[q][0][:], 0.0))(), writes=[("S", q, 0)])
                            d_, p_ = q // 2, q % 2
                            for e in range(2):
                                P.op("sp", (lambda q=q, d_=d_, p_=p_, e=e: lambda h: h.dma_start(
                                    out=S[q][0][e * 64:(e + 1) * 64, e * 128:(e + 1) * 128], in_=s0_in[l, d_, 2 * p_ + e]))(),
                                    reads=[], writes=[("S", q, 0)], dma=True)

                        def chunks_of(step):
                            return [step, step, NTC - 1 - step, NTC - 1 - step]

                        def d1_prep(step):
                            chunk_of = chunks_of(step)
                            pT = pb[step % 2]

                            def ktr(h):
                                ins = None
                                for q in range(4):
                                    c = chunk_of[q]
                                    ins = h.transpose(pT[:].bitcast(BF16)[:, q * 128:(q + 1) * 128], QK[:, 4 + q, c * 128:(c + 1) * 128], identb[:])
                                return ins
                            P.op("pe", ktr, reads=[("QK", 4 + q, chunk_of[q] // 4) for q in range(4)] + ["identb"], writes=[("pb", step % 2)])
                            km = kTm[step % 2]
                            P.op("act", lambda h: h.activation(out=km[:], in_=pT[:].bitcast(BF16)[:, 0:512], func=AF.Copy),
                                 reads=[("pb", step % 2)], writes=[("kTm", step % 2)])
                            kb0 = 2 + (step % 2) * 2

                            def kvmm(h):
                                ins = None
                                for q in range(4):
                                    c = chunk_of[q]
                                    p_ = q % 2
                                    ins = h.matmul(pb[kb0 + q // 2][:, (q % 2) * 256:(q % 2 + 1) * 256], lhsT=km[:, q * 128:(q + 1) * 128],
                                                   rhs=V[:, c, p_ * 256:(p_ + 1) * 256], start=True, stop=True)
                                return ins
                            P.op("pe", kvmm, reads=[("kTm", step % 2)] + [("V", chunk_of[q]) for q in range(4)], writes=[("pb", kb0), ("pb", kb0 + 1)])
                            kv = kvs[step % 2]
                            for q in range(4):
                                c = chunk_of[q]
                                P.op("act", (lambda q=q, c=c: lambda h: h.activation(
                                    out=kv[:, q, :], in_=pb[kb0 + q // 2][:, (q % 2) * 256:(q % 2 + 1) * 256], func=AF.Copy, scale=dec[:, q, c:c + 1]))(),
                                    reads=[("pb", kb0 + q // 2), ("dec", c // 4)], writes=[("kvs", step % 2, q)])

                        def d1_main(step):
                            chunk_of = chunks_of(step)
                            cur, nxt = step % 2, (step + 1) % 2
                            kv = kvs[step % 2]
                            for q in range(4):
                                c = chunk_of[q]
                                d_ = q // 2
                                seg_start = (c % 2 == 0 and c > 0) if d_ == 0 else (c % 2 == 1 and c < NTC - 1)
                                src = S[q][cur][:]
                                dst = Sst[:, q, c, :]
                                if d_ == 0:
                                    sc_ = kp[:, 0:1] if seg_start else 1.0
                                    fn = (lambda src=src, dst=dst, sc_=sc_: lambda h: h.activation(out=dst, in_=src, func=AF.Copy, scale=sc_))()
                                    P.op("act", fn, reads=[("S", q, cur), "kp"], writes=[("Sst", q, c)])
                                else:
                                    if seg_start:
                                        fn = (lambda src=src, dst=dst: lambda h: h.tensor_scalar(out=dst, in0=src, scalar1=kp[:, 0:1], scalar2=None, op0=ALU.mult))()
                                    else:
                                        fn = (lambda src=src, dst=dst: lambda h: h.tensor_copy(out=dst, in_=src))()
                                    P.op("dve", fn, reads=[("S", q, cur), "kp"], writes=[("Sst", q, c)])
                            for q in range(4):
                                c = chunk_of[q]
                                P.op("dve", (lambda q=q, c=c: lambda h: h.scalar_tensor_tensor(
                                    out=S[q][nxt][:], in0=S[q][cur][:], scalar=deckp[:, q, c:c + 1], in1=kv[:, q, :], op0=ALU.mult, op1=ALU.add))(),
                                    reads=[("S", q, cur), "deckp", ("kvs", step % 2, q)], writes=[("S", q, nxt)])
                                d_, p_ = q // 2, q % 2
                                seg_end = (c % 2 == 1) if d_ == 0 else (c % 2 == 0)
                                if seg_end:
                                    seg = c // 2
                                    for e in range(2):
                                        key = ("ns", l, seg, d_, 2 * p_ + e)
                                        P.op("sp", (lambda q=q, seg=seg, d_=d_, p_=p_, e=e: lambda h: h.dma_start(
                                            out=ns_out[l, seg, d_, 2 * p_ + e], in_=S[q][nxt][e * 64:(e + 1) * 64, e * 128:(e + 1) * 128]))(),
                                            reads=[("S", q, nxt)], writes=[key], dma=True)
                                        out_keys.append(key)
                        d1_prep(0)
                        for step in range(NTC):
                            if step + 1 < NTC:
                                d1_prep(step + 1)
                            d1_main(step)
                        P.barrier()
                        if stop == "D1":
                            return "stop"
                        attm = [sb(gl, "attm%d" % i, [128, 1024], BF16) for i in range(3)]
                        qzt = [sb(gl, "qzt%d" % i, [128, 4, 2, 128], BF16) for i in range(3)]
                        for i in range(3):
                            P.op("pool", (lambda i=i: lambda h: h.memset(qzt[i][:], 0.0))(), writes=[("qz", i)])
                        osq = [sb(gl, "osq%d" % i, [128, 512], BF16) for i in range(2)]
                        lno = [sb(gl, "lno%d" % i, [128, 512], F32) for i in range(2)]

                        def d2_a1(c):
                            t0 = c * 128
                            qz = qzt[c % 3]
                            for e in range(2):
                                P.op("act", (lambda e=e: lambda h: h.activation(
                                    out=qz[e * 64:(e + 1) * 64, :, e, :], in_=QK[e * 64:(e + 1) * 64, 0:4, t0:t0 + 128], func=AF.Copy))(),
                                    reads=[("QK", qq, c // 4) for qq in range(4)], writes=[("qz", c % 3)])

                        def d2_a2(c):
                            t0 = c * 128
                            a0 = (c % 2) * 2
                            am = attm[c % 3]
                            qz = qzt[c % 3]

                            def attmm(h):
                                ins = None
                                for d_ in range(2):
                                    for hd in range(4):
                                        e, p_ = hd % 2, hd // 2
                                        ins = h.matmul(pb[a0 + d_][:, hd * 128:(hd + 1) * 128],
                                                       lhsT=QK[:, 4 + d_ * 2 + p_, t0:t0 + 128],
                                                       rhs=qz[:, d_ * 2 + p_, e, :], start=True, stop=True)
                                return ins
                            P.op("pe", attmm, reads=[("QK", i, c // 4) for i in range(4, 8)] + [("qz", c % 3)], writes=[("pb", a0), ("pb", a0 + 1)])
                            for d_ in range(2):
                                P.op("dve", (lambda d_=d_: lambda h: h.tensor_tensor(
                                    out=am[:, d_ * 512:(d_ + 1) * 512], in0=pb[a0 + d_][:], in1=mk[:, d_ * 512:(d_ + 1) * 512], op=ALU.mult))(),
                                    reads=[("pb", a0 + d_), "mk"], writes=[("attm", c % 3, d_)])

                        def d2_b(c):
                            t0 = c * 128
                            po = 4 + (c % 2)
                            am = attm[c % 3]
                            qz = qzt[c % 3]

                            def omm(h):
                                ins = None
                                for hd in range(4):
                                    e, p_ = hd % 2, hd // 2
                                    o_ap = pb[po][:, hd * 128:(hd + 1) * 128]
                                    h.matmul(o_ap, lhsT=V[:, c, hd * 128:(hd + 1) * 128], rhs=am[:, hd * 128:(hd + 1) * 128], start=True, stop=False)
                                    h.matmul(o_ap, lhsT=V[:, c, hd * 128:(hd + 1) * 128], rhs=am[:, 512 + hd * 128:512 + (hd + 1) * 128], start=False, stop=False)
                                    h.matmul(o_ap, lhsT=Sst[:, 0 + p_, c, e * 128:(e + 1) * 128], rhs=qz[:, 0 + p_, e, :], start=False, stop=False)
                                    ins = h.matmul(o_ap, lhsT=Sst[:, 2 + p_, c, e * 128:(e + 1) * 128], rhs=qz[:, 2 + p_, e, :], start=False, stop=True)
                                return ins
                            P.op("pe", omm, reads=[("V", c), ("attm", c % 3, 0), ("attm", c % 3, 1)] + [("Sst", q, c) for q in range(4)] + [("qz", c % 3)],
                                 writes=[("pb", po)])
                            oq = osq[c % 2]
                            P.op("act", lambda h: h.activation(out=oq[:], in_=pb[po][:], func=AF.Square), reads=[("pb", po)], writes=[("osq", c % 2)])

                        def d2_c1(c):
                            pss = 6
                            oq = osq[c % 2]
                            ln_ = lno[c % 2]
                            P.op("pe", lambda h: h.matmul(pb[pss][:], lhsT=onesb[:], rhs=oq[:], start=True, stop=True),
                                 reads=[("osq", c % 2), "onesb"], writes=[("pb", pss)])
                            P.op("act", lambda h: h.activation(out=ln_[:], in_=pb[pss][:], func=AF.Ln, scale=1.0 / 128, bias=EPS),
                                 reads=[("pb", pss)], writes=[("lno", c % 2)])
                            P.op("act", lambda h: h.activation(out=ln_[:], in_=ln_[:], func=AF.Exp, scale=-0.5), reads=[("lno", c % 2)], writes=[("lno", c % 2)])

                        def d2_c2(c):
                            t0 = c * 128
                            po = 4 + (c % 2)
                            ln_ = lno[c % 2]
                            P.op("dve", lambda h: h.scalar_tensor_tensor(out=ln_[:], in0=pb[po][:], scalar=VTT[:, base + 256:base + 257],
                                                                         in1=ln_[:], op0=ALU.mult, op1=ALU.mult),
                                 reads=[("pb", po), ("lno", c % 2), "VTT"], writes=[("lno", c % 2)])
                            og1v = ln_[:].rearrange("p (q c) -> p q c", c=128)
                            P.op("dve", lambda h: h.tensor_tensor(out=ogT[:, :, t0:t0 + 128], in0=og1v, in1=sgT[:, :, t0:t0 + 128], op=ALU.mult),
                                 reads=[("lno", c % 2), ("sgT", c // 4)], writes=[("ogT", c)])
                        for it in range(NTC + 3):
                            if it < NTC:
                                d2_a1(it)
                            if 0 <= it - 2 < NTC:
                                d2_b(it - 2)
                            if 0 <= it - 3 < NTC:
                                d2_c1(it - 3)
                            if it < NTC:
                                d2_a2(it)
                            if 0 <= it - 3 < NTC:
                                d2_c2(it - 3)
                P.barrier()
                if l == 0:
                    with ExitStack() as dd:
                        tmpd = sb(dd, "tmpd2", [128, T], F32)
                        for nm, src in (("yfT", yfT), ("ogT", ogT)):
                            if nm in dbg_out:
                                for k in range(4):
                                    P.op("dve", (lambda k=k, src=src: lambda h: h.tensor_copy(out=tmpd[:], in_=src[:, k, :]))(), reads=list(P.keys), writes=["tmpd2"])
                                    dbg_store(nm, tmpd[:], dbg_out[nm][k * 128:(k + 1) * 128, :], ["tmpd2"])
                        P.barrier()
                if stop == "D2":
                    return "stop"
                with ExitStack() as pe_:
                    wout = sb(pe_, "wout", [128, 8, D], BF16)
                    xrE = [sb(pe_, "xrE%d" % i, [128, D], F32) for i in range(4)]
                    tmpE = [sb(pe_, "tmpE%d" % i, [128, D], F32) for i in range(4)]
                    junkE = sb(pe_, "junkE", [128, 512], BF16)
                    ss2 = sb(pe_, "ss2E", [128, 2 * NTC], F32)
                    lnv2 = sb(pe_, "lnv2E", [128, NTC], F32)
                    rstd2 = sb(pe_, "rstd2E", [128, NTC], F32)
                    for k in range(8):
                        P.op("pool", (lambda k=k: lambda h: h.dma_start(out=wout[:, k, :], in_=w_out[l, k * 128:(k + 1) * 128, :]))(),
                             writes=[("wout", k)], dma=True)
                    for tc in range(NTC):
                        t0 = tc * 128
                        b0 = (tc % 4) * 2

                        def outmm(h, t0=t0, b0=b0):
                            ins = None
                            for cb in range(2):
                                for k in range(8):
                                    src = yfT[:, k, t0:t0 + 128] if k < 4 else ogT[:, k - 4, t0:t0 + 128]
                                    ins = h.matmul(pb[b0 + cb][:], lhsT=src, rhs=wout[:, k, cb * 512:(cb + 1) * 512], start=(k == 0), stop=(k == 7))
                            return ins
                        P.op("pe", outmm, reads=[("wout", k) for k in range(8)] + [("yfT", tc // 4), ("ogT", tc)], writes=[("pb", b0), ("pb", b0 + 1)])
                        post_norm_1(tc, [pb[b0], pb[b0 + 1]], [("pb", b0), ("pb", b0 + 1)], junkE, "junkE", ss2, lnv2, "E", xrE[tc % 4], ("xrE", tc % 4))
                        if tc > 0:
                            pc = tc - 1
                            pb0 = (pc % 4) * 2
                            post_norm_2(l, 0, pc, [pb[pb0], pb[pb0 + 1]], [("pb", pb0), ("pb", pb0 + 1)], xrE[pc % 4], ("xrE", pc % 4),
                                        tmpE[pc % 4], ("tmpE", pc % 4), lnv2, rstd2, "E", False)
                    pc = NTC - 1
                    pb0 = (pc % 4) * 2
                    post_norm_2(l, 0, pc, [pb[pb0], pb[pb0 + 1]], [("pb", pb0), ("pb", pb0 + 1)], xrE[pc % 4], ("xrE", pc % 4),
                                tmpE[pc % 4], ("tmpE", pc % 4), lnv2, rstd2, "E", False)
            P.barrier()
            if l == 0 and "xmix" in dbg_out:
                with ExitStack() as dd:
                    tmpd = sb(dd, "tmpd3", [128, D], F32)
                    for tc in range(NTC):
                        P.op("sp", (lambda tc=tc: lambda h: h.dma_start(out=tmpd[:], in_=xs[tc * 128:(tc + 1) * 128, :]))(), reads=[("xs", tc)], writes=["tmpd3"], dma=True)
                        dbg_store("xmix", tmpd[:], dbg_out["xmix"][tc * 128:(tc + 1) * 128, :], ["tmpd3"])
                    P.barrier()
            if stop == "E":
                return "stop"
            with ExitStack() as ffn:
                h2x = sb(ffn, "h2x", [128, 8, 32, 66], BF16)
                P.op("dve", lambda h: h.memset(h2x[:], 0.0), writes=[("h2x", g, k) for g in range(8) for k in range(8)] + ["h2xhalo"])
                wup = [sb(ffn, "wup%d" % i, [128, 8, 512], BF16) for i in range(3)]
                wupsrc = w_up[l].rearrange("(k p) c -> p k c", p=128)
                WU = [(hf, u) for hf in range(2) for u in range(NPAIR // 2)]

                def wup_dma(wi, part=None):
                    hf, u = WU[wi]
                    wr = wup[wi % 3]
                    wkey = ("wup", wi % 3)
                    for pt in (range(4) if part is None else [part]):
                        half, kq = pt // 2, pt % 2
                        c_src = (DFF if half else 0) + u * 256
                        P.op("pool", (lambda half=half, kq=kq, c_src=c_src: lambda h: h.dma_start(
                            out=wr[:, kq * 4:(kq + 1) * 4, half * 256:(half + 1) * 256],
                            in_=wupsrc[:, kq * 4:(kq + 1) * 4, c_src:c_src + 256]))(), writes=[wkey], dma=True)
                wup_dma(0)
                wup_dma(1)
                with ExitStack() as pf:
                    def dstf(k, g):
                        return h2x[:, k, g * 4:(g + 1) * 4, 1:65], ("h2x", g, k)
                    norm_to_T(pf, l, 1, dstf, "F")
                P.barrier()
                h2k = [("h2x", g, k) for g in range(8) for k in range(8)]
                P.op("dve", lambda h: h.tensor_tensor(out=h2x[:, :, 1:32, 0], in0=h2x[:, :, 0:31, 64],
                                                      in1=hm[:, 1:32].unsqueeze(1).to_broadcast([128, 8, 31]), op=ALU.mult),
                     reads=h2k + ["hm"], writes=["h2xhalo"])
                P.op("dve", lambda h: h.tensor_tensor(out=h2x[:, :, 0:31, 65], in0=h2x[:, :, 1:32, 1],
                                                      in1=hm[:, 32:63].unsqueeze(1).to_broadcast([128, 8, 31]), op=ALU.mult),
                     reads=h2k + ["hm"], writes=["h2xhalo"])
                h2xf = h2x[:].rearrange("p k s c -> p k (s c)")
                wdn = sb(ffn, "wdn", [128, NPAIR, D], BF16)
                aT = sb(ffn, "aT", [128, NPAIR, 1024], BF16)
                NB = int(os.environ.get('FFN_NB', '4'))
                t1 = [sb(ffn, "t1_%d" % i, [128, 6, 64], F32) for i in range(NB)]
                g1 = [sb(ffn, "g1_%d" % i, [128, 6, 64], F32) for i in range(NB)]
                xrG = [sb(ffn, "xrG%d" % i, [128, D], F32) for i in range(2)]
                tmpG = [sb(ffn, "tmpG%d" % i, [128, D], F32) for i in range(2)]
                junkG = sb(ffn, "junkG", [128, 512], BF16)
                ss2g = sb(ffn, "ss2G", [128, 2 * NTC], F32)
                lnv2g = sb(ffn, "lnv2G", [128, NTC], F32)
                rstd2g = sb(ffn, "rstd2G", [128, NTC], F32)
                def wdn_dma():
                    for i in range(NPAIR):
                        P.op("pool", (lambda i=i: lambda h: h.dma_start(out=wdn[:, i, :], in_=w_down[l, i * 128:(i + 1) * 128, :]))(),
                             writes=[("wdn", i)], dma=True)
                BLKS = [(0, 6), (6, 5), (11, 5)] if os.environ.get('FFN_BLK', '655') == '655' else [(0, 4), (4, 4), (8, 4), (12, 4)]
                cnt2 = 0
                for hf in range(2):
                    for u in range(NPAIR // 2):
                        wi = hf * (NPAIR // 2) + u
                        if wi == 1:
                            wdn_dma()
                        wr = wup[wi % 3]
                        wkey = ("wup", wi % 3)
                        uidx = 0
                        for (sl0, nsg) in BLKS:
                            sg0 = hf * 16 + sl0
                            ncol = nsg * 66
                            for ii in range(2):
                                i = 2 * u + ii
                                if wi + 2 < len(WU) and uidx < 4:
                                    wup_dma(wi + 2, uidx)
                                uidx += 1
                                r2 = cnt2 % NB
                                bv = r2 * 2
                                bg = bv + 1
                                cnt2 += 1

                                def upmm(h, wr=wr, ii=ii, sg0=sg0, ncol=ncol, bv=bv, bg=bg):
                                    ins = None
                                    rhsv = [h2xf[:, k, sg0 * 66:sg0 * 66 + ncol] for k in range(8)]
                                    for k in range(8):
                                        h.matmul(pb[bv][:, 0:ncol], lhsT=wr[:, k, ii * 128:(ii + 1) * 128], rhs=rhsv[k], start=(k == 0), stop=(k == 7))
                                    for k in range(8):
                                        ins = h.matmul(pb[bg][:, 0:ncol], lhsT=wr[:, k, 256 + ii * 128:256 + (ii + 1) * 128], rhs=rhsv[k], start=(k == 0), stop=(k == 7))
                                    return ins
                                P.op("pe", upmm, reads=[wkey, "h2xhalo"] + h2k, writes=[("pb", bv), ("pb", bg)])
                                chains = []
                                for (bank, dstt, dkey, coff) in ((bv, t1[r2], ("t1", r2), i), (bg, g1[r2], ("g1", r2), NPAIR + i)):
                                    pvw = pb[bank][:, 0:ncol].rearrange("p (s c) -> p s c", c=66)
                                    dv = dstt[:, 0:nsg, :]
                                    cws = [VTT[:, base + j * 44 + coff:base + j * 44 + coff + 1] for j in range(3)]
                                    cbb = VTT[:, base + 132 + coff:base + 132 + coff + 1]
                                    chains.append((bank, pvw, dv, dkey, cws, cbb))
                                for (bank, pvw, dv, dkey, cws, cbb) in chains:
                                    P.op("act", (lambda pvw=pvw, dv=dv, cws=cws, cbb=cbb: lambda h: h.activation(
                                        out=dv, in_=pvw[:, :, 1:65], func=AF.Identity, scale=cws[1], bias=cbb))(),
                                        reads=[("pb", bank), "VTT"], writes=[dkey])
                                for tap, lo in ((0, 0), (2, 2)):
                                    for (bank, pvw, dv, dkey, cws, cbb) in chains:
                                        P.op("dve", (lambda pvw=pvw, dv=dv, cws=cws, tap=tap, lo=lo: lambda h: h.scalar_tensor_tensor(
                                            out=dv, in0=pvw[:, :, lo:lo + 64], scalar=cws[tap], in1=dv, op0=ALU.mult, op1=ALU.add))(),
                                            reads=[("pb", bank), "VTT", dkey], writes=[dkey])
                                sv = chains[1][2]
                                P.op("act", (lambda sv=sv: lambda h: h.activation(out=sv, in_=sv, func=AF.Silu))(),
                                     reads=[("g1", r2)], writes=[("g1", r2)])
                                a_dst = aT[:, i, sl0 * 64:(sl0 + nsg) * 64].rearrange("p (s c) -> p s c", c=64)
                                P.op("pool", (lambda sv=sv, tv=chains[0][2], a_dst=a_dst: lambda h: h.tensor_tensor(out=a_dst, in0=sv, in1=tv, op=ALU.mult))(),
                                     reads=[("g1", r2), ("t1", r2)], writes=[("aT", i)])
                    P.barrier()
                    for tcl in range(8):
                        tc = hf * 8 + tcl
                        b0 = (tcl % 4) * 2

                        def dnmm(h, tcl=tcl, b0=b0):
                            ins = None
                            for cb in range(2):
                                for i in range(NPAIR):
                                    ins = h.matmul(pb[b0 + cb][:], lhsT=aT[:, i, tcl * 128:(tcl + 1) * 128], rhs=wdn[:, i, cb * 512:(cb + 1) * 512],
                                                   start=(i == 0), stop=(i == NPAIR - 1))
                            return ins
                        P.op("pe", dnmm, reads=[("wdn", i) for i in range(NPAIR)] + [("aT", i) for i in range(NPAIR)],
                             writes=[("pb", b0), ("pb", b0 + 1)])
                        post_norm_1(tc, [pb[b0], pb[b0 + 1]], [("pb", b0), ("pb", b0 + 1)], junkG, "junkG", ss2g, lnv2g, "G", xrG[tc % 2], ("xrG", tc % 2))
                        if tcl > 0:
                            pc = tc - 1
                            pb0 = ((tcl - 1) % 4) * 2
                            post_norm_2(l, 1, pc, [pb[pb0], pb[pb0 + 1]], [("pb", pb0), ("pb", pb0 + 1)], xrG[pc % 2], ("xrG", pc % 2),
                                        tmpG[pc % 2], ("tmpG", pc % 2), lnv2g, rstd2g, "G", last)
                    pc = hf * 8 + 7
                    pb0 = (7 % 4) * 2
                    post_norm_2(l, 1, pc, [pb[pb0], pb[pb0 + 1]], [("pb", pb0), ("pb", pb0 + 1)], xrG[pc % 2], ("xrG", pc % 2),
                                tmpG[pc % 2], ("tmpG", pc % 2), lnv2g, rstd2g, "G", last)
                    P.barrier()
        for l_ in range(n_layers):
            if stop == "stage0" or do_layer(l_) == "stop":
                break
        P.op("sp", None, reads=list(dict.fromkeys(out_keys)))
        run_prog(nc, P)
    return nc, P


_CACHE = {}


def _consts():
    if "c" in _CACHE:
        return _CACHE["c"]
    bf = ml_dtypes.bfloat16

    def dft(n):
        k = np.arange(n)
        ang = 2.0 * np.pi * ((np.outer(k, k) % n).astype(np.float64)) / n
        return np.cos(ang), np.sin(ang)
    c2048, s2048 = dft(2048)
    c256, s256 = dft(256)
    c128, s128 = dft(128)
    samp_c = c2048 / np.sqrt(2048.0)
    samp_s = -s2048 / np.sqrt(2048.0)
    pr_c = np.zeros((2048, 2048))
    pr_s = np.zeros((2048, 2048))
    for i in range(8):
        pr_c[i * 256:(i + 1) * 256, i * 256:(i + 1) * 256] = c256 / 16.0
        pr_s[i * 256:(i + 1) * 256, i * 256:(i + 1) * 256] = -s256 / 16.0

    def tiles(cm, sm):
        out = np.zeros((64, 128, 1024), np.float32)
        for tpb in range(4):
            for tc in range(16):
                out[tpb * 16 + tc, :, 0:512] = cm[tc * 128:(tc + 1) * 128, tpb * 512:(tpb + 1) * 512]
                out[tpb * 16 + tc, :, 512:1024] = sm[tc * 128:(tc + 1) * 128, tpb * 512:(tpb + 1) * 512]
        return out.astype(bf)
    cst_s = tiles(samp_c, samp_s)
    cst_p = tiles(pr_c, pr_s)
    cc = np.concatenate([c128, s128], axis=1) / np.sqrt(128.0)
    j = np.arange(128)[:, None]
    i = np.arange(128)[None, :]
    mf = (j <= i).astype(np.float32)
    mb = (j >= i).astype(np.float32)
    mk = np.concatenate([np.tile(mf, (1, 4)), np.tile(mb, (1, 4))], axis=1).astype(bf)
    u = np.concatenate([mf, mb], axis=1).astype(np.float32) * (-1.0 / 16.0)
    q = 256
    omega = (1.0 / (10000.0 ** (np.arange(q, dtype=np.float32) / q))).astype(np.float32)
    er = np.arange(32, dtype=np.float32)[:, None] * omega
    ec = np.arange(64, dtype=np.float32)[:, None] * omega
    prr = np.concatenate([np.sin(er), np.cos(er)], axis=-1)
    pcc = np.concatenate([np.sin(ec), np.cos(ec)], axis=-1)
    pe = np.concatenate([np.broadcast_to(prr[:, None], (32, 64, 512)), np.broadcast_to(pcc[None], (32, 64, 512))], axis=-1)
    pe = np.ascontiguousarray(pe.reshape(2048, 1024).astype(np.float32))
    hm_s = np.zeros((128, 64), np.float32)
    hm_p = np.zeros((128, 64), np.float32)
    for s in range(32):
        hm_p[:, s] = 0.0 if s % 4 == 0 else 1.0
        hm_p[:, 32 + s] = 0.0 if s % 4 == 3 else 1.0
    c = dict(cst_s=cst_s, cst_p=cst_p, cc=cc.astype(bf), mk=mk, u=u, pe=pe, pe0=np.zeros_like(pe),
             hm_s=hm_s, hm_p=hm_p, idf=np.eye(128, dtype=np.float32), idb=np.eye(128).astype(bf))
    _CACHE["c"] = c
    return c


def _in_maps(inp):
    c = _consts()
    f = lambda a: np.ascontiguousarray(np.asarray(a, dtype=np.float32))
    x_prompt, x_sample = f(inp["x_prompt"]), f(inp["x_sample"])
    state = f(inp["state_gla"])
    cvs = f(inp["c"])
    cctx = f(inp["c_ctx"])
    wgate = np.zeros((2, 33, 512), np.float32)
    wgate[:, 0:16, 0:256] = f(inp["w_gate_f"])
    wgate[:, 16:32, 256:512] = f(inp["w_gate_b"])
    wgate[:, 32, 0:256] = f(inp["b_gate_f"])
    wgate[:, 32, 256:512] = f(inp["b_gate_b"])

    def vt_for(cvec):
        vt = np.zeros((VT_ROWS, 128), np.float32)
        vt[0:8] = cvec.reshape(8, 128)
        for l in range(2):
            b = LBASE(l)
            vt[b:b + 132] = f(inp["conv_w"])[l].reshape(132, 128)
            vt[b + 132:b + 176] = f(inp["conv_b"])[l].reshape(44, 128)
            vt[b + 176:b + 224] = f(inp["b_ada"])[l].reshape(48, 128)
            vt[b + 224:b + 232] = f(inp["g_pre_mix"])[l].reshape(8, 128)
            vt[b + 232:b + 240] = f(inp["g_post_mix"])[l].reshape(8, 128)
            vt[b + 240:b + 248] = f(inp["g_pre_ffn"])[l].reshape(8, 128)
            vt[b + 248:b + 256] = f(inp["g_post_ffn"])[l].reshape(8, 128)
            vt[b + 256] = f(inp["g_gla"])[l]
        return vt
    shared = dict(w_ada=f(inp["w_ada"]), w_in=f(inp["w_in"]), wgate=wgate, w_out=f(inp["w_out"]), w_up=f(inp["w_up"]),
                  w_down=f(inp["w_down"]), cc=c["cc"], mk=c["mk"], u=c["u"], idf=c["idf"], idb=c["idb"])
    maps = []
    for core in range(8):
        m = dict(shared)
        if core < 4:
            b = core
            m["x"] = x_sample[b]
            m["pe"] = c["pe"]
            m["vt"] = vt_for(cvs[b])
            m["s0"] = np.ascontiguousarray(state[b])
            m["kp"] = np.ones((128, 1), np.float32)
            m["hm"] = c["hm_s"]
            m["cst"] = c["cst_s"]
        else:
            j = core - 4
            m["x"] = np.ascontiguousarray(x_prompt[8 * j:8 * j + 8].reshape(T, D))
            m["pe"] = c["pe0"]
            m["vt"] = vt_for(cctx)
            m["s0"] = np.zeros((2, 2, 4, 64, 128), np.float32)
            m["kp"] = np.zeros((128, 1), np.float32)
            m["hm"] = c["hm_p"]
            m["cst"] = c["cst_p"]
        maps.append(m)
    return maps


def kernel(**inputs):
    if "nc" not in _CACHE:
        _CACHE["nc"] = build_nc()[0]
    nc = _CACHE["nc"]
    maps = _in_maps(inputs)
    res = run_bass_kernel_spmd(nc, maps, core_ids=list(range(8)))
    r = res.results
    y_sample = np.stack([r[b]["y"] for b in range(4)], axis=0).astype(np.float32)
    y_prompt = np.concatenate([r[4 + j]["y"].reshape(8, 256, D) for j in range(4)], axis=0).astype(np.float32)
    ns = np.concatenate([np.transpose(r[4 + j]["ns"], (1, 0, 2, 3, 4, 5)) for j in range(4)], axis=0).astype(np.float32)
    return y_prompt, y_sample, ns
```

```python
import os
import numpy as np
import ml_dtypes
from contextlib import ExitStack
import concourse.bass as bass
import concourse.mybir as mybir
from concourse.bass_utils import run_bass_kernel_spmd

F32 = mybir.dt.float32
BF16 = mybir.dt.bfloat16
AF = mybir.ActivationFunctionType
ALU = mybir.AluOpType

ENGS = ("pe", "act", "dve", "pool", "sp")
N_DMA_SEMS = 16

T = 2048
D = 1024
NK = 8
NTC = 16
IN_COLS = 2080
DFF = 2816
NPAIR = 22
EPS = 1e-6
VT_ROWS = 640
LBASE = lambda l: 8 + l * 257


class Prog:
    def __init__(self):
        self.ops = []
        self.keys = set()

    def op(self, eng, fn, reads=(), writes=(), dma=False):
        self.ops.append((eng, fn, tuple(reads), tuple(writes), dma))
        self.keys.update(reads)
        self.keys.update(writes)

    def barrier(self):
        allk = tuple(self.keys)
        self.op("sp", lambda h: h.nop(), reads=(), writes=allk + ("__bar",))
        for e in ("pe", "act", "dve", "pool"):
            self.op(e, None, reads=("__bar",))

    def analyze(self):
        ops = self.ops
        n = len(ops)
        last_writer = {}
        readers = {}
        need = [None] * n
        signal = [False] * n
        for i, (eng, fn, reads, writes, dma) in enumerate(ops):
            raw = set()
            other = set()
            for r in reads:
                j = last_writer.get(r)
                if j is not None:
                    raw.add(j)
            for w in writes:
                j = last_writer.get(w)
                if j is not None:
                    other.add(j)
                for j in readers.get(w, ()):
                    other.add(j)
            other -= raw
            other.discard(i)
            raw.discard(i)
            keep = {}
            for j, is_raw in [(j, True) for j in raw] + [(j, False) for j in other]:
                ej, _, _, _, dj = ops[j]
                if dj:
                    keep[("d", j)] = j
                    continue
                if ej == eng and not dma:
                    if eng == "pe" or not is_raw:
                        continue
                k = ("e", ej)
                if k not in keep or keep[k] < j:
                    keep[k] = j
            need[i] = sorted(keep.values())
            for j in need[i]:
                signal[j] = True
            for r in reads:
                readers.setdefault(r, []).append(i)
            for w in writes:
                last_writer[w] = i
                readers[w] = []
        cnt = {e: 0 for e in ENGS}
        rr = {e: 0 for e in ENGS}
        dcnt = {}
        dprev = {}
        sig = [None] * n
        waits = [None] * n
        seen = {e: {} for e in ENGS}
        for i, (eng, fn, reads, writes, dma) in enumerate(ops):
            w = []
            for j in need[i]:
                w.append((sig[j][0], sig[j][1]))
            if dma:
                s = ("d", eng, rr[eng] % N_DMA_SEMS)
                rr[eng] += 1
                if s in dprev:
                    w.append((s, dprev[s]))
                dcnt[s] = dcnt.get(s, 0) + 16
                sig[i] = (s, dcnt[s], 16)
                dprev[s] = dcnt[s]
            elif signal[i]:
                cnt[eng] += 1
                sig[i] = (("e", eng), cnt[eng], 1)
            m = {}
            for (k, v) in w:
                if seen[eng].get(k, 0) >= v:
                    continue
                m[k] = max(m.get(k, 0), v)
            for k, v in m.items():
                seen[eng][k] = v
            waits[i] = list(m.items())
        self.sig = sig
        self.waits = waits
        self.semkeys = sorted({s[0] for s in sig if s is not None} |
                              {k for w in waits for (k, v) in w}, key=str)
        self.stats = dict(n_ops=n, signals=dict(cnt), n_waits=sum(len(w) for w in waits),
                          per_eng={e: sum(1 for o in ops if o[0] == e) for e in ENGS})

    def emit_engine(self, eng, h, sems):
        for i, (e, fn, reads, writes, dma) in enumerate(self.ops):
            if e != eng:
                continue
            for (k, v) in self.waits[i]:
                h.wait_ge(sems[k], v)
            if fn is None:
                if self.sig[i] is not None:
                    h.nop().then_inc(sems[self.sig[i][0]], self.sig[i][2])
                continue
            ins = fn(h)
            if self.sig[i] is not None:
                assert ins is not None, ("op must return an instruction", i, e)
                ins.then_inc(sems[self.sig[i][0]], self.sig[i][2])


def run_prog(nc, prog):
    prog.analyze()
    with ExitStack() as st:
        sems = {}
        for k in prog.semkeys:
            sems[k] = st.enter_context(nc.semaphore("s_" + "_".join(str(x) for x in k)))
        block = st.enter_context(nc.Block())

        @block.tensor
        def _(h):
            prog.emit_engine("pe", h, sems)

        @block.scalar
        def _(h):
            prog.emit_engine("act", h, sems)

        @block.vector
        def _(h):
            prog.emit_engine("dve", h, sems)

        @block.gpsimd
        def _(h):
            prog.emit_engine("pool", h, sems)

        @block.sync
        def _(h):
            prog.emit_engine("sp", h, sems)


class _Stop(Exception):
    pass


def build_nc(n_layers=2, dbg=None, stop=None):
    nc = bass.Bass("TRN2", target_bir_lowering=False)
    dt_in = lambda name, shape, dt=F32: nc.dram_tensor(name, list(shape), dt, kind="ExternalInput").ap()
    x_in = dt_in("x", [T, D])
    pe_in = dt_in("pe", [T, D])
    vt_in = dt_in("vt", [VT_ROWS, 128])
    s0_in = dt_in("s0", [2, 2, 4, 64, 128])
    kp_in = dt_in("kp", [128, 1])
    hm_in = dt_in("hm", [128, 64])
    w_ada = dt_in("w_ada", [2, D, 6 * D])
    w_in = dt_in("w_in", [2, D, IN_COLS])
    wgate = dt_in("wgate", [2, 33, 512])
    w_out = dt_in("w_out", [2, D, D])
    w_up = dt_in("w_up", [2, D, 2 * DFF])
    w_down = dt_in("w_down", [2, DFF, D])
    cst_in = dt_in("cst", [64, 128, 1024], BF16)
    cc_in = dt_in("cc", [128, 256], BF16)
    mk_in = dt_in("mk", [128, 1024], BF16)
    u_in = dt_in("u", [128, 256])
    idf_in = dt_in("idf", [128, 128])
    idb_in = dt_in("idb", [128, 128], BF16)
    y_out = nc.dram_tensor("y", [T, D], F32, kind="ExternalOutput").ap()
    ns_out = nc.dram_tensor("ns", [2, 8, 2, 4, 64, 128], F32, kind="ExternalOutput").ap()
    xs = nc.dram_tensor("xs", [T, D], F32, kind="Internal").ap()
    modscr = nc.dram_tensor("modscr", [2, 16, 128], F32, kind="Internal").ap()
    dbg_out = {}
    if dbg:
        for name, shape in dbg.items():
            dbg_out[name] = nc.dram_tensor("dbg_" + name, list(shape), F32, kind="ExternalOutput").ap()

    P = Prog()
    out_keys = []

    with ExitStack() as top:
        uid = [0]

        def sb(st, name, shape, dt):
            uid[0] += 1
            return st.enter_context(nc.sbuf_tensor("sb%d_%s" % (uid[0], name), list(shape), dt))

        def ps(st, name, shape, dt=F32):
            uid[0] += 1
            return st.enter_context(nc.psum_tensor("ps%d_%s" % (uid[0], name), list(shape), dt))

        identb = sb(top, "identb", [128, 128], BF16)
        identf = sb(top, "identf", [128, 128], F32)
        onesb = sb(top, "onesb", [128, 128], BF16)
        cc = sb(top, "cc", [128, 256], BF16)
        mk = sb(top, "mk", [128, 1024], BF16)
        uu = sb(top, "uu", [128, 256], F32)
        uub = sb(top, "uub", [128, 256], BF16)
        hm = sb(top, "hm", [128, 64], F32)
        kp = sb(top, "kp", [128, 1], F32)
        VTT = sb(top, "VTT", [128, VT_ROWS], F32)
        scb = sb(top, "scb", [128, 8], BF16)
        modT = sb(top, "modT", [128, 2, 48], F32)
        gs = sb(top, "gs", [128, 2, 16], F32)
        gt = sb(top, "gt", [128, 2, 16], F32)
        gtT = sb(top, "gtT", [16, 128], F32)
        gg = sb(top, "gg", [128, 2048], F32)
        onesf = sb(top, "onesf", [1, 128], F32)
        wada = [sb(top, "wada%d" % i, [128, 8, 256], BF16) for i in range(3)]
        pb = [ps(top, "pb%d" % i, [128, 512]) for i in range(8)]
        pbm = pb[3][:, 256:512]

        def load(eng, dst, src, key, reads=()):
            P.op(eng, lambda h: h.dma_start(out=dst, in_=src), reads=reads, writes=[key], dma=True)

        load("sp", identb[:], idb_in[:, :], "identb")
        load("sp", identf[:], idf_in[:, :], "identf")
        load("sp", cc[:], cc_in[:, :], "cc")
        load("sp", mk[:], mk_in[:, :], "mk")
        load("sp", uu[:], u_in[:, :], "uu")
        load("sp", hm[:], hm_in[:, :], "hm")
        load("sp", kp[:], kp_in[:, :], "kp")
        P.op("dve", lambda h: h.memset(onesb[:], 1.0), writes=["onesb"])
        P.op("dve", lambda h: h.memset(onesf[:], 1.0), writes=["onesf"])
        P.op("dve", lambda h: h.tensor_copy(out=uub[:], in_=uu[:]), reads=["uu"], writes=["uub"])

        with ExitStack() as s0s:
            vtt = [sb(s0s, "vtt%d" % i, [128, 128], F32) for i in range(2)]
            for i in range(VT_ROWS // 128):
                t = vtt[i % 2]
                load("sp", t[:], vt_in[i * 128:(i + 1) * 128, :], ("vtt", i % 2))
                P.op("pe", (lambda t=t, i=i: lambda h: h.matmul(pb[i % 2][:, 0:128], lhsT=t[:], rhs=identf[:], start=True, stop=True))(),
                     reads=[("vtt", i % 2), "identf"], writes=[("pb", i % 2)])
                P.op("dve", (lambda i=i: lambda h: h.tensor_copy(out=VTT[:, i * 128:(i + 1) * 128], in_=pb[i % 2][:, 0:128]))(),
                     reads=[("pb", i % 2)], writes=["VTT"])
            P.op("act", lambda h: h.activation(out=scb[:], in_=VTT[:, 0:8], func=AF.Silu), reads=["VTT"], writes=["scb"])

        modcnt = [0]

        def mod_dma(l, cb):
            wsrc = w_ada[l].rearrange("(k p) c -> p k c", p=128)
            r = (l * 24 + cb) % 3
            t = wada[r]
            P.op("pool", lambda h: h.dma_start(out=t[:], in_=wsrc[:, :, cb * 256:(cb + 1) * 256]), writes=[("wada", r)], dma=True)

        def mod_mm(l, cb):
            r = (l * 24 + cb) % 3
            t = wada[r]

            def mmod(h):
                ins = None
                for j in range(2):
                    fc = cb * 2 + j
                    for k in range(8):
                        ins = h.matmul(pbm[:, l * 48 + fc:l * 48 + fc + 1], lhsT=t[:, k, j * 128:(j + 1) * 128],
                                       rhs=scb[:, k:k + 1], start=(k == 0), stop=(k == 7))
                return ins
            P.op("pe", mmod, reads=[("wada", r), "scb"], writes=[("pb", 3)])

        def mod_finish(l):
            base = LBASE(l)
            P.op("dve", lambda h: h.tensor_tensor(out=modT[:, l, :], in0=pbm[:, l * 48:(l + 1) * 48],
                                                  in1=VTT[:, base + 176:base + 224], op=ALU.add),
                 reads=[("pb", 3), "VTT"], writes=[("modT", l)])
            for j, (sc0, g0) in enumerate([(8, 224), (32, 240)]):
                P.op("dve", (lambda j=j, sc0=sc0, g0=g0: lambda h: h.scalar_tensor_tensor(
                    out=gs[:, l, j * 8:(j + 1) * 8], in0=modT[:, l, sc0:sc0 + 8], scalar=1.0,
                    in1=VTT[:, base + g0:base + g0 + 8], op0=ALU.add, op1=ALU.mult))(),
                    reads=[("modT", l), "VTT"], writes=[("gs", l)])
            for j, (sc0, g0) in enumerate([(16, 232), (40, 248)]):
                P.op("dve", (lambda j=j, sc0=sc0, g0=g0: lambda h: h.tensor_tensor(
                    out=gt[:, l, j * 8:(j + 1) * 8], in0=modT[:, l, sc0:sc0 + 8],
                    in1=VTT[:, base + g0:base + g0 + 8], op=ALU.mult))(),
                    reads=[("modT", l), "VTT"], writes=[("gt", l)])
            P.op("pe", lambda h: h.matmul(pbm[0:16, 128:256], lhsT=gt[:, l, :], rhs=identf[:], start=True, stop=True),
                 reads=[("gt", l), "identf"], writes=[("pb", 3)])
            P.op("dve", lambda h: h.tensor_copy(out=gtT[:], in_=pbm[0:16, 128:256]), reads=[("pb", 3)], writes=["gtT"])
            P.op("sp", lambda h: h.dma_start(out=modscr[l], in_=gtT[:]), reads=["gtT"], writes=[("modscr", l)], dma=True)

        mod_dma(0, 0)
        mod_dma(0, 1)
        for cb in range(24):
            if cb + 2 < 24:
                mod_dma(0, cb + 2)
            mod_mm(0, cb)
        mod_finish(0)
        import os
        MOD_IL = os.environ.get("MOD_IL", "1") == "1"
        if not MOD_IL and n_layers > 1:
            mod_dma(1, 0)
            mod_dma(1, 1)
            for cb in range(24):
                if cb + 2 < 24:
                    mod_dma(1, cb + 2)
                mod_mm(1, cb)
            mod_finish(1)
        P.barrier()

        def norm_to_T(st, l, which, dst_fn, tag, first=False):
            xr = [sb(st, tag + "xr%d" % i, [128, D], F32) for i in range(4)]
            xn = [sb(st, tag + "xn%d" % i, [128, D], BF16) for i in range(4)]
            per = [sb(st, tag + "per%d" % i, [128, D], F32) for i in range(4)] if first else None
            junk = sb(st, tag + "junk", [128, D], BF16)
            ss = sb(st, tag + "ss", [128, NTC], F32)
            lnv = sb(st, tag + "lnv", [128, NTC], F32)
            rstd = sb(st, tag + "rstd", [128, NTC], F32)
            gcol = which * 8
            shcol = 0 if which == 0 else 24
            def stage1(g):
                pT = [pb[(g % 4) * 2], pb[(g % 4) * 2 + 1]]
                pkeys = [("pb", (g % 4) * 2), ("pb", (g % 4) * 2 + 1)]
                for j in range(2):
                    tc = 2 * g + j
                    r = tc % 4
                    if first:
                        load("sp", xr[r][:], x_in[tc * 128:(tc + 1) * 128, :], (tag + "xr", r))
                        load("sp", per[r][:], pe_in[tc * 128:(tc + 1) * 128, :], (tag + "per", r))
                        P.op("dve", (lambda r=r: lambda h: h.tensor_tensor(out=xr[r][:], in0=xr[r][:], in1=per[r][:], op=ALU.add))(),
                             reads=[(tag + "xr", r), (tag + "per", r)], writes=[(tag + "xr", r)])
                        P.op("pool", (lambda r=r, tc=tc: lambda h: h.dma_start(out=xs[tc * 128:(tc + 1) * 128, :], in_=xr[r][:]))(),
                             reads=[(tag + "xr", r)], writes=[("xs", tc)], dma=True)
                    else:
                        load("sp", xr[r][:], xs[tc * 128:(tc + 1) * 128, :], (tag + "xr", r), reads=[("xs", tc)])
                    P.op("act", (lambda r=r, tc=tc: lambda h: h.activation(out=junk[:], in_=xr[r][:], func=AF.Square, accum_out=ss[:, tc:tc + 1]))(),
                         reads=[(tag + "xr", r)], writes=[tag + "junk", (tag + "ss", tc)])
                    P.op("act", (lambda tc=tc: lambda h: h.activation(out=lnv[:, tc:tc + 1], in_=ss[:, tc:tc + 1], func=AF.Ln, scale=1.0 / D, bias=EPS))(),
                         reads=[(tag + "ss", tc)], writes=[(tag + "lnv", tc)])
                    P.op("act", (lambda tc=tc: lambda h: h.activation(out=rstd[:, tc:tc + 1], in_=lnv[:, tc:tc + 1], func=AF.Exp, scale=-0.5))(),
                         reads=[(tag + "lnv", tc)], writes=[(tag + "rstd", tc)])
                    P.op("dve", (lambda r=r, tc=tc: lambda h: h.tensor_scalar(out=xn[tc % 4][:], in0=xr[r][:], scalar1=rstd[:, tc:tc + 1], scalar2=None, op0=ALU.mult))(),
                         reads=[(tag + "xr", r), (tag + "rstd", tc)], writes=[(tag + "xn", tc % 4)])

                    def tr(h, tc=tc, j=j, pT=pT):
                        ins = None
                        for k in range(8):
                            bank = pT[k // 4]
                            dst = bank[:].bitcast(BF16)[:, (k % 4) * 256 + j * 128:(k % 4) * 256 + (j + 1) * 128]
                            ins = h.transpose(dst, xn[tc % 4][:, k * 128:(k + 1) * 128], identb[:])
                        return ins
                    P.op("pe", tr, reads=[(tag + "xn", tc % 4), "identb"], writes=pkeys)

            def stage2(g):
                pT = [pb[(g % 4) * 2], pb[(g % 4) * 2 + 1]]
                pkeys = [("pb", (g % 4) * 2), ("pb", (g % 4) * 2 + 1)]
                for k in range(8):
                    bank = pT[k // 4]
                    src = bank[:].bitcast(BF16)[:, (k % 4) * 256:(k % 4 + 1) * 256]
                    dst, dkey = dst_fn(k, g)
                    if len(dst.shape) == 3:
                        src = src.rearrange("p (s c) -> p s c", c=dst.shape[2])
                    if k < 4:
                        P.op("act", (lambda src=src, dst=dst, k=k: lambda h: h.activation(
                            out=dst, in_=src, func=AF.Identity, scale=gs[:, l, gcol + k:gcol + k + 1],
                            bias=modT[:, l, shcol + k:shcol + k + 1]))(),
                            reads=pkeys + [("gs", l), ("modT", l)], writes=[dkey])
                    else:
                        P.op("dve", (lambda src=src, dst=dst, k=k: lambda h: h.tensor_scalar(
                            out=dst, in0=src, scalar1=gs[:, l, gcol + k:gcol + k + 1],
                            scalar2=modT[:, l, shcol + k:shcol + k + 1], op0=ALU.mult, op1=ALU.add))(),
                            reads=pkeys + [("gs", l), ("modT", l)], writes=[dkey])
            stage1(0)
            for g in range(8):
                if g + 1 < 8:
                    stage1(g + 1)
                stage2(g)

        def post_norm_1(tc, py, pykeys, junk, junkkey, ss2, lnv2, tag, xr=None, xrkey=None):
            if xr is not None:
                load("sp", xr[:], xs[tc * 128:(tc + 1) * 128, :], xrkey, reads=[("xs", tc)])
            for cb in range(2):
                P.op("act", (lambda cb=cb: lambda h: h.activation(out=junk[:, 0:512], in_=py[cb][:], func=AF.Square, accum_out=ss2[:, 2 * tc + cb:2 * tc + cb + 1]))(),
                     reads=[pykeys[cb]], writes=[junkkey, (tag + "ss2", tc, cb)])
            P.op("dve", lambda h: h.tensor_tensor(out=lnv2[:, tc:tc + 1], in0=ss2[:, 2 * tc:2 * tc + 1], in1=ss2[:, 2 * tc + 1:2 * tc + 2], op=ALU.add),
                 reads=[(tag + "ss2", tc, 0), (tag + "ss2", tc, 1)], writes=[(tag + "lnv2", tc)])

        def post_norm_2(l, which, tc, py, pykeys, xr, xrkey, tmp, tmpkey, lnv2, rstd2, tag, final):
            P.op("act", lambda h: h.activation(out=lnv2[:, tc:tc + 1], in_=lnv2[:, tc:tc + 1], func=AF.Ln, scale=1.0 / D, bias=EPS),
                 reads=[(tag + "lnv2", tc)], writes=[(tag + "lnv2", tc)])
            P.op("act", lambda h: h.activation(out=rstd2[:, tc:tc + 1], in_=lnv2[:, tc:tc + 1], func=AF.Exp, scale=-0.5),
                 reads=[(tag + "lnv2", tc)], writes=[(tag + "rstd2", tc)])
            for cb in range(2):
                P.op("dve", (lambda cb=cb: lambda h: h.scalar_tensor_tensor(
                    out=tmp[:, cb * 512:(cb + 1) * 512], in0=py[cb][:], scalar=rstd2[:, tc:tc + 1],
                    in1=gg[:, which * 1024 + cb * 512:which * 1024 + (cb + 1) * 512], op0=ALU.mult, op1=ALU.mult))(),
                    reads=[pykeys[cb], (tag + "rstd2", tc), "gg"], writes=[tmpkey])
            P.op("dve", lambda h: h.tensor_tensor(out=xr[:], in0=xr[:], in1=tmp[:], op=ALU.add),
                 reads=[xrkey, tmpkey], writes=[xrkey])
            if final:
                P.op("pool", lambda h: h.dma_start(out=y_out[tc * 128:(tc + 1) * 128, :], in_=xr[:]),
                     reads=[xrkey], writes=[("yout", tc)], dma=True)
                out_keys.append(("yout", tc))
            else:
                P.op("pool", lambda h: h.dma_start(out=xs[tc * 128:(tc + 1) * 128, :], in_=xr[:]),
                     reads=[xrkey], writes=[("xs", tc)], dma=True)

        def dbg_store(name, src_ap, dst_ap, rkeys):
            if name in dbg_out:
                P.op("sp", lambda h: h.dma_start(out=dst_ap, in_=src_ap), reads=rkeys, writes=[("dbg", name, id(dst_ap))], dma=True)
                out_keys.append(("dbg", name, id(dst_ap)))

        def do_layer(l):
            base = LBASE(l)
            last = (l == n_layers - 1)
            with ExitStack() as gsc:
                ggrow = sb(gsc, "ggrow", [1, 2048], F32)
                P.op("sp", lambda h: h.dma_start(out=ggrow[:], in_=modscr[l:l + 1].rearrange("o a b -> o (a b)")),
                     reads=[("modscr", l)], writes=["ggrow"], dma=True)
                for j in range(4):
                    P.op("pe", (lambda j=j: lambda h: h.matmul(pb[j % 2][:], lhsT=onesf[:], rhs=ggrow[:, j * 512:(j + 1) * 512], start=True, stop=True))(),
                         reads=["onesf", "ggrow"], writes=[("pb", j % 2)])
                    P.op("dve", (lambda j=j: lambda h: h.tensor_copy(out=gg[:, j * 512:(j + 1) * 512], in_=pb[j % 2][:]))(),
                         reads=[("pb", j % 2)], writes=["gg"])
            P.barrier()
            if stop == "A0":
                return "stop"
            with ExitStack() as mix:
                ogT = sb(mix, "ogT", [128, 4, T], BF16)
                yfT = sb(mix, "yfT", [128, 4, T], BF16)
                with ExitStack() as inp:
                    hT = sb(inp, "hT", [128, 8, T], BF16)
                    winF = sb(inp, "winF", [128, 8, 512], BF16)
                    for k in range(8):
                        P.op("pool", (lambda k=k: lambda h: h.dma_start(out=winF[:, k, :], in_=w_in[l, k * 128:(k + 1) * 128, 0:512]))(),
                             writes=[("win", k)], dma=True)
                    wVA = sb(inp, "wVA", [128, 8, 544], BF16)
                    for k in range(8):
                        P.op("pool", (lambda k=k: lambda h: h.dma_start(out=wVA[:, k, :], in_=w_in[l, k * 128:(k + 1) * 128, 1024:1568]))(),
                             writes=[("wva", k)], dma=True)
                    wvar = [("wva", k) for k in range(8)]
                    with ExitStack() as pa:
                        norm_to_T(pa, l, 0, lambda k, g: (hT[:, k, g * 256:(g + 1) * 256], ("hT", g // 2, k)), "A", first=(l == 0))
                    P.barrier()
                    if stop == "A":
                        return "stop"
                    if "hT" in dbg_out and l == 0:
                        with ExitStack() as dd:
                            tmpd = sb(dd, "tmpd", [128, T], F32)
                            for k in range(8):
                                P.op("dve", (lambda k=k: lambda h: h.tensor_copy(out=tmpd[:], in_=hT[:, k, :]))(), reads=[("hT", i, kk) for i in range(4) for kk in range(8)], writes=["tmpd"])
                                dbg_store("hT", tmpd[:], dbg_out["hT"][k * 128:(k + 1) * 128, :], ["tmpd"])
                            P.barrier()
                    rot = [0]

                    def nextbank():
                        b = rot[0] % 3
                        rot[0] += 1
                        return b
                    winr = [("win", k) for k in range(8)]
                    with ExitStack() as pbx:
                        win = winF
                        zfT = sb(pbx, "zfT", [128, 4, T], BF16)
                        ZCS = sb(pbx, "ZCS", [128, NTC, 1024], BF16)
                        WOFF = 0

                        def fm_mm(bank, col0, m, tb, win=win, WOFF=WOFF):
                            def f(h):
                                ins = None
                                for k in range(8):
                                    ins = h.matmul(pb[bank][0:m, :], lhsT=win[:, k, col0 - WOFF:col0 - WOFF + m], rhs=hT[:, k, tb * 512:(tb + 1) * 512],
                                                   start=(k == 0), stop=(k == 7))
                                return ins
                            return f
                        for tb in range(4):
                            c0 = tb * 512
                            hk = [("hT", tb, k) for k in range(8)]
                            for fc in range(4):
                                b = nextbank()
                                P.op("pe", fm_mm(b, fc * 128, 128, tb), reads=winr + hk, writes=[("pb", b)])
                                P.op("dve", (lambda b=b, fc=fc, c0=c0: lambda h: h.tensor_copy(out=zfT[:, fc, c0:c0 + 512], in_=pb[b][:]))(),
                                     reads=[("pb", b)], writes=[("zfT", tb)])
                        P.barrier()
                        for tc in range(NTC):
                            t0 = tc * 128
                            b0 = (tc % 2) * 2

                            def zmm(h, t0=t0, b0=b0):
                                ins = None
                                for hd in range(4):
                                    ins = h.matmul(pb[b0 + hd // 2][:, (hd % 2) * 256:(hd % 2 + 1) * 256], lhsT=zfT[:, hd, t0:t0 + 128], rhs=cc[:],
                                                   start=True, stop=True)
                                return ins
                            P.op("pe", zmm, reads=[("zfT", tc // 4), "cc"], writes=[("pb", b0), ("pb", b0 + 1)])
                            P.op("act", (lambda tc=tc, b0=b0: lambda h: h.activation(out=ZCS[:, tc, 0:512], in_=pb[b0][:], func=AF.Copy))(),
                                 reads=[("pb", b0)], writes=[("ZCS", tc)])
                            P.op("dve", (lambda tc=tc, b0=b0: lambda h: h.tensor_copy(out=ZCS[:, tc, 512:1024], in_=pb[b0 + 1][:]))(),
                                 reads=[("pb", b0 + 1)], writes=[("ZCS", tc)])
                        cring = [sb(pbx, "cring%d" % i, [128, 1024], BF16) for i in range(4)]
                        ci = 0
                        for tpb in range(4):
                            for tc in range(NTC):
                                r = ci % 4
                                ci += 1
                                load("sp", cring[r][:], cst_in[tpb * 16 + tc], ("cring", r))

                                def ymm(h, tc=tc, r=r):
                                    ins = None
                                    for hd in range(4):
                                        h.matmul(pb[4 + hd][:], lhsT=ZCS[:, tc, hd * 256:hd * 256 + 128], rhs=cring[r][:, 0:512],
                                                 start=(tc == 0), stop=False)
                                        ins = h.matmul(pb[4 + hd][:], lhsT=ZCS[:, tc, hd * 256 + 128:hd * 256 + 256], rhs=cring[r][:, 512:1024],
                                                       start=False, stop=(tc == NTC - 1))
                                    return ins
                                P.op("pe", ymm, reads=[("ZCS", tc), ("cring", r)], writes=[("pb", 4 + hd) for hd in range(4)])
                                if MOD_IL and l + 1 < n_layers and (ci % 2 == 0):
                                    mcb = ci // 2 - 1
                                    if mcb == 0:
                                        mod_dma(l + 1, 0)
                                        mod_dma(l + 1, 1)
                                    if mcb < 24:
                                        if mcb + 2 < 24:
                                            mod_dma(l + 1, mcb + 2)
                                        mod_mm(l + 1, mcb)
                                    if mcb == 24:
                                        mod_finish(l + 1)
                            for hd in range(4):
                                eng = "act" if hd % 2 == 0 else "dve"
                                if eng == "act":
                                    fn = (lambda hd=hd, tpb=tpb: lambda h: h.activation(out=yfT[:, hd, tpb * 512:(tpb + 1) * 512], in_=pb[4 + hd][:], func=AF.Copy))()
                                else:
                                    fn = (lambda hd=hd, tpb=tpb: lambda h: h.tensor_copy(out=yfT[:, hd, tpb * 512:(tpb + 1) * 512], in_=pb[4 + hd][:]))()
                                P.op(eng, fn, reads=[("pb", 4 + hd)], writes=[("yfT", tpb)])
                    P.barrier()
                    if stop == "C":
                        return "stop"
                    QK = sb(inp, "QK", [128, 8, T], BF16)
                    V = sb(inp, "V", [128, NTC, 512], BF16)
                    sgT = sb(inp, "sgT", [128, 4, T], BF16)
                    dec = sb(inp, "dec", [128, 4, NTC], F32)
                    with ExitStack() as pbx:
                        win = sb(pbx, "winG", [128, 8, 1024], BF16)
                        WOFF = 512
                        wg = sb(pbx, "wg", [33, 512], BF16)
                        zab = sb(pbx, "zab", [33, 1024], BF16)
                        etmp = [sb(pbx, "etmp%d" % i, [128, 512], F32) for i in range(2)]
                        lhi = sb(pbx, "lhi", [128, 512], BF16)
                        llo = sb(pbx, "llo", [128, 512], BF16)
                        Ep = sb(pbx, "Ep", [128, 4, 512], F32)
                        Em = winF[:].rearrange("p k c -> p (k c)").bitcast(F32).rearrange("p (q c) -> p q c", c=512)
                        for k in range(8):
                            P.op("pool", (lambda k=k, win=win: lambda h: h.dma_start(out=win[:, k, 0:512], in_=w_in[l, k * 128:(k + 1) * 128, 512:1024]))(),
                                 writes=[("win", k)], dma=True)
                        for k in range(8):
                            P.op("pool", (lambda k=k, win=win: lambda h: h.dma_start(out=win[:, k, 512:1024], in_=w_in[l, k * 128:(k + 1) * 128, 1568:IN_COLS]))(),
                                 writes=[("win", k)], dma=True)
                        P.op("pool", lambda h: h.dma_start(out=wg[:], in_=wgate[l]), writes=["wg"], dma=True)
                        P.op("dve", lambda h: h.memset(zab[32:33, :], 1.0), writes=["zab1"])

                        def fm_mm(bank, col0, m, tb, win=win):
                            if 1024 <= col0 < 1568:
                                wt, lc = wVA, col0 - 1024
                            elif col0 >= 1568:
                                wt, lc = win, col0 - 1568 + 512
                            else:
                                wt, lc = win, col0 - 512

                            def f(h):
                                ins = None
                                for k in range(8):
                                    ins = h.matmul(pb[bank][0:m, :], lhsT=wt[:, k, lc:lc + m], rhs=hT[:, k, tb * 512:(tb + 1) * 512],
                                                   start=(k == 0), stop=(k == 7))
                                return ins
                            return f
                        for tb in range(4):
                            c0 = tb * 512
                            hk = [("hT", tb, k) for k in range(8)]
                            zt = zab[:, (tb % 2) * 512:(tb % 2 + 1) * 512]
                            zkey = ("zab", tb % 2)
                            P.op("pe", fm_mm(7, 1536, 32, tb), reads=wvar + hk, writes=[("pb", 7)])
                            P.op("act", (lambda zt=zt: lambda h: h.activation(out=zt[0:32, :], in_=pb[7][0:32, :], func=AF.Copy))(),
                                 reads=[("pb", 7)], writes=[zkey])

                            def gate_a(j, tb=tb, zt=zt, zkey=zkey):
                                xb = 3 + (j % 2)
                                et = etmp[j % 2]
                                P.op("pe", lambda h: h.matmul(pb[xb][:], lhsT=zt[0:33, j * 128:(j + 1) * 128], rhs=wg[0:33, :], start=True, stop=True),
                                     reads=[zkey, "zab1", "wg"], writes=[("pb", xb)])
                                P.op("act", lambda h: h.activation(out=et[:], in_=pb[xb][:], func=AF.Exp, scale=-1.0), reads=[("pb", xb)], writes=[("etmp", j % 2)])
                                P.op("act", lambda h: h.activation(out=et[:], in_=et[:], func=AF.Ln, bias=1.0), reads=[("etmp", j % 2)], writes=[("etmp", j % 2)])

                            def gate_a2(j):
                                et = etmp[j % 2]
                                P.op("dve", lambda h: h.tensor_copy(out=lhi[:], in_=et[:]), reads=[("etmp", j % 2)], writes=["lhi"])
                                P.op("dve", lambda h: h.tensor_tensor(out=llo[:], in0=et[:], in1=lhi[:], op=ALU.subtract), reads=[("etmp", j % 2), "lhi"], writes=["llo"])

                            def gate_b(j):
                                def cums(h):
                                    ins = None
                                    for q in range(4):
                                        h.matmul(pb[5][:, q * 128:(q + 1) * 128], lhsT=lhi[:, q * 128:(q + 1) * 128],
                                                 rhs=uub[:, (q // 2) * 128:(q // 2 + 1) * 128], start=True, stop=False)
                                        ins = h.matmul(pb[5][:, q * 128:(q + 1) * 128], lhsT=llo[:, q * 128:(q + 1) * 128],
                                                       rhs=uub[:, (q // 2) * 128:(q // 2 + 1) * 128], start=False, stop=True)
                                    return ins
                                P.op("pe", cums, reads=["lhi", "llo", "uub"], writes=[("pb", 5)])
                                pv4 = pb[5][:].rearrange("p (q c) -> p q c", c=128)
                                P.op("act", lambda h: h.activation(out=Ep[:, :, j * 128:(j + 1) * 128], in_=pv4, func=AF.Exp),
                                     reads=[("pb", 5)], writes=[("Ep", j)])
                                P.op("act", lambda h: h.activation(out=Em[:, :, j * 128:(j + 1) * 128], in_=pv4, func=AF.Exp, scale=-1.0),
                                     reads=[("pb", 5)], writes=[("Em", j)])

                            def v_step(j, tb=tb, win=win, WOFF=WOFF):
                                tc = tb * 4 + j
                                t0 = tc * 128
                                vb = 6 + (j % 2)

                                def vmm(h):
                                    ins = None
                                    for k in range(8):
                                        ins = h.matmul(pb[vb][:], lhsT=hT[:, k, t0:t0 + 128], rhs=wVA[:, k, 0:512], start=(k == 0), stop=(k == 7))
                                    return ins
                                P.op("pe", vmm, reads=wvar + hk, writes=[("pb", vb)])
                                P.op("dve", lambda h: h.tensor_copy(out=V[:, tc, :], in_=pb[vb][:]), reads=[("pb", vb)], writes=[("V", tc)])
                            gate_a(0)
                            gate_a(1)
                            gate_a2(0)
                            v_step(0)
                            gate_b(0)
                            gate_a(2)
                            gate_a2(1)
                            v_step(1)
                            gate_b(1)
                            gate_a(3)
                            gate_a2(2)
                            v_step(2)
                            gate_b(2)
                            gate_a2(3)
                            v_step(3)
                            gate_b(3)
                            Epk = [("Ep", j) for j in range(4)]
                            Emk = [("Em", j) for j in range(4)]
                            Epv = Ep[:].rearrange("p q (j c) -> p q j c", c=128)
                            P.op("dve", (lambda tb=tb, Epv=Epv: lambda h: h.tensor_copy(out=dec[:, 0:2, tb * 4:(tb + 1) * 4], in_=Epv[:, 0:2, :, 127]))(),
                                 reads=Epk, writes=[("dec", tb)])
                            P.op("dve", (lambda tb=tb, Epv=Epv: lambda h: h.tensor_copy(out=dec[:, 2:4, tb * 4:(tb + 1) * 4], in_=Epv[:, 2:4, :, 0]))(),
                                 reads=Epk, writes=[("dec", tb)])
                            for p in range(2):
                                b = nextbank()
                                P.op("pe", fm_mm(b, 512 + p * 128, 128, tb), reads=winr + hk, writes=[("pb", b)])
                                for d in range(2):
                                    P.op("dve", (lambda b=b, p=p, d=d, c0=c0: lambda h: h.scalar_tensor_tensor(
                                        out=QK[:, d * 2 + p, c0:c0 + 512], in0=pb[b][:], scalar=0.125, in1=Ep[:, d * 2 + p, :],
                                        op0=ALU.mult, op1=ALU.mult))(),
                                        reads=[("pb", b)] + Epk, writes=[("QK", d * 2 + p, tb)])
                            for p in range(2):
                                b = nextbank()
                                P.op("pe", fm_mm(b, 768 + p * 128, 128, tb), reads=winr + hk, writes=[("pb", b)])
                                for d in range(2):
                                    P.op("dve", (lambda b=b, p=p, d=d, c0=c0: lambda h: h.tensor_tensor(
                                        out=QK[:, 4 + d * 2 + p, c0:c0 + 512], in0=pb[b][:], in1=Em[:, d * 2 + p, :], op=ALU.mult))(),
                                        reads=[("pb", b)] + Emk, writes=[("QK", 4 + d * 2 + p, tb)])
                            for fc in range(4):
                                b = nextbank()
                                P.op("pe", fm_mm(b, 1568 + fc * 128, 128, tb), reads=winr + hk, writes=[("pb", b)])
                                P.op("act", (lambda b=b, fc=fc, c0=c0: lambda h: h.activation(out=sgT[:, fc, c0:c0 + 512], in_=pb[b][:], func=AF.Silu))(),
                                     reads=[("pb", b)], writes=[("sgT", tb)])
                    P.barrier()
                    if stop == "B2":
                        return "stop"
                    with ExitStack() as gl:
                        S = [[sb(gl, "S%d_%d" % (q, i), [128, 256], F32) for i in range(2)] for q in range(4)]
                        Sst = hT[:].rearrange("p k t -> p (k t)").rearrange("p (q c v) -> p q c v", q=4, c=NTC)
                        kTm = [sb(gl, "kTm%d" % i, [128, 512], BF16) for i in range(2)]
                        kvs = [sb(gl, "kvs%d" % i, [128, 4, 256], F32) for i in range(2)]
                        deckp = sb(gl, "deckp", [128, 4, NTC], F32)
                        deck = [("dec", i) for i in range(4)]
                        P.op("dve", lambda h: h.tensor_copy(out=deckp[:], in_=dec[:]), reads=deck, writes=["deckp"])
                        dkv = deckp[:].rearrange("p q (a b) -> p q a b", b=2)
                        P.op("dve", lambda h: h.tensor_scalar(out=dkv[:, 0:2, 1:8, 0], in0=dkv[:, 0:2, 1:8, 0], scalar1=kp[:, 0:1], scalar2=None, op0=ALU.mult),
                             reads=["deckp", "kp"], writes=["deckp"])
                        P.op("dve", lambda h: h.tensor_scalar(out=dkv[:, 2:4, 0:7, 1], in0=dkv[:, 2:4, 0:7, 1], scalar1=kp[:, 0:1], scalar2=None, op0=ALU.mult),
                             reads=["deckp", "kp"], writes=["deckp"])
                        for q in range(4):
                            P.op("dve", (lambda q=q: lambda h: h.memset(S[q][0][:], 0.0))(), writes=[("S", q, 0)])
                            d_, p_ = q // 2, q % 2
                            for e in range(2):
                                P.op("sp", (lambda q=q, d_=d_, p_=p_, e=e: lambda h: h.dma_start(
                                    out=S[q][0][e * 64:(e + 1) * 64, e * 128:(e + 1) * 128], in_=s0_in[l, d_, 2 * p_ + e]))(),
                                    reads=[], writes=[("S", q, 0)], dma=True)

                        def chunks_of(step):
                            return [step, step, NTC - 1 - step, NTC - 1 - step]

                        def d1_prep(step):
                            chunk_of = chunks_of(step)
                            pT = pb[step % 2]

                            def ktr(h):
                                ins = None
                                for q in range(4):
                                    c = chunk_of[q]
                                    ins = h.transpose(pT[:].bitcast(BF16)[:, q * 128:(q + 1) * 128], QK[:, 4 + q, c * 128:(c + 1) * 128], identb[:])
                                return ins
                            P.op("pe", ktr, reads=[("QK", 4 + q, chunk_of[q] // 4) for q in range(4)] + ["identb"], writes=[("pb", step % 2)])
                            km = kTm[step % 2]
                            P.op("act", lambda h: h.activation(out=km[:], in_=pT[:].bitcast(BF16)[:, 0:512], func=AF.Copy),
                                 reads=[("pb", step % 2)], writes=[("kTm", step % 2)])
                            kb0 = 2 + (step % 2) * 2

                            def kvmm(h):
                                ins = None
                                for q in range(4):
                                    c = chunk_of[q]
                                    p_ = q % 2
                                    ins = h.matmul(pb[kb0 + q // 2][:, (q % 2) * 256:(q % 2 + 1) * 256], lhsT=km[:, q * 128:(q + 1) * 128],
                                                   rhs=V[:, c, p_ * 256:(p_ + 1) * 256], start=True, stop=True)
                                return ins
                            P.op("pe", kvmm, reads=[("kTm", step % 2)] + [("V", chunk_of[q]) for q in range(4)], writes=[("pb", kb0), ("pb", kb0 + 1)])
                            kv = kvs[step % 2]
                            for q in range(4):
                                c = chunk_of[q]
                                P.op("act", (lambda q=q, c=c: lambda h: h.activation(
                                    out=kv[:, q, :], in_=pb[kb0 + q // 2][:, (q % 2) * 256:(q % 2 + 1) * 256], func=AF.Copy, scale=dec[:, q, c:c + 1]))(),
                                    reads=[("pb", kb0 + q // 2), ("dec", c // 4)], writes=[("kvs", step % 2, q)])

                        def d1_main(step):
                            chunk_of = chunks_of(step)
                            cur, nxt = step % 2, (step + 1) % 2
                            kv = kvs[step % 2]
                            for q in range(4):
                                c = chunk_of[q]
                                d_ = q // 2
                                seg_start = (c % 2 == 0 and c > 0) if d_ == 0 else (c % 2 == 1 and c < NTC - 1)
                                src = S[q][cur][:]
                                dst = Sst[:, q, c, :]
                                if d_ == 0:
                                    sc_ = kp[:, 0:1] if seg_start else 1.0
                                    fn = (lambda src=src, dst=dst, sc_=sc_: lambda h: h.activation(out=dst, in_=src, func=AF.Copy, scale=sc_))()
                                    P.op("act", fn, reads=[("S", q, cur), "kp"], writes=[("Sst", q, c)])
                                else:
                                    if seg_start:
                                        fn = (lambda src=src, dst=dst: lambda h: h.tensor_scalar(out=dst, in0=src, scalar1=kp[:, 0:1], scalar2=None, op0=ALU.mult))()
                                    else:
                                        fn = (lambda src=src, dst=dst: lambda h: h.tensor_copy(out=dst, in_=src))()
                                    P.op("dve", fn, reads=[("S", q, cur), "kp"], writes=[("Sst", q, c)])
                            for q in range(4):
                                c = chunk_of[q]
                                P.op("dve", (lambda q=q, c=c: lambda h: h.scalar_tensor_tensor(
                                    out=S[q][nxt][:], in0=S[q][cur][:], scalar=deckp[:, q, c:c + 1], in1=kv[:, q, :], op0=ALU.mult, op1=ALU.add))(),
                                    reads=[("S", q, cur), "deckp", ("kvs", step % 2, q)], writes=[("S", q, nxt)])
                                d_, p_ = q // 2, q % 2
                                seg_end = (c % 2 == 1) if d_ == 0 else (c % 2 == 0)
                                if seg_end:
                                    seg = c // 2
                                    for e in range(2):
                                        key = ("ns", l, seg, d_, 2 * p_ + e)
                                        P.op("sp", (lambda q=q, seg=seg, d_=d_, p_=p_, e=e: lambda h: h.dma_start(
                                            out=ns_out[l, seg, d_, 2 * p_ + e], in_=S[q][nxt][e * 64:(e + 1) * 64, e * 128:(e + 1) * 128]))(),
                                            reads=[("S", q, nxt)], writes=[key], dma=True)
                                        out_keys.append(key)
                        d1_prep(0)
                        for step in range(NTC):
                            if step + 1 < NTC:
                                d1_prep(step + 1)
                            d1_main(step)
                        P.barrier()
                        if stop == "D1":
                            return "stop"
                        attm = [sb(gl, "attm%d" % i, [128, 1024], BF16) for i in range(3)]
                        qzt = [winF[:, 2 * i:2 * i + 2, :].rearrange("p k c -> p (k c)").rearrange("p (q e c) -> p q e c", q=4, e=2) for i in range(3)]
                        for i in range(3):
                            P.op("pool", (lambda i=i: lambda h: h.memset(qzt[i][:], 0.0))(), writes=[("qz", i)])
                        osq = [sb(gl, "osq%d" % i, [128, 512], BF16) for i in range(2)]
                        lno = [sb(gl, "lno%d" % i, [128, 512], F32) for i in range(2)]

                        def d2_a1(c):
                            t0 = c * 128
                            qz = qzt[c % 3]
                            for e in range(2):
                                P.op("act", (lambda e=e: lambda h: h.activation(
                                    out=qz[e * 64:(e + 1) * 64, :, e, :], in_=QK[e * 64:(e + 1) * 64, 0:4, t0:t0 + 128], func=AF.Copy))(),
                                    reads=[("QK", qq, c // 4) for qq in range(4)], writes=[("qz", c % 3)])

                        def d2_a2(c):
                            t0 = c * 128
                            a0 = (c % 2) * 2
                            am = attm[c % 3]
                            qz = qzt[c % 3]

                            def attmm(h):
                                ins = None
                                for d_ in range(2):
                                    for hd in range(4):
                                        e, p_ = hd % 2, hd // 2
                                        ins = h.matmul(pb[a0 + d_][:, hd * 128:(hd + 1) * 128],
                                                       lhsT=QK[:, 4 + d_ * 2 + p_, t0:t0 + 128],
                                                       rhs=qz[:, d_ * 2 + p_, e, :], start=True, stop=True)
                                return ins
                            P.op("pe", attmm, reads=[("QK", i, c // 4) for i in range(4, 8)] + [("qz", c % 3)], writes=[("pb", a0), ("pb", a0 + 1)])
                            for d_ in range(2):
                                P.op("dve", (lambda d_=d_: lambda h: h.tensor_tensor(
                                    out=am[:, d_ * 512:(d_ + 1) * 512], in0=pb[a0 + d_][:], in1=mk[:, d_ * 512:(d_ + 1) * 512], op=ALU.mult))(),
                                    reads=[("pb", a0 + d_), "mk"], writes=[("attm", c % 3, d_)])

                        def d2_b(c):
                            t0 = c * 128
                            po = 4 + (c % 2)
                            am = attm[c % 3]
                            qz = qzt[c % 3]

                            def omm(h):
                                ins = None
                                for hd in range(4):
                                    e, p_ = hd % 2, hd // 2
                                    o_ap = pb[po][:, hd * 128:(hd + 1) * 128]
                                    h.matmul(o_ap, lhsT=V[:, c, hd * 128:(hd + 1) * 128], rhs=am[:, hd * 128:(hd + 1) * 128], start=True, stop=False)
                                    h.matmul(o_ap, lhsT=V[:, c, hd * 128:(hd + 1) * 128], rhs=am[:, 512 + hd * 128:512 + (hd + 1) * 128], start=False, stop=False)
                                    h.matmul(o_ap, lhsT=Sst[:, 0 + p_, c, e * 128:(e + 1) * 128], rhs=qz[:, 0 + p_, e, :], start=False, stop=False)
                                    ins = h.matmul(o_ap, lhsT=Sst[:, 2 + p_, c, e * 128:(e + 1) * 128], rhs=qz[:, 2 + p_, e, :], start=False, stop=True)
                                return ins
                            P.op("pe", omm, reads=[("V", c), ("attm", c % 3, 0), ("attm", c % 3, 1)] + [("Sst", q, c) for q in range(4)] + [("qz", c % 3)],
                                 writes=[("pb", po)])
                            oq = osq[c % 2]
                            P.op("act", lambda h: h.activation(out=oq[:], in_=pb[po][:], func=AF.Square), reads=[("pb", po)], writes=[("osq", c % 2)])

                        def d2_c1(c):
                            pss = 6
                            oq = osq[c % 2]
                            ln_ = lno[c % 2]
                            P.op("pe", lambda h: h.matmul(pb[pss][:], lhsT=onesb[:], rhs=oq[:], start=True, stop=True),
                                 reads=[("osq", c % 2), "onesb"], writes=[("pb", pss)])
                            P.op("act", lambda h: h.activation(out=ln_[:], in_=pb[pss][:], func=AF.Ln, scale=1.0 / 128, bias=EPS),
                                 reads=[("pb", pss)], writes=[("lno", c % 2)])
                            P.op("act", lambda h: h.activation(out=ln_[:], in_=ln_[:], func=AF.Exp, scale=-0.5), reads=[("lno", c % 2)], writes=[("lno", c % 2)])

                        def d2_c2(c):
                            t0 = c * 128
                            po = 4 + (c % 2)
                            ln_ = lno[c % 2]
                            P.op("dve", lambda h: h.scalar_tensor_tensor(out=ln_[:], in0=pb[po][:], scalar=VTT[:, base + 256:base + 257],
                                                                         in1=ln_[:], op0=ALU.mult, op1=ALU.mult),
                                 reads=[("pb", po), ("lno", c % 2), "VTT"], writes=[("lno", c % 2)])
                            og1v = ln_[:].rearrange("p (q c) -> p q c", c=128)
                            P.op("dve", lambda h: h.tensor_tensor(out=ogT[:, :, t0:t0 + 128], in0=og1v, in1=sgT[:, :, t0:t0 + 128], op=ALU.mult),
                                 reads=[("lno", c % 2), ("sgT", c // 4)], writes=[("ogT", c)])
                        for it in range(NTC + 3):
                            if it < NTC:
                                d2_a1(it)
                            if 0 <= it - 2 < NTC:
                                d2_b(it - 2)
                            if 0 <= it - 3 < NTC:
                                d2_c1(it - 3)
                            if it < NTC:
                                d2_a2(it)
                            if 0 <= it - 3 < NTC:
                                d2_c2(it - 3)
                P.barrier()
                if l == 0:
                    with ExitStack() as dd:
                        tmpd = sb(dd, "tmpd2", [128, T], F32)
                        for nm, src in (("yfT", yfT), ("ogT", ogT)):
                            if nm in dbg_out:
                                for k in range(4):
                                    P.op("dve", (lambda k=k, src=src: lambda h: h.tensor_copy(out=tmpd[:], in_=src[:, k, :]))(), reads=list(P.keys), writes=["tmpd2"])
                                    dbg_store(nm, tmpd[:], dbg_out[nm][k * 128:(k + 1) * 128, :], ["tmpd2"])
                        P.barrier()
                if stop == "D2":
                    return "stop"
                with ExitStack() as pe_:
                    wout = sb(pe_, "wout", [128, 8, D], BF16)
                    xrE = [sb(pe_, "xrE%d" % i, [128, D], F32) for i in range(4)]
                    tmpE = [sb(pe_, "tmpE%d" % i, [128, D], F32) for i in range(4)]
                    junkE = sb(pe_, "junkE", [128, 512], BF16)
                    ss2 = sb(pe_, "ss2E", [128, 2 * NTC], F32)
                    lnv2 = sb(pe_, "lnv2E", [128, NTC], F32)
                    rstd2 = sb(pe_, "rstd2E", [128, NTC], F32)
                    for k in range(8):
                        P.op("pool", (lambda k=k: lambda h: h.dma_start(out=wout[:, k, :], in_=w_out[l, k * 128:(k + 1) * 128, :]))(),
                             writes=[("wout", k)], dma=True)
                    for tc in range(NTC):
                        t0 = tc * 128
                        b0 = (tc % 4) * 2

                        def outmm(h, t0=t0, b0=b0):
                            ins = None
                            for cb in range(2):
                                for k in range(8):
                                    src = yfT[:, k, t0:t0 + 128] if k < 4 else ogT[:, k - 4, t0:t0 + 128]
                                    ins = h.matmul(pb[b0 + cb][:], lhsT=src, rhs=wout[:, k, cb * 512:(cb + 1) * 512], start=(k == 0), stop=(k == 7))
                            return ins
                        P.op("pe", outmm, reads=[("wout", k) for k in range(8)] + [("yfT", tc // 4), ("ogT", tc)], writes=[("pb", b0), ("pb", b0 + 1)])
                        post_norm_1(tc, [pb[b0], pb[b0 + 1]], [("pb", b0), ("pb", b0 + 1)], junkE, "junkE", ss2, lnv2, "E", xrE[tc % 4], ("xrE", tc % 4))
                        if tc > 0:
                            pc = tc - 1
                            pb0 = (pc % 4) * 2
                            post_norm_2(l, 0, pc, [pb[pb0], pb[pb0 + 1]], [("pb", pb0), ("pb", pb0 + 1)], xrE[pc % 4], ("xrE", pc % 4),
                                        tmpE[pc % 4], ("tmpE", pc % 4), lnv2, rstd2, "E", False)
                    pc = NTC - 1
                    pb0 = (pc % 4) * 2
                    post_norm_2(l, 0, pc, [pb[pb0], pb[pb0 + 1]], [("pb", pb0), ("pb", pb0 + 1)], xrE[pc % 4], ("xrE", pc % 4),
                                tmpE[pc % 4], ("tmpE", pc % 4), lnv2, rstd2, "E", False)
            P.barrier()
            if l == 0 and "xmix" in dbg_out:
                with ExitStack() as dd:
                    tmpd = sb(dd, "tmpd3", [128, D], F32)
                    for tc in range(NTC):
                        P.op("sp", (lambda tc=tc: lambda h: h.dma_start(out=tmpd[:], in_=xs[tc * 128:(tc + 1) * 128, :]))(), reads=[("xs", tc)], writes=["tmpd3"], dma=True)
                        dbg_store("xmix", tmpd[:], dbg_out["xmix"][tc * 128:(tc + 1) * 128, :], ["tmpd3"])
                    P.barrier()
            if stop == "E":
                return "stop"
            with ExitStack() as ffn:
                h2x = sb(ffn, "h2x", [128, 8, 32, 66], BF16)
                P.op("dve", lambda h: h.memset(h2x[:], 0.0), writes=[("h2x", g, k) for g in range(8) for k in range(8)] + ["h2xhalo"])
                wup = [sb(ffn, "wup%d" % i, [128, 8, 512], BF16) for i in range(3)]
                wupsrc = w_up[l].rearrange("(k p) c -> p k c", p=128)
                WU = [(hf, u) for hf in range(2) for u in range(NPAIR // 2)]

                def wup_dma(wi, part=None):
                    hf, u = WU[wi]
                    wr = wup[wi % 3]
                    wkey = ("wup", wi % 3)
                    for pt in (range(4) if part is None else [part]):
                        half, kq = pt // 2, pt % 2
                        c_src = (DFF if half else 0) + u * 256
                        P.op("pool", (lambda half=half, kq=kq, c_src=c_src: lambda h: h.dma_start(
                            out=wr[:, kq * 4:(kq + 1) * 4, half * 256:(half + 1) * 256],
                            in_=wupsrc[:, kq * 4:(kq + 1) * 4, c_src:c_src + 256]))(), writes=[wkey], dma=True)
                wup_dma(0)
                wup_dma(1)
                with ExitStack() as pf:
                    def dstf(k, g):
                        return h2x[:, k, g * 4:(g + 1) * 4, 1:65], ("h2x", g, k)
                    norm_to_T(pf, l, 1, dstf, "F")
                P.barrier()
                h2k = [("h2x", g, k) for g in range(8) for k in range(8)]
                P.op("dve", lambda h: h.tensor_tensor(out=h2x[:, :, 1:32, 0], in0=h2x[:, :, 0:31, 64],
                                                      in1=hm[:, 1:32].unsqueeze(1).to_broadcast([128, 8, 31]), op=ALU.mult),
                     reads=h2k + ["hm"], writes=["h2xhalo"])
                P.op("dve", lambda h: h.tensor_tensor(out=h2x[:, :, 0:31, 65], in0=h2x[:, :, 1:32, 1],
                                                      in1=hm[:, 32:63].unsqueeze(1).to_broadcast([128, 8, 31]), op=ALU.mult),
                     reads=h2k + ["hm"], writes=["h2xhalo"])
                h2xf = h2x[:].rearrange("p k s c -> p k (s c)")
                wdn = sb(ffn, "wdn", [128, NPAIR, D], BF16)
                aT = sb(ffn, "aT", [128, NPAIR, 1024], BF16)
                NB = int(os.environ.get('FFN_NB', '4'))
                t1 = [sb(ffn, "t1_%d" % i, [128, 6, 64], F32) for i in range(NB)]
                g1 = [sb(ffn, "g1_%d" % i, [128, 6, 64], F32) for i in range(NB)]
                xrG = [sb(ffn, "xrG%d" % i, [128, D], F32) for i in range(2)]
                tmpG = [sb(ffn, "tmpG%d" % i, [128, D], F32) for i in range(2)]
                junkG = sb(ffn, "junkG", [128, 512], BF16)
                ss2g = sb(ffn, "ss2G", [128, 2 * NTC], F32)
                lnv2g = sb(ffn, "lnv2G", [128, NTC], F32)
                rstd2g = sb(ffn, "rstd2G", [128, NTC], F32)
                def wdn_dma():
                    for i in range(NPAIR):
                        P.op("pool", (lambda i=i: lambda h: h.dma_start(out=wdn[:, i, :], in_=w_down[l, i * 128:(i + 1) * 128, :]))(),
                             writes=[("wdn", i)], dma=True)
                BLKS = [(0, 6), (6, 5), (11, 5)] if os.environ.get('FFN_BLK', '655') == '655' else [(0, 4), (4, 4), (8, 4), (12, 4)]
                cnt2 = 0
                for hf in range(2):
                    for u in range(NPAIR // 2):
                        wi = hf * (NPAIR // 2) + u
                        if wi == 1:
                            wdn_dma()
                        wr = wup[wi % 3]
                        wkey = ("wup", wi % 3)
                        uidx = 0
                        for (sl0, nsg) in BLKS:
                            sg0 = hf * 16 + sl0
                            ncol = nsg * 66
                            for ii in range(2):
                                i = 2 * u + ii
                                if wi + 2 < len(WU) and uidx < 4:
                                    wup_dma(wi + 2, uidx)
                                uidx += 1
                                r2 = cnt2 % NB
                                bv = r2 * 2
                                bg = bv + 1
                                cnt2 += 1

                                def upmm(h, wr=wr, ii=ii, sg0=sg0, ncol=ncol, bv=bv, bg=bg):
                                    ins = None
                                    rhsv = [h2xf[:, k, sg0 * 66:sg0 * 66 + ncol] for k in range(8)]
                                    for k in range(8):
                                        h.matmul(pb[bv][:, 0:ncol], lhsT=wr[:, k, ii * 128:(ii + 1) * 128], rhs=rhsv[k], start=(k == 0), stop=(k == 7))
                                    for k in range(8):
                                        ins = h.matmul(pb[bg][:, 0:ncol], lhsT=wr[:, k, 256 + ii * 128:256 + (ii + 1) * 128], rhs=rhsv[k], start=(k == 0), stop=(k == 7))
                                    return ins
                                P.op("pe", upmm, reads=[wkey, "h2xhalo"] + h2k, writes=[("pb", bv), ("pb", bg)])
                                chains = []
                                for (bank, dstt, dkey, coff) in ((bv, t1[r2], ("t1", r2), i), (bg, g1[r2], ("g1", r2), NPAIR + i)):
                                    pvw = pb[bank][:, 0:ncol].rearrange("p (s c) -> p s c", c=66)
                                    dv = dstt[:, 0:nsg, :]
                                    cws = [VTT[:, base + j * 44 + coff:base + j * 44 + coff + 1] for j in range(3)]
                                    cbb = VTT[:, base + 132 + coff:base + 132 + coff + 1]
                                    chains.append((bank, pvw, dv, dkey, cws, cbb))
                                for (bank, pvw, dv, dkey, cws, cbb) in chains:
                                    P.op("act", (lambda pvw=pvw, dv=dv, cws=cws, cbb=cbb: lambda h: h.activation(
                                        out=dv, in_=pvw[:, :, 1:65], func=AF.Identity, scale=cws[1], bias=cbb))(),
                                        reads=[("pb", bank), "VTT"], writes=[dkey])
                                for tap, lo in ((0, 0), (2, 2)):
                                    for (bank, pvw, dv, dkey, cws, cbb) in chains:
                                        P.op("dve", (lambda pvw=pvw, dv=dv, cws=cws, tap=tap, lo=lo: lambda h: h.scalar_tensor_tensor(
                                            out=dv, in0=pvw[:, :, lo:lo + 64], scalar=cws[tap], in1=dv, op0=ALU.mult, op1=ALU.add))(),
                                            reads=[("pb", bank), "VTT", dkey], writes=[dkey])
                                sv = chains[1][2]
                                P.op("act", (lambda sv=sv: lambda h: h.activation(out=sv, in_=sv, func=AF.Silu))(),
                                     reads=[("g1", r2)], writes=[("g1", r2)])
                                a_dst = aT[:, i, sl0 * 64:(sl0 + nsg) * 64].rearrange("p (s c) -> p s c", c=64)
                                P.op("pool", (lambda sv=sv, tv=chains[0][2], a_dst=a_dst: lambda h: h.tensor_tensor(out=a_dst, in0=sv, in1=tv, op=ALU.mult))(),
                                     reads=[("g1", r2), ("t1", r2)], writes=[("aT", i)])
                    P.barrier()
                    for tcl in range(8):
                        tc = hf * 8 + tcl
                        b0 = (tcl % 4) * 2

                        def dnmm(h, tcl=tcl, b0=b0):
                            ins = None
                            for cb in range(2):
                                for i in range(NPAIR):
                                    ins = h.matmul(pb[b0 + cb][:], lhsT=aT[:, i, tcl * 128:(tcl + 1) * 128], rhs=wdn[:, i, cb * 512:(cb + 1) * 512],
                                                   start=(i == 0), stop=(i == NPAIR - 1))
                            return ins
                        P.op("pe", dnmm, reads=[("wdn", i) for i in range(NPAIR)] + [("aT", i) for i in range(NPAIR)],
                             writes=[("pb", b0), ("pb", b0 + 1)])
                        post_norm_1(tc, [pb[b0], pb[b0 + 1]], [("pb", b0), ("pb", b0 + 1)], junkG, "junkG", ss2g, lnv2g, "G", xrG[tc % 2], ("xrG", tc % 2))
                        if tcl > 0:
                            pc = tc - 1
                            pb0 = ((tcl - 1) % 4) * 2
                            post_norm_2(l, 1, pc, [pb[pb0], pb[pb0 + 1]], [("pb", pb0), ("pb", pb0 + 1)], xrG[pc % 2], ("xrG", pc % 2),
                                        tmpG[pc % 2], ("tmpG", pc % 2), lnv2g, rstd2g, "G", last)
                    pc = hf * 8 + 7
                    pb0 = (7 % 4) * 2
                    post_norm_2(l, 1, pc, [pb[pb0], pb[pb0 + 1]], [("pb", pb0), ("pb", pb0 + 1)], xrG[pc % 2], ("xrG", pc % 2),
                                tmpG[pc % 2], ("tmpG", pc % 2), lnv2g, rstd2g, "G", last)
                    P.barrier()
        for l_ in range(n_layers):
            if stop == "stage0" or do_layer(l_) == "stop":
                break
        P.op("sp", None, reads=list(dict.fromkeys(out_keys)))
        run_prog(nc, P)
    return nc, P


_CACHE = {}


def _consts():
    if "c" in _CACHE:
        return _CACHE["c"]
    bf = ml_dtypes.bfloat16

    def dft(n):
        k = np.arange(n)
        ang = 2.0 * np.pi * ((np.outer(k, k) % n).astype(np.float64)) / n
        return np.cos(ang), np.sin(ang)
    c2048, s2048 = dft(2048)
    c256, s256 = dft(256)
    c128, s128 = dft(128)
    samp_c = c2048 / np.sqrt(2048.0)
    samp_s = -s2048 / np.sqrt(2048.0)
    pr_c = np.zeros((2048, 2048))
    pr_s = np.zeros((2048, 2048))
    for i in range(8):
        pr_c[i * 256:(i + 1) * 256, i * 256:(i + 1) * 256] = c256 / 16.0
        pr_s[i * 256:(i + 1) * 256, i * 256:(i + 1) * 256] = -s256 / 16.0

    def tiles(cm, sm):
        out = np.zeros((64, 128, 1024), np.float32)
        for tpb in range(4):
            for tc in range(16):
                out[tpb * 16 + tc, :, 0:512] = cm[tc * 128:(tc + 1) * 128, tpb * 512:(tpb + 1) * 512]
                out[tpb * 16 + tc, :, 512:1024] = sm[tc * 128:(tc + 1) * 128, tpb * 512:(tpb + 1) * 512]
        return out.astype(bf)
    cst_s = tiles(samp_c, samp_s)
    cst_p = tiles(pr_c, pr_s)
    cc = np.concatenate([c128, s128], axis=1) / np.sqrt(128.0)
    j = np.arange(128)[:, None]
    i = np.arange(128)[None, :]
    mf = (j <= i).astype(np.float32)
    mb = (j >= i).astype(np.float32)
    mk = np.concatenate([np.tile(mf, (1, 4)), np.tile(mb, (1, 4))], axis=1).astype(bf)
    u = np.concatenate([mf, mb], axis=1).astype(np.float32) * (-1.0 / 16.0)
    q = 256
    omega = (1.0 / (10000.0 ** (np.arange(q, dtype=np.float32) / q))).astype(np.float32)
    er = np.arange(32, dtype=np.float32)[:, None] * omega
    ec = np.arange(64, dtype=np.float32)[:, None] * omega
    prr = np.concatenate([np.sin(er), np.cos(er)], axis=-1)
    pcc = np.concatenate([np.sin(ec), np.cos(ec)], axis=-1)
    pe = np.concatenate([np.broadcast_to(prr[:, None], (32, 64, 512)), np.broadcast_to(pcc[None], (32, 64, 512))], axis=-1)
    pe = np.ascontiguousarray(pe.reshape(2048, 1024).astype(np.float32))
    hm_s = np.zeros((128, 64), np.float32)
    hm_p = np.zeros((128, 64), np.float32)
    for s in range(32):
        hm_p[:, s] = 0.0 if s % 4 == 0 else 1.0
        hm_p[:, 32 + s] = 0.0 if s % 4 == 3 else 1.0
    c = dict(cst_s=cst_s, cst_p=cst_p, cc=cc.astype(bf), mk=mk, u=u, pe=pe, pe0=np.zeros_like(pe),
             hm_s=hm_s, hm_p=hm_p, idf=np.eye(128, dtype=np.float32), idb=np.eye(128).astype(bf))
    _CACHE["c"] = c
    return c


def _in_maps(inp):
    c = _consts()
    f = lambda a: np.ascontiguousarray(np.asarray(a, dtype=np.float32))
    x_prompt, x_sample = f(inp["x_prompt"]), f(inp["x_sample"])
    state = f(inp["state_gla"])
    cvs = f(inp["c"])
    cctx = f(inp["c_ctx"])
    wgate = np.zeros((2, 33, 512), np.float32)
    wgate[:, 0:16, 0:256] = f(inp["w_gate_f"])
    wgate[:, 16:32, 256:512] = f(inp["w_gate_b"])
    wgate[:, 32, 0:256] = f(inp["b_gate_f"])
    wgate[:, 32, 256:512] = f(inp["b_gate_b"])

    def vt_for(cvec):
        vt = np.zeros((VT_ROWS, 128), np.float32)
        vt[0:8] = cvec.reshape(8, 128)
        for l in range(2):
            b = LBASE(l)
            vt[b:b + 132] = f(inp["conv_w"])[l].reshape(132, 128)
            vt[b + 132:b + 176] = f(inp["conv_b"])[l].reshape(44, 128)
            vt[b + 176:b + 224] = f(inp["b_ada"])[l].reshape(48, 128)
            vt[b + 224:b + 232] = f(inp["g_pre_mix"])[l].reshape(8, 128)
            vt[b + 232:b + 240] = f(inp["g_post_mix"])[l].reshape(8, 128)
            vt[b + 240:b + 248] = f(inp["g_pre_ffn"])[l].reshape(8, 128)
            vt[b + 248:b + 256] = f(inp["g_post_ffn"])[l].reshape(8, 128)
            vt[b + 256] = f(inp["g_gla"])[l]
        return vt
    shared = dict(w_ada=f(inp["w_ada"]), w_in=f(inp["w_in"]), wgate=wgate, w_out=f(inp["w_out"]), w_up=f(inp["w_up"]),
                  w_down=f(inp["w_down"]), cc=c["cc"], mk=c["mk"], u=c["u"], idf=c["idf"], idb=c["idb"])
    maps = []
    for core in range(8):
        m = dict(shared)
        if core < 4:
            b = core
            m["x"] = x_sample[b]
            m["pe"] = c["pe"]
            m["vt"] = vt_for(cvs[b])
            m["s0"] = np.ascontiguousarray(state[b])
            m["kp"] = np.ones((128, 1), np.float32)
            m["hm"] = c["hm_s"]
            m["cst"] = c["cst_s"]
        else:
            j = core - 4
            m["x"] = np.ascontiguousarray(x_prompt[8 * j:8 * j + 8].reshape(T, D))
            m["pe"] = c["pe0"]
            m["vt"] = vt_for(cctx)
            m["s0"] = np.zeros((2, 2, 4, 64, 128), np.float32)
            m["kp"] = np.zeros((128, 1), np.float32)
            m["hm"] = c["hm_p"]
            m["cst"] = c["cst_p"]
        maps.append(m)
    return maps


def kernel(**inputs):
    if "nc" not in _CACHE:
        _CACHE["nc"] = build_nc()[0]
    nc = _CACHE["nc"]
    maps = _in_maps(inputs)
    res = run_bass_kernel_spmd(nc, maps, core_ids=list(range(8)))
    r = res.results
    y_sample = np.stack([r[b]["y"] for b in range(4)], axis=0).astype(np.float32)
    y_prompt = np.concatenate([r[4 + j]["y"].reshape(8, 256, D) for j in range(4)], axis=0).astype(np.float32)
    ns = np.concatenate([np.transpose(r[4 + j]["ns"], (1, 0, 2, 3, 4, 5)) for j in range(4)], axis=0).astype(np.float32)
    return y_prompt, y_sample, ns
```
